# Optimizing a Trainium2 kernel written in Bass

```python
import math
import jax, jax.numpy as jnp
from jax import lax
import numpy as np

D_MODEL = 1024
BATCH = 4
SEQ = 4096
DEPTH = 2
DEC_BATCH = 32
DEC_SEQ = 2048
PAST_LEN = 128

N_MIXERS = 2
N_SSM_LAYERS = (DEPTH + 1) // 2
N_ATTN_LAYERS = DEPTH // 2
D_FF = 2816
D_INNER = 2 * D_MODEL
SSM_HEAD_DIM = 64
SSM_HEADS = D_INNER // SSM_HEAD_DIM
SSM_GROUPS = 4
D_STATE = 128
D_CONV = 5
CHUNK = 128
CONV_DIM = D_INNER + 2 * SSM_GROUPS * D_STATE
SSM_IN_DIM = D_INNER + CONV_DIM + 2 * SSM_HEADS
ATTN_HEADS = 8
ATTN_HEAD_DIM = 64
ATTN_QK_DIM = ATTN_HEADS * 2 * ATTN_HEAD_DIM
ATTN_V_DIM = ATTN_HEADS * 2 * ATTN_HEAD_DIM
ATTN_QKV_DIM = 2 * ATTN_QK_DIM + ATTN_V_DIM
Q_BLOCK = 128
N_BUCKETS = 32
MAX_DISTANCE = 128
EPS = 1e-6

kernel_name = 'hybrid_bidir_ssd_diffattn_encoder'


def rms_norm(x, g):
    xf = x.astype(jnp.float32)
    y = xf * lax.rsqrt(jnp.mean(xf * xf, axis=-1, keepdims=True) + EPS)
    return (y * g.astype(jnp.float32)).astype(x.dtype)


def swiglu(x, wg, wu, wd):
    return (jax.nn.silu(x @ wg) * (x @ wu)) @ wd


def centred_depthwise_conv(x, w, b):
    pad = D_CONV // 2
    y = lax.conv_general_dilated(x, w[:, None, :].astype(x.dtype), window_strides=(1,), padding=[(pad, pad)], dimension_numbers=('NWC', 'WIO', 'NWC'), feature_group_count=x.shape[-1])
    return y + b.astype(x.dtype)


def ssd_chunked(x, dt, a, b_mat, c_mat):
    bsz, seqlen, nh, hp = x.shape
    ng, ns = b_mat.shape[-2:]
    nc = seqlen // CHUNK
    hpg = nh // ng
    xc = x.astype(jnp.float32).reshape(bsz, nc, CHUNK, ng, hpg, hp)
    dtc = dt.reshape(bsz, nc, CHUNK, ng, hpg)
    bc = b_mat.astype(jnp.float32).reshape(bsz, nc, CHUNK, ng, ns)
    cc = c_mat.astype(jnp.float32).reshape(bsz, nc, CHUNK, ng, ns)
    da_cs = jnp.cumsum(dtc * a.reshape(ng, hpg), axis=2)
    mask = jnp.tril(jnp.ones((CHUNK, CHUNK), dtype=bool))[:, :, None, None]
    seg = da_cs[:, :, :, None] - da_cs[:, :, None, :]
    cb = jnp.einsum('bcign,bcjgn->bcijg', cc, bc)
    w = cb[..., None] * jnp.exp(jnp.where(mask, seg, -jnp.inf)) * dtc[:, :, None]
    y_diag = jnp.einsum('bcijgr,bcjgrp->bcigrp', w, xc)
    decay_states = jnp.exp(da_cs[:, :, -1:] - da_cs) * dtc
    states = jnp.einsum('bcjgn,bcjgrp->bcgrpn', bc, xc * decay_states[..., None])
    chunk_decay = jnp.exp(da_cs[:, :, -1])

    def step(h, inp):
        dec, st = inp
        return h * dec[..., None, None] + st, h

    h0 = jnp.zeros((bsz, ng, hpg, hp, ns), jnp.float32)
    _, prev = lax.scan(step, h0, (jnp.moveaxis(chunk_decay, 1, 0), jnp.moveaxis(states, 1, 0)))
    prev = jnp.moveaxis(prev, 0, 1)
    y_off = jnp.einsum('bcign,bcgrpn->bcigrp', cc, prev) * jnp.exp(da_cs)[..., None]
    return (y_diag + y_off).reshape(bsz, seqlen, nh, hp)


def mamba_mixer(u, w_in, conv_w, conv_b, dt_bias, a_log, d_skip, norm_g, w_out):
    bsz, seqlen, _ = u.shape
    proj = u @ w_in
    z, xbc, dt_raw = jnp.split(proj, [D_INNER, D_INNER + CONV_DIM], axis=-1)
    xbc = jax.nn.silu(centred_depthwise_conv(xbc, conv_w, conv_b))
    xs, b_mat, c_mat = jnp.split(xbc, [D_INNER, D_INNER + SSM_GROUPS * D_STATE], axis=-1)
    xs = xs.reshape(bsz, seqlen, SSM_HEADS, SSM_HEAD_DIM)
    b_mat = b_mat.reshape(bsz, seqlen, SSM_GROUPS, D_STATE)
    c_mat = c_mat.reshape(bsz, seqlen, SSM_GROUPS, D_STATE)
    dt = jax.nn.softplus(dt_raw.astype(jnp.float32).reshape(bsz, seqlen, 2, SSM_HEADS) + dt_bias.astype(jnp.float32))
    a = -jnp.exp(a_log.astype(jnp.float32))
    y_fwd = ssd_chunked(xs, dt[:, :, 0], a[0], b_mat, c_mat)
    y_bwd = ssd_chunked(xs[:, ::-1], dt[:, ::-1, 1], a[1], b_mat[:, ::-1], c_mat[:, ::-1])[:, ::-1]
    y = y_fwd + y_bwd + xs.astype(jnp.float32) * d_skip.astype(jnp.float32)[:, None]
    y = y.reshape(bsz, seqlen, D_INNER) * jax.nn.silu(z.astype(jnp.float32))
    yg = y.reshape(bsz, seqlen, SSM_GROUPS, D_INNER // SSM_GROUPS)
    yg = yg * lax.rsqrt(jnp.mean(yg * yg, axis=-1, keepdims=True) + EPS)
    y = yg.reshape(bsz, seqlen, D_INNER) * norm_g.astype(jnp.float32)
    return y.astype(u.dtype) @ w_out


def relative_bucket(rel):
    half = N_BUCKETS // 2
    max_exact = half // 2
    ret = jnp.where(rel > 0, half, 0)
    n = jnp.abs(rel)
    nf = jnp.maximum(n, 1).astype(jnp.float32)
    large = max_exact + (jnp.log(nf / max_exact) / math.log(MAX_DISTANCE / max_exact) * (half - max_exact)).astype(jnp.int32)
    large = jnp.minimum(large, half - 1)
    return ret + jnp.where(n < max_exact, n, large)


def diff_attention(u, w_qkv, lam, subln_g, w_out, rel_bias, lambda_init):
    bsz, seqlen, _ = u.shape
    qkv = u @ w_qkv
    q, k, v = jnp.split(qkv, [ATTN_QK_DIM, 2 * ATTN_QK_DIM], axis=-1)
    q = q.reshape(bsz, seqlen, ATTN_HEADS, 2, ATTN_HEAD_DIM)
    k = k.reshape(bsz, seqlen, ATTN_HEADS, 2, ATTN_HEAD_DIM)
    v = v.reshape(bsz, seqlen, ATTN_HEADS, 2 * ATTN_HEAD_DIM)
    lf = lam.astype(jnp.float32)
    lam_full = jnp.exp(jnp.sum(lf[0] * lf[1])) - jnp.exp(jnp.sum(lf[2] * lf[3])) + lambda_init
    scale = ATTN_HEAD_DIM ** -0.5
    nblk = seqlen // Q_BLOCK
    qb = jnp.moveaxis(q.reshape(bsz, nblk, Q_BLOCK, ATTN_HEADS, 2, ATTN_HEAD_DIM), 1, 0)
    key_pos = jnp.arange(seqlen)
    bias_table = rel_bias.astype(jnp.float32).T

    def block(args):
        q_blk, blk_idx = args
        q_pos = blk_idx * Q_BLOCK + jnp.arange(Q_BLOCK)
        bias = bias_table[:, relative_bucket(key_pos[None, :] - q_pos[:, None])]
        logits = jnp.einsum('bqhmd,bkhmd->bhmqk', q_blk, k).astype(jnp.float32) * scale + bias[None, :, None]
        p = jax.nn.softmax(logits, axis=-1)
        attn = p[:, :, 0] - lam_full * p[:, :, 1]
        return jnp.einsum('bhqk,bkhd->bqhd', attn.astype(v.dtype), v)

    o = lax.map(block, (qb, jnp.arange(nblk)))
    o = jnp.moveaxis(o, 0, 1).reshape(bsz, seqlen, ATTN_HEADS, 2 * ATTN_HEAD_DIM)
    o = rms_norm(o, subln_g) * (1.0 - lambda_init)
    return o.reshape(bsz, seqlen, ATTN_V_DIM) @ w_out


def trunk(x, norm_pre, norm_post, ffn_w_gate, ffn_w_up, ffn_w_down, ssm_w_in, ssm_conv_w, ssm_conv_b, ssm_dt_bias, ssm_a_log, ssm_d, ssm_norm, ssm_w_out, attn_w_qkv, attn_lambda, attn_subln, attn_w_out, rel_bias):
    for i in range(DEPTH):
        h = swiglu(rms_norm(x, norm_pre[i, 0]), ffn_w_gate[i, 0], ffn_w_up[i, 0], ffn_w_down[i, 0])
        x = x + 0.5 * rms_norm(h, norm_post[i, 0])
        u = rms_norm(x, norm_pre[i, 1])
        j = i // N_MIXERS
        if i % N_MIXERS == 0:
            m = mamba_mixer(u, ssm_w_in[j], ssm_conv_w[j], ssm_conv_b[j], ssm_dt_bias[j], ssm_a_log[j], ssm_d[j], ssm_norm[j], ssm_w_out[j])
        else:
            lambda_init = 0.8 - 0.6 * math.exp(-0.3 * i)
            m = diff_attention(u, attn_w_qkv[j], attn_lambda[j], attn_subln[j], attn_w_out[j], rel_bias, lambda_init)
        x = x + rms_norm(m, norm_post[i, 1])
        h = swiglu(rms_norm(x, norm_pre[i, 2]), ffn_w_gate[i, 1], ffn_w_up[i, 1], ffn_w_down[i, 1])
        x = x + 0.5 * rms_norm(h, norm_post[i, 2])
    return x


def setup_inputs(seed: int = 0) -> dict:
    key = jax.random.key(seed)
    ks = jax.random.split(key, 20)
    f32 = jnp.float32

    def nrm(k, shape, scale):
        return jax.random.normal(k, shape, f32) * scale

    n_a, n_b = N_SSM_LAYERS, N_ATTN_LAYERS
    dt0 = jnp.exp(jax.random.uniform(ks[10], (n_a, 2, SSM_HEADS), f32, math.log(1e-3), math.log(1e-1)))
    return {
        'x_prompt': nrm(ks[0], (BATCH, SEQ, D_MODEL), 1.0),
        'x_sample': nrm(ks[1], (DEC_BATCH, DEC_SEQ, D_MODEL), 1.0),
        'norm_pre': 1.0 + nrm(ks[2], (DEPTH, 3, D_MODEL), 0.05),
        'norm_post': 1.0 + nrm(ks[3], (DEPTH, 3, D_MODEL), 0.05),
        'ffn_w_gate': nrm(ks[4], (DEPTH, 2, D_MODEL, D_FF), D_MODEL ** -0.5),
        'ffn_w_up': nrm(ks[5], (DEPTH, 2, D_MODEL, D_FF), D_MODEL ** -0.5),
        'ffn_w_down': nrm(ks[6], (DEPTH, 2, D_FF, D_MODEL), D_FF ** -0.5),
        'ssm_w_in': nrm(ks[7], (n_a, D_MODEL, SSM_IN_DIM), D_MODEL ** -0.5),
        'ssm_conv_w': nrm(ks[8], (n_a, D_CONV, CONV_DIM), D_CONV ** -0.5),
        'ssm_conv_b': nrm(ks[9], (n_a, CONV_DIM), 0.02),
        'ssm_dt_bias': dt0 + jnp.log(-jnp.expm1(-dt0)),
        'ssm_a_log': jnp.log(jax.random.uniform(ks[11], (n_a, 2, SSM_HEADS), f32, 1.0, 16.0)),
        'ssm_d': 1.0 + nrm(ks[12], (n_a, SSM_HEADS), 0.1),
        'ssm_norm': 1.0 + nrm(ks[13], (n_a, D_INNER), 0.05),
        'ssm_w_out': nrm(ks[14], (n_a, D_INNER, D_MODEL), D_INNER ** -0.5),
        'attn_w_qkv': nrm(ks[15], (n_b, D_MODEL, ATTN_QKV_DIM), D_MODEL ** -0.5),
        'attn_lambda': nrm(ks[16], (n_b, 4, ATTN_HEAD_DIM), 0.1),
        'attn_subln': 1.0 + nrm(ks[17], (n_b, 2 * ATTN_HEAD_DIM), 0.05),
        'attn_w_out': nrm(ks[18], (n_b, ATTN_V_DIM, D_MODEL), ATTN_V_DIM ** -0.5),
        'rel_bias': nrm(ks[19], (N_BUCKETS, ATTN_HEADS), 0.5),
    }


def reference(x_prompt, x_sample, norm_pre, norm_post, ffn_w_gate, ffn_w_up, ffn_w_down, ssm_w_in, ssm_conv_w, ssm_conv_b, ssm_dt_bias, ssm_a_log, ssm_d, ssm_norm, ssm_w_out, attn_w_qkv, attn_lambda, attn_subln, attn_w_out, rel_bias):
    y_prompt = trunk(x_prompt, norm_pre, norm_post, ffn_w_gate, ffn_w_up, ffn_w_down, ssm_w_in, ssm_conv_w, ssm_conv_b, ssm_dt_bias, ssm_a_log, ssm_d, ssm_norm, ssm_w_out, attn_w_qkv, attn_lambda, attn_subln, attn_w_out, rel_bias)
    y_sample = trunk(x_sample, norm_pre, norm_post, ffn_w_gate, ffn_w_up, ffn_w_down, ssm_w_in, ssm_conv_w, ssm_conv_b, ssm_dt_bias, ssm_a_log, ssm_d, ssm_norm, ssm_w_out, attn_w_qkv, attn_lambda, attn_subln, attn_w_out, rel_bias)
    return (y_prompt, y_sample)
```

```python
import math
from contextlib import ExitStack
import numpy as np
import concourse.bass as bass
import concourse.mybir as mybir
from concourse.bass_utils import run_bass_kernel_spmd

F32 = mybir.dt.float32
BF16 = mybir.dt.bfloat16
AF = mybir.ActivationFunctionType
ALU = mybir.AluOpType

D = 1024
DC = 8
DFF = 2816
FC = 22
DIN = 2048
NHS = 32
HD = 64
NG = 4
NST = 128
CONVD = 3072
SSM_IN = 5184
EPS = 1e-6
NBUCK = 32
LAMBDA_INIT = 0.8 - 0.6 * math.exp(-0.3 * 1)
SCALE = 64 ** -0.5
NEGBIG = -30000.0
FV = 1280


def _bucket(rel):
    half = NBUCK // 2
    max_exact = half // 2
    ret = np.where(rel > 0, half, 0)
    n = np.abs(rel)
    nf = np.maximum(n, 1).astype(np.float32)
    large = max_exact + (np.log(nf / np.float32(max_exact)) / np.float32(math.log(128 / max_exact)) * np.float32(half - max_exact)).astype(np.int32)
    large = np.minimum(large, half - 1)
    return ret + np.where(n < max_exact, n, large)


def make_consts():
    c = {}
    i = np.arange(128)
    c["ident"] = np.eye(128, dtype=np.float32)
    c["antiid"] = np.eye(128, dtype=np.float32)[::-1].copy()
    c["trif"] = (i[:, None] <= i[None, :]).astype(np.float32)
    c["trib"] = (i[:, None] >= i[None, :]).astype(np.float32)
    c["maskf"] = np.where(i[None, :] >= i[:, None], 0.0, NEGBIG).astype(np.float32)
    c["maskb"] = np.where(i[None, :] <= i[:, None], 0.0, NEGBIG).astype(np.float32)
    rel = np.arange(-FV // 2, FV // 2)
    b = _bucket(rel)
    oh = np.zeros((NBUCK, FV), np.float32)
    oh[b, np.arange(FV)] = 1.0
    c["bucket_oh"] = oh
    return np.concatenate([c["ident"], c["antiid"], c["trif"], c["trib"], c["maskf"], c["maskb"]], axis=1), oh


class Cell:
    __slots__ = ("w", "r", "aw")

    def __init__(self):
        self.w = None
        self.r = {}
        self.aw = {}


class Sem:
    __slots__ = ("h", "id")

    def __init__(self, h, i):
        self.h = h
        self.id = i


class Eng:
    def __init__(self, name, eng, sem, is_pe=False):
        self.name = name
        self.eng = eng
        self.sem = sem
        self.count = 0
        self.seen = {}
        self.is_pe = is_pe
        self.pend_r = []
        self.pend_w = []


class Slot:
    def __init__(self, sem):
        self.sem = sem
        self.val = 0


def cells_of(x):
    if isinstance(x, Cell):
        return [x]
    out = []
    for y in x:
        out.extend(cells_of(y))
    return out


class MK:
    def __init__(self, NU=5, UL=2048, TT=1024, debug_stage=None):
        self.NU, self.UL, self.TT = NU, UL, TT
        self.SW = 512
        assert TT % 512 == 0 and UL % TT == 0
        self.TS = TT // 512
        self.NTOK = NU * UL
        self.NT = self.NTOK // TT
        self.NCH = UL // 128
        self.debug_stage = debug_stage
        self.nc = bass.Bass("TRN2", target_bir_lowering=False)
        self.es = ExitStack()
        self.n_inst = 0
        self._uid = 0

    def uid(self, p):
        self._uid += 1
        return f"{p}{self._uid}"

    def sb(self, es, name, shape, dt):
        return es.enter_context(self.nc.sbuf_tensor(self.uid(name), list(shape), dt))

    def dram(self, name, shape, dt, kind="Internal"):
        if name in getattr(self, "debug", ()):
            kind = "ExternalOutput"
        return self.nc.dram_tensor(name, list(shape), dt, kind=kind).ap()

    def setup_engines(self):
        nc = self.nc
        self.sems = []

        def mksem(name):
            h = self.es.enter_context(nc.semaphore(name))
            s = Sem(h, len(self.sems))
            self.sems.append(s)
            return s
        self.E = {
            "pe": Eng("pe", nc.tensor, mksem("s_pe"), is_pe=True),
            "act": Eng("act", nc.scalar, mksem("s_act")),
            "dve": Eng("dve", nc.vector, mksem("s_dve")),
            "pool": Eng("pool", nc.gpsimd, mksem("s_pool")),
            "sp": Eng("sp", nc.sync, mksem("s_sp")),
        }
        NS = 16
        self.slots = {q: [Slot(mksem(f"d_{q}{i}")) for i in range(NS)] for q in ("sp", "pool")}
        self.slot_i = {"sp": 0, "pool": 0}

    def _waits(self, e, reads, writes, is_dma=False, accum=()):
        need = {}

        def add(tok, raw):
            s, v = tok
            if (not is_dma) and s is e.sem:
                if (not raw) or e.is_pe:
                    return
            if need.get(s.id, (None, 0))[1] < v:
                need[s.id] = (s, v)
        for c in reads:
            if c.w is not None:
                add(c.w, True)
            for sid, tok in c.aw.items():
                add(tok, True)
        for c in writes:
            if c.w is not None:
                add(c.w, False)
            for sid, tok in c.aw.items():
                add(tok, False)
            for sid, tok in c.r.items():
                add(tok, False)
        for c in accum:
            if c.w is not None:
                add(c.w, False)
            for sid, tok in c.r.items():
                add(tok, False)
        for sid, (s, v) in need.items():
            if e.seen.get(sid, 0) < v:
                e.eng.wait_ge(s.h, v)
                e.seen[sid] = v
                self.n_inst += 1

    def emit(self, en, make, reads=(), writes=(), inc=True):
        e = self.E[en]
        reads = cells_of(reads)
        writes = cells_of(writes)
        self._waits(e, reads, writes)
        ins = make(e.eng)
        self.n_inst += 1
        if not inc:
            e.pend_r.extend(reads)
            e.pend_w.extend(writes)
            return ins
        e.count += 1
        ins.then_inc(e.sem.h, 1)
        tok = (e.sem, e.count)
        for c in e.pend_r + reads:
            c.r[e.sem.id] = tok
        for c in e.pend_w + writes:
            c.w = tok
            c.r = {}
            c.aw = {}
        e.pend_r = []
        e.pend_w = []
        return ins

    def dma(self, q, out, in_, reads=(), writes=(), accum=(), **kw):
        e = self.E[q]
        reads = cells_of(reads)
        writes = cells_of(writes)
        accum = cells_of(accum)
        i = self.slot_i[q]
        self.slot_i[q] = i + 1
        sl = self.slots[q][i % len(self.slots[q])]
        if sl.val > 0 and e.seen.get(sl.sem.id, 0) < sl.val:
            e.eng.wait_ge(sl.sem.h, sl.val)
            e.seen[sl.sem.id] = sl.val
        self._waits(e, reads, writes, is_dma=True, accum=accum)
        ins = e.eng.dma_start(out=out, in_=in_, **kw)
        self.n_inst += 1
        sl.val += 16
        ins.then_inc(sl.sem.h, 16)
        tok = (sl.sem, sl.val)
        for c in reads:
            c.r[sl.sem.id] = tok
        for c in writes:
            c.w = tok
            c.r = {}
            c.aw = {}
        for c in accum:
            c.aw[sl.sem.id] = tok
        return ins

    def barrier(self):
        toks = []
        for e in self.E.values():
            assert not e.pend_r and not e.pend_w
            if e.count > 0:
                toks.append((e.sem, e.count))
        for q in self.slots:
            for sl in self.slots[q]:
                if sl.val > 0:
                    toks.append((sl.sem, sl.val))
        for e in self.E.values():
            for s, v in toks:
                if s is e.sem:
                    continue
                if e.seen.get(s.id, 0) < v:
                    e.eng.wait_ge(s.h, v)
                    e.seen[s.id] = v
                    self.n_inst += 1

    def V(self, make, r=(), w=(), inc=True):
        return self.emit("dve", make, r, w, inc)

    def A(self, make, r=(), w=(), inc=True):
        return self.emit("act", make, r, w, inc)

    def G(self, make, r=(), w=(), inc=True):
        return self.emit("pool", make, r, w, inc)

    def P(self, make, r=(), w=(), inc=True):
        return self.emit("pe", make, r, w, inc)

    def mm(self, out, lhsT, rhs, start, stop, r=(), w=(), inc=True, skip=False):
        if skip:
            return self.P(lambda e: e.matmul(out, lhsT=lhsT, rhs=rhs, start=start, stop=stop, skip_group_check=True), r, w, inc)
        return self.P(lambda e: e.matmul(out, lhsT=lhsT, rhs=rhs, start=start, stop=stop), r, w, inc)

    def declare(self):
        nc = self.nc
        NTOK = self.NTOK
        ext = lambda n, s: nc.dram_tensor(n, list(s), F32, kind="ExternalInput").ap()
        self.x_in = ext("x", [NTOK, D])
        self.flag_in = ext("flag", [1, 1])
        self.consts_in = ext("consts", [128, 6 * 128])
        self.oh_in = ext("bucket_oh", [NBUCK, FV])
        self.w_in = {
            "norm_pre": ext("norm_pre", [2, 3, D]), "norm_post": ext("norm_post", [2, 3, D]),
            "ffn_w_gate": ext("ffn_w_gate", [2, 2, D, DFF]), "ffn_w_up": ext("ffn_w_up", [2, 2, D, DFF]),
            "ffn_w_down": ext("ffn_w_down", [2, 2, DFF, D]),
            "ssm_w_in": ext("ssm_w_in", [1, D, SSM_IN]), "ssm_conv_w": ext("ssm_conv_w", [1, 5, CONVD]),
            "ssm_conv_b": ext("ssm_conv_b", [1, CONVD]), "ssm_dt_bias": ext("ssm_dt_bias", [1, 2, NHS]),
            "ssm_a_log": ext("ssm_a_log", [1, 2, NHS]), "ssm_d": ext("ssm_d", [1, NHS]),
            "ssm_norm": ext("ssm_norm", [1, DIN]), "ssm_w_out": ext("ssm_w_out", [1, DIN, D]),
            "attn_w_qkv": ext("attn_w_qkv", [1, D, 3072]), "attn_lambda": ext("attn_lambda", [1, 4, 64]),
            "attn_subln": ext("attn_subln", [1, 128]), "attn_w_out": ext("attn_w_out", [1, D, D]),
            "rel_bias": ext("rel_bias", [NBUCK, 8]),
        }
        self.y_out = nc.dram_tensor("y", [NTOK, D], F32, kind="ExternalOutput").ap()
        self.wb = {}
        self.wb_cell = {}
        for li in range(2):
            for fi in range(2):
                for nm, shp in (("gate", [D, DFF]), ("up", [D, DFF]), ("down", [DFF, D])):
                    k = f"{nm}{li}{fi}"
                    self.wb[k] = self.dram("wb_" + k, shp, BF16)
                    self.wb_cell[k] = Cell()
        for k, shp in (("ssm_in", [D, SSM_IN]), ("ssm_out", [DIN, D]), ("qkv", [D, 3072]), ("attn_out", [D, D])):
            self.wb[k] = self.dram("wb_" + k, shp, BF16)
            self.wb_cell[k] = Cell()
        NT, TT = self.NT, self.TT
        self.xres = self.dram("xres", [NT, 128, DC * TT], F32)
        self.xres_c = [Cell() for _ in range(NT)]
        self.xbc = self.dram("xbc", [24, 128, NTOK], BF16)
        self.zsc = self.dram("zsc", [NTOK, DIN], BF16)
        self.dtr = self.dram("dtr", [NTOK, 64], F32)
        self.ssm_c = [Cell() for _ in range(NT)]
        self.yT = self.dram("yT", [16, 128, NTOK], BF16)
        self.yT_c = [Cell() for _ in range(self.NU)]
        self.qT = self.dram("qT", [8, 128, NTOK], BF16)
        self.kT = self.dram("kT", [8, 128, NTOK], BF16)
        self.vtm = self.dram("vtm", [NTOK, D], BF16)
        self.qkv_c = [Cell() for _ in range(NT)]
        self.oT = self.dram("oT", [8, 128, NTOK], BF16)
        self.oT_c = [Cell() for _ in range(self.NU)]
        self.fvec = self.dram("fvec", [8, FV], F32)
        self.fvec_c = Cell()
        self.dbg_out = None

    def setup_consts(self):
        es = self.es
        nc = self.nc
        sb = lambda n, s, d: self.sb(es, n, s, d)
        self.cF = sb("cF", [128, 6 * 128], F32)
        self.cF_c = Cell()
        self.cB = sb("cB", [128, 6 * 128], BF16)
        self.cB_c = Cell()
        self.onesB = sb("onesB", [128, 128], BF16)
        self.onesB_c = Cell()
        self.dma("sp", self.cF[:], self.consts_in[:, :], writes=[self.cF_c])
        self.V(lambda e: e.tensor_copy(out=self.cB[:], in_=self.cF[:]), [self.cF_c], [self.cB_c])
        self.V(lambda e: e.memset(self.onesB[:], 1.0), [], [self.onesB_c])
        self.identF = self.cF[:, 0:128]
        self.antiF = self.cF[:, 128:256]
        self.identB = self.cB[:, 0:128]
        self.trifB = self.cB[:, 256:384]
        self.tribB = self.cB[:, 384:512]
        self.maskfB = self.cB[:, 512:640]
        self.maskbB = self.cB[:, 640:768]
        self.gpre = sb("gpre", [128, 6, DC], F32)
        self.gpost = sb("gpost", [128, 6, DC], F32)
        self.gposth = sb("gposth", [128, 6, DC], F32)
        self.g_c = Cell()
        self.dma("sp", self.gpre[:], self.w_in["norm_pre"].rearrange("l j (c p) -> p (l j) c", p=128),
                 writes=[self.g_c], allow_slow_non_contiguous=True)
        self.dma("sp", self.gpost[:], self.w_in["norm_post"].rearrange("l j (c p) -> p (l j) c", p=128),
                 writes=[self.g_c], allow_slow_non_contiguous=True)
        self.V(lambda e: e.tensor_scalar(out=self.gposth[:], in0=self.gpost[:], scalar1=0.5, scalar2=None, op0=ALU.mult),
               [self.g_c], [self.g_c])
        self.flagc = sb("flagc", [128, 1], F32)
        self.maskc = sb("maskc", [128, 1], F32)
        self.flag_c = Cell()
        self.dma("sp", self.flagc[:], self.flag_in.partition_broadcast(128), writes=[self.flag_c])
        self.V(lambda e: e.tensor_scalar(out=self.maskc[:], in0=self.flagc[:], scalar1=-NEGBIG, scalar2=NEGBIG,
                                         op0=ALU.mult, op1=ALU.add), [self.flag_c], [self.flag_c])
        self.epsc = sb("epsc", [128, 1], F32)
        self.eps_c = Cell()
        self.V(lambda e: e.memset(self.epsc[:], EPS), [], [self.eps_c])

    def cast_weights(self):
        def cast(dst, src, cell, rows, cols, nsplit):
            rs = rows // nsplit
            for i in range(nsplit):
                self.dma("pool", dst[i * rs:(i + 1) * rs, :], src[i * rs:(i + 1) * rs, :], accum=[cell])
        order = []
        for li in range(2):
            for fi in range(2):
                order.append((li, fi))
        w = self.w_in
        def ffn(li, fi):
            cast(self.wb[f"gate{li}{fi}"], w["ffn_w_gate"][li, fi], self.wb_cell[f"gate{li}{fi}"], D, DFF, 4)
            cast(self.wb[f"up{li}{fi}"], w["ffn_w_up"][li, fi], self.wb_cell[f"up{li}{fi}"], D, DFF, 4)
            cast(self.wb[f"down{li}{fi}"], w["ffn_w_down"][li, fi], self.wb_cell[f"down{li}{fi}"], DFF, D, 4)
        ffn(0, 0)
        cast(self.wb["ssm_in"], w["ssm_w_in"][0], self.wb_cell["ssm_in"], D, SSM_IN, 8)
        cast(self.wb["ssm_out"], w["ssm_w_out"][0], self.wb_cell["ssm_out"], DIN, D, 2)
        ffn(0, 1)
        ffn(1, 0)
        cast(self.wb["qkv"], w["attn_w_qkv"][0], self.wb_cell["qkv"], D, 3072, 4)
        cast(self.wb["attn_out"], w["attn_w_out"][0], self.wb_cell["attn_out"], D, D, 1)
        ffn(1, 1)


class Bank:
    def __init__(self, ap, cell):
        self.ap = ap
        self.cell = cell


class Pair:
    def __init__(self, ap, c0, c1):
        self.ap = ap
        self.c0 = c0
        self.c1 = c1


def _grid(*dims):
    if len(dims) == 1:
        return [Cell() for _ in range(dims[0])]
    return [_grid(*dims[1:]) for _ in range(dims[0])]


class MK2(MK):
    def setup_psum(self):
        nc = self.nc
        es = self.es
        self.PA = es.enter_context(nc.psum_tensor("PA", [128, 1024], F32))
        self.PB = es.enter_context(nc.psum_tensor("PB", [128, 1024], F32))
        self.PC = es.enter_context(nc.psum_tensor("PC", [128, 1024], F32))
        self.PD0 = es.enter_context(nc.psum_tensor("PD0", [128, 512], F32))
        self.PT = es.enter_context(nc.psum_tensor("PT", [128, 1024], BF16))
        self.pairs = []
        self.banks = []
        for t in (self.PA, self.PB, self.PC):
            c0, c1 = Cell(), Cell()
            self.pairs.append(Pair(t, c0, c1))
            self.banks.append(Bank(t[:, 0:512], c0))
            self.banks.append(Bank(t[:, 512:1024], c1))
        self.bankD = Bank(self.PD0[:, :], Cell())
        self.banks.append(self.bankD)
        self.PT_c = Cell()
        self.rotc = {}

    def rot(self, name, n):
        i = self.rotc.get(name, 0)
        self.rotc[name] = i + 1
        return i % n

    def next_bank(self):
        return self.banks[self.rot("bank", len(self.banks))]

    def next_pair(self):
        p = self.pairs[self.rot("pair", 3)]
        return p

    def tl_alloc(self, es):
        TT, TS = self.TT, self.TS
        sb = lambda n, s, d: self.sb(es, n, s, d)
        self.xT = sb("xT", [128, DC, TT], F32)
        self.xT_c = _grid(DC, TS)
        self.uT = sb("uT", [128, DC, TT], BF16)
        self.uT_c = _grid(DC, TS)
        self.hT = sb("hT", [128, FC, TT], BF16)
        self.hT_c = _grid(FC, TS)
        self.hout = sb("hout", [128, DC, TT], F32)
        self.hout_c = _grid(DC, TS)
        self.sq = sb("sq", [128, DC, 512], BF16)
        self.sq_c = _grid(DC)
        self.NW = 3
        self.wpool = [sb(f"wp{i}", [128, 5632], BF16) for i in range(self.NW)]
        self.wpool_c = _grid(self.NW)
        self.sg = sb("sg", [128, 3, 512], F32)
        self.sg_c = _grid(3)
        self.lnv = sb("lnv", [128, 512], F32)
        self.lnv_c = Cell()
        self.rstd = sb("rstd", [128, 2, 512], F32)
        self.rstd_c = _grid(2)
        self.tmpn = sb("tmpn", [128, 2, 512], F32)
        self.tmpn_c = _grid(2)
        self.xstage = sb("xstage", [128, 2, D], F32)
        self.xstage_c = _grid(2)
        self.zst = sb("zst", [128, 3, 512], BF16)
        self.zst_c = _grid(3)
        self.dst = sb("dst", [128, 2, 64], F32)
        self.dst_c = _grid(2)

    def tsl(self, ts):
        return slice(ts * 512, (ts + 1) * 512)

    def wload(self, W, wcell, KC, col0, pw):
        i = self.rot("w", self.NW)
        buf = self.wpool[i]
        view = buf[:, 0:KC * pw].rearrange("p (k n) -> p k n", k=KC)
        src = W.rearrange("(k p) n -> p k n", p=128)[:, :, col0:col0 + pw]
        self.dma("sp", view, src, reads=[wcell], writes=[self.wpool_c[i]])
        return view, self.wpool_c[i]

    def rms_stats(self, srcs, nfeat):
        C = len(srcs)
        for c, (ap, cl) in enumerate(srcs):
            self.A(lambda e, ap=ap, c=c: e.activation(out=self.sq[:, c, :], in_=ap, func=AF.Square), cl, [self.sq_c[c]])
        bank = self.next_bank()
        for c in range(C):
            self.mm(bank.ap, self.onesB[:], self.sq[:, c, :], c == 0, c == C - 1,
                    [self.onesB_c, self.sq_c[c]], [bank.cell], inc=(c == C - 1))
        j = self.rot("rs", 2)
        self.A(lambda e: e.activation(out=self.lnv[:], in_=bank.ap, func=AF.Ln, scale=1.0 / nfeat, bias=self.epsc[:]),
               [bank.cell, self.eps_c], [self.lnv_c])
        self.A(lambda e: e.activation(out=self.rstd[:, j, :], in_=self.lnv[:], func=AF.Exp, scale=-0.5),
               [self.lnv_c], [self.rstd_c[j]])
        return self.rstd[:, j, :], self.rstd_c[j]

    def prenorm(self, n):
        for ts in range(self.TS):
            sl = self.tsl(ts)
            srcs = [(self.xT[:, c, sl], [self.xT_c[c][ts]]) for c in range(DC)]
            rs, rc = self.rms_stats(srcs, D)
            for c in range(DC):
                self.V(lambda e, c=c: e.scalar_tensor_tensor(out=self.uT[:, c, sl], in0=self.xT[:, c, sl],
                                                             scalar=self.gpre[:, n, c:c + 1], in1=rs,
                                                             op0=ALU.mult, op1=ALU.mult),
                       [self.xT_c[c][ts], self.g_c, rc], [self.uT_c[c][ts]])

    def post_res(self, n, half):
        g = self.gposth if half else self.gpost
        for ts in range(self.TS):
            sl = self.tsl(ts)
            srcs = [(self.hout[:, c, sl], [self.hout_c[c][ts]]) for c in range(DC)]
            rs, rc = self.rms_stats(srcs, D)
            for c in range(DC):
                j = self.rot("tmpn", 2)
                self.V(lambda e, c=c, j=j: e.scalar_tensor_tensor(out=self.tmpn[:, j, :], in0=self.hout[:, c, sl],
                                                                  scalar=g[:, n, c:c + 1], in1=rs,
                                                                  op0=ALU.mult, op1=ALU.mult),
                       [self.hout_c[c][ts], self.g_c, rc], [self.tmpn_c[j]])
                self.G(lambda e, c=c, j=j: e.tensor_tensor(out=self.xT[:, c, sl], in0=self.xT[:, c, sl],
                                                           in1=self.tmpn[:, j, :], op=ALU.add),
                       [self.xT_c[c][ts], self.tmpn_c[j]], [self.xT_c[c][ts]])

    def linear_fm(self, src, src_c, KC, W, wcell, col0, ncols, consumer):
        PW = 512 if KC <= 8 else 256
        for pc0 in range(0, ncols, PW):
            pw = min(PW, ncols - pc0)
            wbuf, wc = self.wload(W, wcell, KC, col0 + pc0, pw)
            for ml in range(pw // 128):
                m = pc0 // 128 + ml
                for ts in range(self.TS):
                    bank = self.next_bank()
                    for kc in range(KC):
                        self.mm(bank.ap, wbuf[:, kc, ml * 128:(ml + 1) * 128], src[:, kc, self.tsl(ts)],
                                kc == 0, kc == KC - 1, [wc, src_c[kc][ts]], [bank.cell], inc=(kc == KC - 1))
                    consumer(m, ts, bank)

    def linear_tm(self, src, src_c, KC, W, wcell, col0, ncols, consumer):
        for pc0 in range(0, ncols, 512):
            pw = min(512, ncols - pc0)
            wbuf, wc = self.wload(W, wcell, KC, col0 + pc0, pw)
            for tb in range(self.TT // 128):
                ts = (tb * 128) // 512
                bank = self.next_bank()
                for kc in range(KC):
                    self.mm(bank.ap[:, 0:pw], src[:, kc, tb * 128:(tb + 1) * 128], wbuf[:, kc, :],
                            kc == 0, kc == KC - 1, [wc, src_c[kc][ts]], [bank.cell], inc=(kc == KC - 1))
                consumer(tb, pc0, pw, bank)

    def ffn(self, li, fi):
        TS = self.TS
        n = li * 3 + (0 if fi == 0 else 2)
        self.prenorm(n)
        Wg, Wu, Wd = self.wb[f"gate{li}{fi}"], self.wb[f"up{li}{fi}"], self.wb[f"down{li}{fi}"]
        cg, cu, cd = self.wb_cell[f"gate{li}{fi}"], self.wb_cell[f"up{li}{fi}"], self.wb_cell[f"down{li}{fi}"]
        for p0 in range(0, FC, 4):
            nf = min(4, FC - p0)
            pw = nf * 128
            gbuf, gc = self.wload(Wg, cg, DC, p0 * 128, pw)
            ubuf, uc = self.wload(Wu, cu, DC, p0 * 128, pw)
            for fl in range(nf):
                f = p0 + fl
                for ts in range(TS):
                    sl = self.tsl(ts)
                    pr = self.next_pair()
                    for kc in range(DC):
                        self.mm(pr.ap[:, 0:512], gbuf[:, kc, fl * 128:(fl + 1) * 128], self.uT[:, kc, sl],
                                kc == 0, kc == DC - 1, [gc, self.uT_c[kc][ts]], [pr.c0], inc=(kc == DC - 1))
                    for kc in range(DC):
                        self.mm(pr.ap[:, 512:1024], ubuf[:, kc, fl * 128:(fl + 1) * 128], self.uT[:, kc, sl],
                                kc == 0, kc == DC - 1, [uc, self.uT_c[kc][ts]], [pr.c1], inc=(kc == DC - 1))
                    j = self.rot("sg", 3)
                    self.A(lambda e, j=j: e.activation(out=self.sg[:, j, :], in_=pr.ap[:, 0:512], func=AF.Silu),
                           [pr.c0], [self.sg_c[j]])
                    self.V(lambda e, j=j, f=f: e.tensor_tensor(out=self.hT[:, f, sl], in0=pr.ap[:, 512:1024],
                                                               in1=self.sg[:, j, :], op=ALU.mult),
                           [pr.c1, self.sg_c[j]], [self.hT_c[f][ts]])

        def cons(m, ts, bank):
            self.V(lambda e: e.tensor_copy(out=self.hout[:, m, self.tsl(ts)], in_=bank.ap),
                   [bank.cell], [self.hout_c[m][ts]])
        self.linear_fm(self.hT, self.hT_c, FC, Wd, cd, 0, D, cons)
        self.post_res(n, half=True)

    def load_x_tile(self, t):
        TT = self.TT
        for tb in range(TT // 128):
            ts = (tb * 128) // 512
            k = self.rot("xs", 2)
            r0 = t * TT + tb * 128
            self.dma("sp", self.xstage[:, k, :], self.x_in[r0:r0 + 128, :], writes=[self.xstage_c[k]])
            for hb in range(2):
                bank = self.next_bank()
                for cl in range(4):
                    c = hb * 4 + cl
                    self.P(lambda e, c=c, cl=cl: e.transpose(bank.ap[:, cl * 128:(cl + 1) * 128],
                                                              self.xstage[:, k, c * 128:(c + 1) * 128], self.identF),
                           [self.xstage_c[k], self.cF_c], [bank.cell], inc=(cl == 3))
                outv = self.xT[:, hb * 4:(hb + 1) * 4, tb * 128:(tb + 1) * 128]
                inv = bank.ap.rearrange("p (c n) -> p c n", c=4)
                wc = [self.xT_c[hb * 4 + cl][ts] for cl in range(4)]
                if hb == 0:
                    self.A(lambda e: e.activation(out=outv, in_=inv, func=AF.Copy), [bank.cell], wc)
                else:
                    self.V(lambda e: e.tensor_copy(out=outv, in_=inv), [bank.cell], wc)

    def store_y_tile(self, t):
        TT = self.TT
        for tb in range(TT // 128):
            ts = (tb * 128) // 512
            k = self.rot("xs", 2)
            for hb in range(2):
                bank = self.next_bank()
                for cl in range(4):
                    c = hb * 4 + cl
                    self.P(lambda e, c=c, cl=cl: e.transpose(bank.ap[:, cl * 128:(cl + 1) * 128],
                                                              self.xT[:, c, tb * 128:(tb + 1) * 128], self.identF),
                           [self.xT_c[c][ts], self.cF_c], [bank.cell], inc=(cl == 3))
                if hb == 0:
                    self.A(lambda e: e.activation(out=self.xstage[:, k, 0:512], in_=bank.ap, func=AF.Copy),
                           [bank.cell], [self.xstage_c[k]])
                else:
                    self.V(lambda e: e.tensor_copy(out=self.xstage[:, k, 512:1024], in_=bank.ap),
                           [bank.cell], [self.xstage_c[k]])
            r0 = t * TT + tb * 128
            self.dma("pool", self.y_out[r0:r0 + 128, :], self.xstage[:, k, :], reads=[self.xstage_c[k]])

    def all_xT_cells(self):
        return cells_of(self.xT_c)

    def store_xres(self, t):
        self.dma("pool", self.xres[t].rearrange("p (c n) -> p c n", c=DC), self.xT[:],
                 reads=self.all_xT_cells(), writes=[self.xres_c[t]])

    def load_xres(self, t):
        self.dma("sp", self.xT[:], self.xres[t].rearrange("p (c n) -> p c n", c=DC),
                 reads=[self.xres_c[t]], writes=self.all_xT_cells())

    def stage_A(self, es):
        TT = self.TT
        for t in range(self.NT):
            tok0 = t * TT
            self.load_x_tile(t)
            self.ffn(0, 0)
            self.prenorm(1)
            W, wc = self.wb["ssm_in"], self.wb_cell["ssm_in"]

            def cons_xbc(m, ts, bank):
                j = self.rot("zst", 3)
                self.V(lambda e: e.tensor_copy(out=self.zst[:, j, :], in_=bank.ap), [bank.cell], [self.zst_c[j]])
                a = tok0 + ts * 512
                self.dma("pool", self.xbc[m][:, a:a + 512], self.zst[:, j, :], reads=[self.zst_c[j]], accum=[self.ssm_c[t]])
            self.linear_fm(self.uT, self.uT_c, DC, W, wc, DIN, CONVD, cons_xbc)

            def cons_z(tb, pc0, pw, bank):
                j = self.rot("zst", 3)
                self.A(lambda e: e.activation(out=self.zst[:, j, 0:pw], in_=bank.ap[:, 0:pw], func=AF.Silu), [bank.cell], [self.zst_c[j]])
                r0 = tok0 + tb * 128
                self.dma("pool", self.zsc[r0:r0 + 128, pc0:pc0 + pw], self.zst[:, j, 0:pw], reads=[self.zst_c[j]], accum=[self.ssm_c[t]])
            self.linear_tm(self.uT, self.uT_c, DC, W, wc, 0, DIN, cons_z)

            def cons_dt(tb, pc0, pw, bank):
                j = self.rot("dst", 2)
                self.V(lambda e: e.tensor_copy(out=self.dst[:, j, :], in_=bank.ap[:, 0:64]), [bank.cell], [self.dst_c[j]])
                r0 = tok0 + tb * 128
                self.dma("pool", self.dtr[r0:r0 + 128, :], self.dst[:, j, :], reads=[self.dst_c[j]], accum=[self.ssm_c[t]])
            self.linear_tm(self.uT, self.uT_c, DC, W, wc, DIN + CONVD, 64, cons_dt)
            self.store_xres(t)

    def mixer_out(self, t, src_dram, src_cell, KC, wkey, n):
        TT = self.TT
        tok0 = t * TT
        u = tok0 // self.UL
        view = self.hT[:, 0:KC, :]
        self.dma("sp", view, src_dram.rearrange("c p n -> p c n")[:, :, tok0:tok0 + TT],
                 reads=[src_cell[u]], writes=[self.hT_c[c] for c in range(KC)])

        def cons(m, ts, bank):
            self.V(lambda e: e.tensor_copy(out=self.hout[:, m, self.tsl(ts)], in_=bank.ap),
                   [bank.cell], [self.hout_c[m][ts]])
        self.linear_fm(self.hT, self.hT_c, KC, self.wb[wkey], self.wb_cell[wkey], 0, D, cons)
        self.post_res(n, half=False)

    def stage_B(self, es):
        TT = self.TT
        for t in range(self.NT):
            tok0 = t * TT
            self.load_xres(t)
            self.mixer_out(t, self.yT, self.yT_c, 16, "ssm_out", 1)
            self.ffn(0, 1)
            self.ffn(1, 0)
            self.prenorm(4)
            W, wc = self.wb["qkv"], self.wb_cell["qkv"]

            def cons_qk(m, ts, bank):
                j = self.rot("zst", 3)
                self.A(lambda e: e.activation(out=self.zst[:, j, :], in_=bank.ap, func=AF.Copy), [bank.cell], [self.zst_c[j]])
                a = tok0 + ts * 512
                dstT = self.qT[m] if m < 8 else self.kT[m - 8]
                self.dma("pool", dstT[:, a:a + 512], self.zst[:, j, :], reads=[self.zst_c[j]], accum=[self.qkv_c[t]])
            self.linear_fm(self.uT, self.uT_c, DC, W, wc, 0, 2048, cons_qk)

            def cons_v(tb, pc0, pw, bank):
                j = self.rot("zst", 3)
                self.V(lambda e: e.tensor_copy(out=self.zst[:, j, 0:pw], in_=bank.ap[:, 0:pw]), [bank.cell], [self.zst_c[j]])
                r0 = tok0 + tb * 128
                self.dma("pool", self.vtm[r0:r0 + 128, pc0:pc0 + pw], self.zst[:, j, 0:pw], reads=[self.zst_c[j]], accum=[self.qkv_c[t]])
            self.linear_tm(self.uT, self.uT_c, DC, W, wc, 2048, 1024, cons_v)
            self.store_xres(t)

    def stage_C(self, es):
        for t in range(self.NT):
            self.load_xres(t)
            self.mixer_out(t, self.oT, self.oT_c, 8, "attn_out", 4)
            self.ffn(1, 1)
            self.store_y_tile(t)


class MK3(MK2):
    def seq_groups(self):
        sgs = [[0, 1]] if self.NU >= 2 else [[0]]
        sgs += [[u] for u in range(2, self.NU)]
        return sgs

    def stage_ssd(self, es):
        UL, NCH = self.UL, self.NCH
        NCm = 2 * NCH
        CSEG = min(1024, UL)
        NBS = CSEG // 128
        sb = lambda n, s, d: self.sb(es, n, s, d)
        w = self.w_in
        a_bc = sb("a_bc", [128, 64], F32)
        dtb_bc = sb("dtb_bc", [128, 64], F32)
        D_bc = sb("D_bc", [128, 32], F32)
        convw = sb("convw", [128, 5, 24], F32)
        convb = sb("convb", [128, 24], F32)
        ng_bc = sb("ng_bc", [128, DIN], F32)
        mask4 = sb("mask4", [128, 2, 512], BF16)
        diagW = sb("diagW", [128, 6, 5, 128], BF16)
        sc = Cell()
        dg_c = Cell()
        self.dma("sp", a_bc[:], w["ssm_a_log"][0].rearrange("d h -> (d h)").partition_broadcast(128), writes=[sc])
        self.dma("sp", dtb_bc[:], w["ssm_dt_bias"][0].rearrange("d h -> (d h)").partition_broadcast(128), writes=[sc])
        self.dma("sp", D_bc[:], w["ssm_d"][0].partition_broadcast(128), writes=[sc])
        self.dma("sp", ng_bc[:], w["ssm_norm"][0].partition_broadcast(128), writes=[sc])
        for k5 in range(5):
            self.dma("sp", convw[:, k5, :], w["ssm_conv_w"][0][k5].rearrange("(c p) -> p c", p=128), writes=[sc], allow_slow_non_contiguous=True)
        self.dma("sp", convb[:], w["ssm_conv_b"][0].rearrange("(c p) -> p c", p=128), writes=[sc], allow_slow_non_contiguous=True)
        self.A(lambda e: e.activation(out=a_bc[:], in_=a_bc[:], func=AF.Exp), [sc], [sc])
        self.V(lambda e: e.tensor_scalar(out=a_bc[:], in0=a_bc[:], scalar1=-1.0, scalar2=None, op0=ALU.mult), [sc], [sc])
        for d, mk_ in enumerate((self.maskfB, self.maskbB)):
            self.V(lambda e, d=d, mk_=mk_: e.tensor_copy(out=mask4[:, d, :].rearrange("p (h i) -> p h i", h=4),
                                                       in_=mk_.unsqueeze(1).broadcast_to([128, 4, 128])), [self.cB_c, sc], [sc])
        xg = sb("xg", [128, NCm, 512], BF16); xg_c = _grid(NCm)
        Bg = sb("Bg", [128, NCm, 128], BF16); Bg_c = _grid(NCm)
        BT = sb("BT", [128, NCm * 128], BF16); BT_c = _grid(NCm)
        CT = sb("CT", [128, NCm * 128], BF16); CT_c = _grid(NCm)
        hpb = sb("hpb", [128, NCm, 512], BF16); hpb_c = _grid(NCm)
        mkdt = lambda n, d_: sb(n, [128, 2, NCm, 8], d_)
        dtraw = mkdt("dtraw", F32); tA = mkdt("tA", F32); tB = mkdt("tB", F32)
        dtv = mkdt("dtv", F32); dav = mkdt("dav", F32); csv = mkdt("csv", F32); csl = mkdt("csl", F32)
        Ef = mkdt("Ef", F32); Dec = mkdt("Dec", F32); CDt = mkdt("CDt", F32)
        da16 = mkdt("da16", BF16); nda16 = mkdt("nda16", BF16)
        dts_c = Cell()
        xin = sb("xin", [128, 3, CSEG + 4], BF16); xin_c = _grid(3)
        cvo = sb("cvo", [128, 2, CSEG], BF16); cvo_c = _grid(2)
        cbT = sb("cbT", [128, 2, 128], F32); cbT_c = _grid(2)
        ex = sb("ex", [128, 4, 512], F32); ex_c = _grid(4)
        WT = sb("WT", [128, 8, 512], BF16); WT_c = _grid(8)
        xdt = sb("xdt", [128, 2, 2, 512], BF16); xdt_c = _grid(2)
        xdec = sb("xdec", [128, 2, 512], BF16); xdec_c = _grid(2)
        xdecm = sb("xdecm", [128, 2, 512], BF16); xdecm_c = _grid(2)
        Hf = sb("Hf", [128, 512], F32); Hf_c = Cell()
        Hb = sb("Hb", [128, 512], F32); Hb_c = Cell()
        Hf16 = sb("Hf16", [128, 512], BF16); Hf16_c = Cell()
        tH = sb("tH", [128, 2, 512], F32); tH_c = _grid(2)
        t1 = sb("t1", [128, 2, 512], F32); t1_c = _grid(2)
        t2 = sb("t2", [128, 2, 512], F32); t2_c = _grid(2)
        t3 = sb("t3", [128, 2, 512], F32); t3_c = _grid(2)
        yv = sb("yv", [128, 2, 512], F32); yv_c = _grid(2)
        zs = sb("zs", [128, 2, 512], BF16); zs_c = _grid(2)
        yz = sb("yz", [128, 2, 512], F32); yz_c = _grid(2)
        ssq = sb("ssq", [128, 2, 2], F32); ssq_c = _grid(2)
        yn = sb("yn", [128, 2, 512], BF16); yn_c = _grid(2)
        yst = sb("yst", [128, 2, 512], BF16); yst_c = _grid(2)
        _ptc = Cell()
        PTh = [_ptc, _ptc]

        def bc8(ap8):
            return ap8.unsqueeze(2).broadcast_to([128, 8, 64])

        def v3(ap512):
            return ap512.rearrange("p (h d) -> p h d", h=8)

        TTn = self.TT
        for g in range(NG):
            chs = [4 * g, 4 * g + 1, 4 * g + 2, 4 * g + 3, 16 + g, 20 + g]
            for cl, ch in enumerate(chs):
                for k5 in range(5):
                    self.G(lambda e, cl=cl, ch=ch, k5=k5: e.tensor_scalar(out=diagW[:, cl, k5, :], in0=self.identF, scalar1=convw[:, k5, ch:ch + 1],
                                                                          scalar2=None, op0=ALU.mult), [self.cF_c, sc], [dg_c])
            for sg in self.seq_groups():
                NC = len(sg) * NCH
                tok0 = sg[0] * UL
                pair = len(sg) == 2
                sgcells = [self.ssm_c[(tok0 // TTn) + i] for i in range(NC * 128 // TTn)]
                for d in range(2):
                    src = self.dtr[tok0:tok0 + NC * 128, d * 32 + 8 * g:d * 32 + 8 * g + 8].rearrange("(c p) h -> p c h", p=128)
                    self.dma("sp", dtraw[:, d, 0:NC, :], src, reads=sgcells, writes=[dts_c])
                S_ = lambda t_: t_[:, :, 0:NC, :]
                bsl = dtb_bc[:].rearrange("p (d h) -> p d h", d=2)[:, :, 8 * g:8 * g + 8].unsqueeze(2).broadcast_to([128, 2, NC, 8])
                asl = a_bc[:].rearrange("p (d h) -> p d h", d=2)[:, :, 8 * g:8 * g + 8].unsqueeze(2).broadcast_to([128, 2, NC, 8])
                self.V(lambda e: e.tensor_tensor(out=S_(tA), in0=S_(dtraw), in1=bsl, op=ALU.add), [dts_c, sc], [dts_c])
                self.A(lambda e: e.activation(out=S_(tB), in_=S_(tA), func=AF.Abs), [dts_c], [dts_c])
                self.A(lambda e: e.activation(out=S_(tB), in_=S_(tB), func=AF.Exp, scale=-1.0), [dts_c], [dts_c])
                self.A(lambda e: e.activation(out=S_(tB), in_=S_(tB), func=AF.Ln, bias=1.0), [dts_c], [dts_c])
                self.V(lambda e: e.scalar_tensor_tensor(out=S_(dtv), in0=S_(tA), scalar=0.0, in1=S_(tB), op0=ALU.max, op1=ALU.add), [dts_c], [dts_c])
                self.V(lambda e: e.tensor_tensor(out=S_(dav), in0=S_(dtv), in1=asl, op=ALU.mult), [dts_c, sc], [dts_c])
                self.V(lambda e: e.tensor_copy(out=S_(da16), in_=S_(dav)), [dts_c], [dts_c])
                self.V(lambda e: e.tensor_scalar(out=S_(nda16), in0=S_(dav), scalar1=-1.0, scalar2=None, op0=ALU.mult), [dts_c], [dts_c])
                bk1 = self.next_bank()
                bk2 = self.next_bank()
                for c in range(NC):
                    for d in range(2):
                        tri = self.trifB if d == 0 else self.tribB
                        o = (d * NC + c) * 8
                        self.mm(bk1.ap[:, o:o + 8], tri, da16[:, d, c, :], True, True, [self.cB_c, dts_c], [bk1.cell], inc=False)
                        self.mm(bk2.ap[:, o:o + 8], self.onesB[:], da16[:, d, c, :], True, True, [self.onesB_c, dts_c], [bk1.cell, bk2.cell], inc=(c == NC - 1 and d == 1))
                vb = lambda bk: bk.ap[:, 0:2 * NC * 8].rearrange("p (d c h) -> p d c h", d=2, c=NC)
                self.V(lambda e: e.tensor_copy(out=S_(csv), in_=vb(bk1)), [bk1.cell], [dts_c])
                self.V(lambda e: e.tensor_copy(out=S_(csl), in_=vb(bk2)), [bk2.cell], [dts_c])
                self.A(lambda e: e.activation(out=S_(Ef), in_=S_(csv), func=AF.Exp), [dts_c], [dts_c])
                self.A(lambda e: e.activation(out=S_(CDt), in_=S_(csl), func=AF.Exp), [dts_c], [dts_c])
                self.V(lambda e: e.tensor_tensor(out=S_(tA), in0=S_(csl), in1=S_(csv), op=ALU.subtract), [dts_c], [dts_c])
                self.A(lambda e: e.activation(out=S_(tA), in_=S_(tA), func=AF.Exp), [dts_c], [dts_c])
                self.V(lambda e: e.tensor_tensor(out=S_(Dec), in0=S_(tA), in1=S_(dtv), op=ALU.mult), [dts_c], [dts_c])
                for ui, u in enumerate(sg):
                    utok = u * UL
                    co = ui * NCH
                    nseg = UL // CSEG
                    for seg in range(nseg):
                        s0 = utok + seg * CSEG
                        cb0 = co + seg * NBS
                        lh = "real" if seg > 0 else ("flag" if (pair and ui == 1) else "zero")
                        rh = "real" if seg < nseg - 1 else ("flag" if (pair and ui == 0) else "zero")
                        for cl, ch in enumerate(chs):
                            k = self.rot("xin", 3)
                            a0 = s0 - (0 if lh == "zero" else 2)
                            a1 = s0 + CSEG + (0 if rh == "zero" else 2)
                            o0 = 0 if lh != "zero" else 2
                            self.dma("sp", xin[:, k, o0:o0 + (a1 - a0)], self.xbc[ch][:, a0:a1], reads=sgcells, writes=[xin_c[k]])
                            if lh == "zero":
                                self.G(lambda e: e.memset(xin[:, k, 0:2], 0.0), [], [xin_c[k]])
                            elif lh == "flag":
                                self.G(lambda e: e.tensor_scalar(out=xin[:, k, 0:2], in0=xin[:, k, 0:2], scalar1=self.flagc[:, 0:1], scalar2=None, op0=ALU.mult),
                                       [xin_c[k], self.flag_c], [xin_c[k]])
                            if rh == "zero":
                                self.G(lambda e: e.memset(xin[:, k, CSEG + 2:CSEG + 4], 0.0), [], [xin_c[k]])
                            elif rh == "flag":
                                self.G(lambda e: e.tensor_scalar(out=xin[:, k, CSEG + 2:CSEG + 4], in0=xin[:, k, CSEG + 2:CSEG + 4], scalar1=self.flagc[:, 0:1],
                                                                 scalar2=None, op0=ALU.mult), [xin_c[k], self.flag_c], [xin_c[k]])
                            if ch >= 16:
                                dstT, dst_cells = (BT, BT_c) if ch < 20 else (CT, CT_c)
                            else:
                                kk = self.rot("cvo", 2)
                            for b5 in range(CSEG // 512):
                                bk = self.next_bank()
                                for k5 in range(5):
                                    self.mm(bk.ap, diagW[:, cl, k5, :], xin[:, k, b5 * 512 + k5:b5 * 512 + k5 + 512], k5 == 0, k5 == 4,
                                            [dg_c, xin_c[k]], [bk.cell], inc=(k5 == 4))
                                if ch >= 16:
                                    c4 = cb0 + b5 * 4
                                    self.A(lambda e: e.activation(out=dstT[:, c4 * 128:c4 * 128 + 512], in_=bk.ap, func=AF.Silu, bias=convb[:, ch:ch + 1]),
                                           [bk.cell, sc], dst_cells[c4:c4 + 4])
                                else:
                                    self.A(lambda e: e.activation(out=cvo[:, kk, b5 * 512:(b5 + 1) * 512], in_=bk.ap, func=AF.Silu, bias=convb[:, ch:ch + 1]),
                                           [bk.cell, sc], [cvo_c[kk]])
                            if 16 <= ch < 20:
                                for b in range(NBS):
                                    self.P(lambda e, b=b: e.transpose(self.PT[:, b * 128:(b + 1) * 128], BT[:, (cb0 + b) * 128:(cb0 + b + 1) * 128], self.identB),
                                           [BT_c[cb0 + b], self.cB_c], PTh, inc=(b == NBS - 1))
                                self.V(lambda e: e.tensor_copy(out=Bg[:, cb0:cb0 + NBS, :], in_=self.PT[:, 0:NBS * 128].rearrange("p (b n) -> p b n", b=NBS)),
                                       PTh, Bg_c[cb0:cb0 + NBS])
                            elif ch < 16:
                                xi = ch - 4 * g
                                for b in range(NBS):
                                    self.P(lambda e, b=b: e.transpose(self.PT[:, b * 128:(b + 1) * 128], cvo[:, kk, b * 128:(b + 1) * 128], self.identB),
                                           [cvo_c[kk], self.cB_c], PTh, inc=(b == NBS - 1))
                                self.V(lambda e: e.tensor_copy(out=xg[:, cb0:cb0 + NBS, xi * 128:(xi + 1) * 128], in_=self.PT[:, 0:NBS * 128].rearrange("p (b n) -> p b n", b=NBS)),
                                       PTh, xg_c[cb0:cb0 + NBS])
                self.V(lambda e: e.memset(Hb[:], 0.0), [], [Hb_c])

                def pre_states(c):
                    k = self.rot("xdec", 2)
                    self.G(lambda e: e.tensor_tensor(out=v3(xdec[:, k, :]), in0=v3(xg[:, c, :]), in1=bc8(Dec[:, 1, c, :]), op=ALU.mult),
                           [xg_c[c], dts_c], [xdec_c[k]])
                    bk = self.next_bank()
                    self.mm(bk.ap, Bg[:, c, :], xdec[:, k, :], True, True, [Bg_c[c], xdec_c[k]], [bk.cell])
                    return bk

                def pre_rec(c, bk):
                    if pair and c == NCH - 1:
                        self.V(lambda e: e.tensor_scalar(out=Hb[:], in0=Hb[:], scalar1=self.flagc[:, 0:1], scalar2=None, op0=ALU.mult), [Hb_c, self.flag_c], [Hb_c])
                    self.A(lambda e: e.activation(out=hpb[:, c, :], in_=Hb[:], func=AF.Copy), [Hb_c], [hpb_c[c]])
                    j = self.rot("tH", 2)
                    self.V(lambda e: e.tensor_tensor(out=v3(tH[:, j, :]), in0=v3(Hb[:]), in1=bc8(CDt[:, 1, c, :]), op=ALU.mult), [Hb_c, dts_c], [tH_c[j]])
                    self.V(lambda e: e.tensor_tensor(out=Hb[:], in0=bk.ap, in1=tH[:, j, :], op=ALU.add), [bk.cell, tH_c[j]], [Hb_c])
                prev = None
                for c in range(NC - 1, -1, -1):
                    bk = pre_states(c)
                    if prev is not None:
                        pre_rec(*prev)
                    prev = (c, bk)
                pre_rec(*prev)
                self.V(lambda e: e.memset(Hf[:], 0.0), [], [Hf_c])
                self.V(lambda e: e.memset(Hf16[:], 0.0), [], [Hf16_c])

                def phase1(c):
                    tc = slice(c * 128, (c + 1) * 128)
                    bcb = self.next_bank()
                    self.mm(bcb.ap[:, 0:128], BT[:, tc], CT[:, tc], True, True, [BT_c[c], CT_c[c]], [bcb.cell])
                    kc_ = self.rot("cbT", 2)
                    self.A(lambda e: e.activation(out=cbT[:, kc_, :], in_=bcb.ap[:, 0:128], func=AF.Copy), [bcb.cell], [cbT_c[kc_]])
                    kx = self.rot("xdt", 2)
                    self.V(lambda e: e.tensor_tensor(out=xdt[:, kx, :, :].rearrange("p d (h e) -> p d h e", h=8),
                                                     in0=v3(xg[:, c, :]).unsqueeze(1).broadcast_to([128, 2, 8, 64]),
                                                     in1=dtv[:, :, c, :].unsqueeze(3).broadcast_to([128, 2, 8, 64]), op=ALU.mult),
                           [xg_c[c], dts_c], [xdt_c[kx]])
                    kd = self.rot("xdecm", 2)
                    self.G(lambda e: e.tensor_tensor(out=v3(xdecm[:, kd, :]), in0=v3(xg[:, c, :]), in1=bc8(Dec[:, 0, c, :]), op=ALU.mult), [xg_c[c], dts_c], [xdecm_c[kd]])
                    kes, wts = [], []
                    for bi in range(4):
                        d = bi // 2
                        h0 = (bi % 2) * 4
                        tri = self.trifB if d == 0 else self.tribB
                        bs = self.next_bank()
                        self.mm(bs.ap, self.identB, mask4[:, d, :], True, False, [self.cB_c, sc], [bs.cell], inc=False)
                        self.mm(bs.ap, tri, nda16[:, d, c, h0:h0 + 4].unsqueeze(2).broadcast_to([128, 4, 128]), False, False, [self.cB_c, dts_c], [bs.cell], inc=False)
                        for hl in range(4):
                            self.mm(bs.ap[:, hl * 128:(hl + 1) * 128], da16[:, d, c, h0 + hl:h0 + hl + 1].broadcast_to([128, 128]), tri, False, hl == 3,
                                    [self.cB_c, dts_c], [bs.cell], inc=(hl == 3))
                        ke = self.rot("ex", 4)
                        self.A(lambda e: e.activation(out=ex[:, ke, :], in_=bs.ap, func=AF.Exp), [bs.cell], [ex_c[ke]])
                        kes.append(ke)
                    for bi in range(4):
                        ke = kes[bi]
                        kw_ = self.rot("WT", 8)
                        self.V(lambda e: e.tensor_tensor(out=WT[:, kw_, :].rearrange("p (h i) -> p h i", h=4), in0=ex[:, ke, :].rearrange("p (h i) -> p h i", h=4),
                                                         in1=cbT[:, kc_, :].unsqueeze(1).broadcast_to([128, 4, 128]), op=ALU.mult), [ex_c[ke], cbT_c[kc_]], [WT_c[kw_]])
                        wts.append(kw_)
                    return (kx, kd, wts)

                def phase2(c, st):
                    kx, kd, wts = st
                    tc = slice(c * 128, (c + 1) * 128)
                    if pair and c == NCH:
                        self.V(lambda e: e.tensor_scalar(out=Hf[:], in0=Hf[:], scalar1=self.flagc[:, 0:1], scalar2=None, op0=ALU.mult), [Hf_c, self.flag_c], [Hf_c])
                        self.A(lambda e: e.activation(out=Hf16[:], in_=Hf[:], func=AF.Copy), [Hf_c], [Hf16_c])
                    bof = self.next_bank()
                    self.mm(bof.ap, CT[:, tc], Hf16[:], True, True, [CT_c[c], Hf16_c], [bof.cell])
                    bst = self.next_bank()
                    self.mm(bst.ap, Bg[:, c, :], xdecm[:, kd, :], True, True, [Bg_c[c], xdecm_c[kd]], [bst.cell])
                    j = self.rot("tH", 2)
                    self.V(lambda e: e.tensor_tensor(out=v3(tH[:, j, :]), in0=v3(Hf[:]), in1=bc8(CDt[:, 0, c, :]), op=ALU.mult), [Hf_c, dts_c], [tH_c[j]])
                    self.V(lambda e: e.tensor_tensor(out=Hf[:], in0=bst.ap, in1=tH[:, j, :], op=ALU.add), [bst.cell, tH_c[j]], [Hf_c])
                    self.A(lambda e: e.activation(out=Hf16[:], in_=Hf[:], func=AF.Copy), [Hf_c], [Hf16_c])
                    bob = self.next_bank()
                    self.mm(bob.ap, CT[:, tc], hpb[:, c, :], True, True, [CT_c[c], hpb_c[c]], [bob.cell])
                    by = self.next_bank()
                    for h in range(8):
                        for d in range(2):
                            kw_ = wts[d * 2 + h // 4]
                            hl = h % 4
                            self.mm(by.ap[:, h * 64:(h + 1) * 64], WT[:, kw_, hl * 128:(hl + 1) * 128], xdt[:, kx, d, h * 64:(h + 1) * 64], d == 0, d == 1,
                                    [WT_c[kw_], xdt_c[kx]], [by.cell], inc=(h == 7 and d == 1))
                    kt = self.rot("t1", 2)
                    self.V(lambda e: e.tensor_tensor(out=v3(t1[:, kt, :]), in0=v3(bof.ap), in1=bc8(Ef[:, 0, c, :]), op=ALU.mult), [bof.cell, dts_c], [t1_c[kt]])
                    self.V(lambda e: e.tensor_tensor(out=v3(t2[:, kt, :]), in0=v3(bob.ap), in1=bc8(Ef[:, 1, c, :]), op=ALU.mult), [bob.cell, dts_c], [t2_c[kt]])
                    self.G(lambda e: e.tensor_tensor(out=v3(t3[:, kt, :]), in0=v3(xg[:, c, :]), in1=bc8(D_bc[:, 8 * g:8 * g + 8]), op=ALU.mult), [xg_c[c], sc], [t3_c[kt]])
                    self.G(lambda e: e.tensor_tensor(out=t1[:, kt, :], in0=t1[:, kt, :], in1=t2[:, kt, :], op=ALU.add), [t1_c[kt], t2_c[kt]], [t1_c[kt]])
                    self.G(lambda e: e.tensor_tensor(out=t3[:, kt, :], in0=t3[:, kt, :], in1=t1[:, kt, :], op=ALU.add), [t1_c[kt], t3_c[kt]], [t3_c[kt]])
                    self.V(lambda e: e.tensor_tensor(out=yv[:, kt, :], in0=by.ap, in1=t3[:, kt, :], op=ALU.add), [by.cell, t3_c[kt]], [yv_c[kt]])
                    r0 = tok0 + c * 128
                    self.dma("sp", zs[:, kt, :], self.zsc[r0:r0 + 128, g * 512:(g + 1) * 512], reads=[self.ssm_c[r0 // TTn]], writes=[zs_c[kt]])
                    self.G(lambda e: e.tensor_tensor(out=yz[:, kt, :], in0=yv[:, kt, :], in1=zs[:, kt, :], op=ALU.mult), [yv_c[kt], zs_c[kt]], [yz_c[kt]])
                    self.A(lambda e: e.activation(out=yv[:, kt, :], in_=yz[:, kt, :], func=AF.Square, accum_out=ssq[:, kt, 0:1]), [yz_c[kt]], [yv_c[kt], ssq_c[kt]])
                    self.A(lambda e: e.activation(out=ssq[:, kt, 1:2], in_=ssq[:, kt, 0:1], func=AF.Ln, scale=1.0 / 512, bias=self.epsc[:]), [ssq_c[kt], self.eps_c], [ssq_c[kt]])
                    self.A(lambda e: e.activation(out=ssq[:, kt, 1:2], in_=ssq[:, kt, 1:2], func=AF.Exp, scale=-0.5), [ssq_c[kt]], [ssq_c[kt]])
                    self.V(lambda e: e.scalar_tensor_tensor(out=yn[:, kt, :], in0=yz[:, kt, :], scalar=ssq[:, kt, 1:2], in1=ng_bc[:, g * 512:(g + 1) * 512],
                                                            op0=ALU.mult, op1=ALU.mult), [yz_c[kt], ssq_c[kt], sc], [yn_c[kt]])
                    ph = self.rot("PTh", 2)
                    for k4 in range(4):
                        self.P(lambda e, k4=k4: e.transpose(self.PT[:, ph * 512 + k4 * 128:ph * 512 + (k4 + 1) * 128], yn[:, kt, k4 * 128:(k4 + 1) * 128], self.identB),
                               [yn_c[kt], self.cB_c], [PTh[ph]], inc=(k4 == 3))
                    self.A(lambda e: e.activation(out=yst[:, kt, :], in_=self.PT[:, ph * 512:(ph + 1) * 512], func=AF.Copy), [PTh[ph]], [yst_c[kt]])
                    u = r0 // UL
                    self.dma("pool", self.yT[4 * g:4 * g + 4].rearrange("k p n -> p k n")[:, :, r0:r0 + 128], yst[:, kt, :].rearrange("p (k n) -> p k n", k=4),
                             reads=[yst_c[kt]], accum=[self.yT_c[u]])

                st_next = phase1(0)
                for c in range(NC):
                    st_cur = st_next
                    if c + 1 < NC:
                        st_next = phase1(c + 1)
                    phase2(c, st_cur)


class MK4(MK3):
    def stage_attn(self, es):
        UL = self.UL
        Sm = 2 * UL if self.NU >= 2 else UL
        NBm = Sm // 128
        sb = lambda n, s, d: self.sb(es, n, s, d)
        w = self.w_in
        sc = Cell()
        relb = sb("relb", [128, NBUCK * 8], F32)
        self.dma("sp", relb[:], w["rel_bias"].rearrange("b h -> (b h)").partition_broadcast(128), writes=[sc])
        cLx = sb("cLx", [128, 8], F32)
        cRx = sb("cRx", [128, 8], F32)
        self.V(lambda e: e.tensor_scalar(out=cLx[:], in0=relb[:, 15 * 8:16 * 8], scalar1=self.maskc[:, 0:1], scalar2=None, op0=ALU.add), [sc, self.flag_c], [sc])
        self.V(lambda e: e.tensor_scalar(out=cRx[:], in0=relb[:, 31 * 8:32 * 8], scalar1=self.maskc[:, 0:1], scalar2=None, op0=ALU.add), [sc, self.flag_c], [sc])
        zeroc = sb("zeroc", [128, 1], F32)
        self.V(lambda e: e.memset(zeroc[:], 0.0), [], [sc])
        lamb = sb("lamb", [128, 4, 64], F32)
        self.dma("sp", lamb[:], w["attn_lambda"][0].rearrange("a d -> (a d)").partition_broadcast(128), writes=[sc])
        lt = sb("lt", [128, 2, 64], F32)
        ls = sb("ls", [128, 4], F32)
        self.V(lambda e: e.tensor_tensor(out=lt[:, 0, :], in0=lamb[:, 0, :], in1=lamb[:, 1, :], op=ALU.mult), [sc], [sc])
        self.V(lambda e: e.tensor_tensor(out=lt[:, 1, :], in0=lamb[:, 2, :], in1=lamb[:, 3, :], op=ALU.mult), [sc], [sc])
        self.V(lambda e: e.reduce_sum(out=ls[:, 0:2], in_=lt[:], axis=mybir.AxisListType.X), [sc], [sc])
        self.A(lambda e: e.activation(out=ls[:, 0:2], in_=ls[:, 0:2], func=AF.Exp), [sc], [sc])
        self.V(lambda e: e.tensor_tensor(out=ls[:, 2:3], in0=ls[:, 1:2], in1=ls[:, 0:1], op=ALU.subtract), [sc], [sc])
        self.V(lambda e: e.tensor_scalar(out=ls[:, 3:4], in0=ls[:, 2:3], scalar1=-LAMBDA_INIT, scalar2=None, op0=ALU.add), [sc], [sc])
        neglam = ls[:, 3:4]
        gsub = sb("gsub", [128, 128], F32)
        self.dma("sp", gsub[:], w["attn_subln"][0].partition_broadcast(128), writes=[sc])
        self.V(lambda e: e.tensor_scalar(out=gsub[:], in0=gsub[:], scalar1=1.0 - LAMBDA_INIT, scalar2=None, op0=ALU.mult), [sc], [sc])
        tab = sb("tab", [NBUCK, 8], F32)
        ohs = sb("ohs", [NBUCK, FV], F32)
        fsb = sb("fsb", [8, FV], F32)
        self.dma("sp", tab[:], w["rel_bias"][:, :], writes=[sc])
        self.dma("sp", ohs[:], self.oh_in[:, :], writes=[sc])
        for i0 in range(0, FV, 512):
            n = min(512, FV - i0)
            bk = self.next_bank()
            self.mm(bk.ap[0:8, 0:n], tab[:], ohs[:, i0:i0 + n], True, True, [sc], [bk.cell])
            self.V(lambda e: e.tensor_copy(out=fsb[:, i0:i0 + n], in_=bk.ap[0:8, 0:n]), [bk.cell], [sc])
        self.dma("pool", self.fvec[:, :], fsb[:], reads=[sc], writes=[self.fvec_c])
        QT = sb("QT", [128, 2, Sm], BF16); QT_c = _grid(2)
        KT = sb("KT", [128, 2, Sm], BF16); KT_c = _grid(2)
        Vh = sb("Vh", [128, 2, NBm, 129], BF16); Vh_c = _grid(2)
        self.V(lambda e: e.memset(Vh[:, :, :, 128:129], 1.0), [], Vh_c)
        hk = sb("hk", [128, 2, 9 * 128], F32); hk_c = _grid(2)
        TBr = sb("TBr", [128, 2, 9 * 128], F32); TBr_c = _grid(2)
        et = sb("et", [128, 3, 2, 512], BF16); et_c = _grid(3)
        accs = sb("accs", [128, 2, 8, 129], F32); accs_c = _grid(2)
        rr = sb("rr", [128, 2, 8], F32); rr_c = _grid(2)
        nl = sb("nl", [128, 2, 4], F32); nl_c = _grid(2)
        o0 = sb("o0", [128, 2, 512], F32); o0_c = _grid(2)
        o1 = sb("o1", [128, 2, 512], F32); o1_c = _grid(2)
        sqt = sb("sqt", [128, 512], F32); sqt_c = Cell()
        ss = sb("ss", [128, 2, 8], F32); ss_c = _grid(2)
        on16 = sb("on16", [128, 2, 512], BF16); on_c = _grid(2)
        ost = sb("ost", [128, 2, Sm], BF16); ost_c = _grid(2)
        _ptc = Cell()
        PTh = [_ptc, _ptc]
        acc_banks = [self.banks[4], self.banks[5], self.bankD]
        lg_pairs = [self.pairs[0], self.pairs[1]]

        def acc_ap(a):
            b, sl = divmod(a, 3)
            return acc_banks[b].ap[:, sl * 129:(sl + 1) * 129], acc_banks[b].cell

        sgs = self.seq_groups()
        for h in range(8):
            kh = self.rot("hk", 2)
            hap = bass.AP(tensor=self.fvec.tensor, offset=h * FV + (FV // 2 - 4 * 128 - 127), ap=[[1, 128], [1, 9 * 128]])
            self.dma("sp", hk[:, kh, :], hap, reads=[self.fvec_c], writes=[hk_c[kh]])
            for b0 in range(0, 9, 4):
                nb = min(4, 9 - b0)
                bk = self.next_bank()
                for i in range(b0, b0 + nb):
                    dl = 4 - i
                    self.mm(bk.ap[:, (i - b0) * 128:(i - b0 + 1) * 128], hk[:, kh, (dl + 4) * 128:(dl + 5) * 128], self.antiF, True, True,
                            [hk_c[kh], self.cF_c], [bk.cell], inc=(i == b0 + nb - 1))
                self.V(lambda e: e.tensor_scalar(out=TBr[:, kh, b0 * 128:(b0 + nb) * 128], in0=bk.ap[:, 0:nb * 128], scalar1=8.0, scalar2=None, op0=ALU.mult), [bk.cell], [TBr_c[kh]])
            for sg in sgs:
                S = len(sg) * UL
                NB = S // 128
                tok0 = sg[0] * UL
                kq = self.rot("qkv", 2)
                tcells = [self.qkv_c[(tok0 // self.TT) + i] for i in range(S // self.TT)]
                self.dma("sp", QT[:, kq, 0:S], self.qT[h][:, tok0:tok0 + S], reads=tcells, writes=[QT_c[kq]])
                self.dma("sp", KT[:, kq, 0:S], self.kT[h][:, tok0:tok0 + S], reads=tcells, writes=[KT_c[kq]])
                self.dma("sp", Vh[:, kq, 0:NB, 0:128], self.vtm[tok0:tok0 + S, h * 128:(h + 1) * 128].rearrange("(b p) d -> p b d", p=128),
                         reads=tcells, writes=[Vh_c[kq]])
                ko = self.rot("ost", 2)
                def front(qc, kb):
                    uq = (qc * 512) // UL
                    uk = (kb * 128) // UL
                    dl = kb - 4 * qc
                    near = -1 <= dl <= 4
                    pr = lg_pairs[self.rot("lgp", 2)]
                    for m in range(2):
                        self.mm(pr.ap[:, m * 512:(m + 1) * 512], KT[64 * m:64 * m + 64, kq, kb * 128:(kb + 1) * 128],
                                QT[64 * m:64 * m + 64, kq, qc * 512:(qc + 1) * 512], True, True, [KT_c[kq], QT_c[kq]], [pr.c0, pr.c1], inc=(m == 1))
                    if near:
                        bview = TBr[:, kh, (4 - dl) * 128:(8 - dl) * 128]
                        self.V(lambda e: e.tensor_tensor(out=pr.ap.rearrange("p (m q) -> p m q", m=2), in0=pr.ap.rearrange("p (m q) -> p m q", m=2),
                                                         in1=bview.unsqueeze(1).broadcast_to([128, 2, 512]), op=ALU.add),
                               [pr.c0, pr.c1, TBr_c[kh]], [pr.c0, pr.c1])
                        bcol = self.maskc[:, 0:1] if uk != uq else zeroc[:, 0:1]
                    elif dl < -1:
                        bcol = cLx[:, h:h + 1] if uk != uq else relb[:, 15 * 8 + h:15 * 8 + h + 1]
                    else:
                        bcol = cRx[:, h:h + 1] if uk != uq else relb[:, 31 * 8 + h:31 * 8 + h + 1]
                    ke = self.rot("et", 3)
                    self.A(lambda e: e.activation(out=et[:, ke, :, :], in_=pr.ap.rearrange("p (m q) -> p m q", m=2), func=AF.Exp, scale=SCALE, bias=bcol),
                           [pr.c0, pr.c1, sc, self.flag_c], [et_c[ke]])
                    return ke

                def back(qc, kb, ke):
                    for m in range(2):
                        for qb in range(4):
                            ap_, cell_ = acc_ap(m * 4 + qb)
                            self.mm(ap_, et[:, ke, m, qb * 128:(qb + 1) * 128], Vh[:, kq, kb, :], kb == 0 and (m * 4 + qb) % 3 == 0, kb == NB - 1,
                                    [et_c[ke], Vh_c[kq]], [cell_], inc=(m == 1 and qb == 3), skip=True)
                    if kb == NB - 1:
                        finalize(qc)

                def finalize(qc):
                    ka = self.rot("accs", 2)
                    for b in range(3):
                        ns = 3 if b < 2 else 2
                        self.A(lambda e, b=b, ns=ns: e.activation(out=accs[:, ka, 3 * b:3 * b + ns, :].rearrange("p a d -> p (a d)"), in_=acc_banks[b].ap[:, 0:ns * 129], func=AF.Copy),
                               [acc_banks[b].cell], [accs_c[ka]])
                    self.V(lambda e: e.reciprocal(out=rr[:, ka, :], in_=accs[:, ka, :, 128]), [accs_c[ka]], [rr_c[ka]])
                    self.V(lambda e: e.tensor_scalar(out=nl[:, ka, :], in0=rr[:, ka, 4:8], scalar1=neglam, scalar2=None, op0=ALU.mult), [rr_c[ka], sc], [nl_c[ka]])
                    v4 = lambda ap: ap.rearrange("p (a d) -> p a d", a=4)
                    self.V(lambda e: e.tensor_tensor(out=v4(o0[:, ka, :]), in0=accs[:, ka, 0:4, 0:128], in1=rr[:, ka, 0:4].unsqueeze(2).broadcast_to([128, 4, 128]), op=ALU.mult),
                           [accs_c[ka], rr_c[ka]], [o0_c[ka]])
                    self.V(lambda e: e.tensor_tensor(out=v4(o1[:, ka, :]), in0=accs[:, ka, 4:8, 0:128], in1=nl[:, ka, :].unsqueeze(2).broadcast_to([128, 4, 128]), op=ALU.mult),
                           [accs_c[ka], nl_c[ka]], [o1_c[ka]])
                    self.G(lambda e: e.tensor_tensor(out=o0[:, ka, :], in0=o0[:, ka, :], in1=o1[:, ka, :], op=ALU.add), [o0_c[ka], o1_c[ka]], [o0_c[ka]])
                    self.G(lambda e: e.tensor_tensor(out=sqt[:], in0=o0[:, ka, :], in1=o0[:, ka, :], op=ALU.mult), [o0_c[ka]], [sqt_c])
                    self.V(lambda e: e.reduce_sum(out=ss[:, ka, 0:4], in_=v4(sqt[:]), axis=mybir.AxisListType.X), [sqt_c], [ss_c[ka]])
                    self.A(lambda e: e.activation(out=ss[:, ka, 4:8], in_=ss[:, ka, 0:4], func=AF.Ln, scale=1.0 / 128, bias=self.epsc[:]), [ss_c[ka], self.eps_c], [ss_c[ka]])
                    self.A(lambda e: e.activation(out=ss[:, ka, 4:8], in_=ss[:, ka, 4:8], func=AF.Exp, scale=-0.5), [ss_c[ka]], [ss_c[ka]])
                    self.V(lambda e: e.tensor_tensor(out=v4(o1[:, ka, :]), in0=v4(o0[:, ka, :]), in1=ss[:, ka, 4:8].unsqueeze(2).broadcast_to([128, 4, 128]), op=ALU.mult),
                           [o0_c[ka], ss_c[ka]], [o1_c[ka]])
                    self.V(lambda e: e.tensor_tensor(out=v4(on16[:, ka, :]), in0=v4(o1[:, ka, :]), in1=gsub[:].unsqueeze(1).broadcast_to([128, 4, 128]), op=ALU.mult),
                           [o1_c[ka], sc], [on_c[ka]])
                    ph = self.rot("PTh", 2)
                    for qb in range(4):
                        self.P(lambda e, qb=qb: e.transpose(self.PT[:, ph * 512 + qb * 128:ph * 512 + (qb + 1) * 128], on16[:, ka, qb * 128:(qb + 1) * 128], self.identB),
                               [on_c[ka], self.cB_c], [PTh[ph]], inc=(qb == 3))
                    self.A(lambda e: e.activation(out=ost[:, ko, qc * 512:(qc + 1) * 512], in_=self.PT[:, ph * 512:(ph + 1) * 512], func=AF.Copy), [PTh[ph]], [ost_c[ko]])

                its = [(qc, kb) for qc in range(S // 512) for kb in range(NB)]
                pend = None
                for it in its:
                    ke = front(*it)
                    if pend is not None:
                        back(*pend)
                    pend = (it[0], it[1], ke)
                back(*pend)
                self.dma("pool", self.oT[h][:, tok0:tok0 + S], ost[:, ko, 0:S], reads=[ost_c[ko]], accum=[self.oT_c[u] for u in sg])


WEIGHT_KEYS = ["norm_pre", "norm_post", "ffn_w_gate", "ffn_w_up", "ffn_w_down", "ssm_w_in", "ssm_conv_w",
               "ssm_conv_b", "ssm_dt_bias", "ssm_a_log", "ssm_d", "ssm_norm", "ssm_w_out", "attn_w_qkv",
               "attn_lambda", "attn_subln", "attn_w_out", "rel_bias"]


def build(NU=5, UL=2048, TT=1024, stages="AMBNC", debug=()):
    mk = MKF(NU, UL, TT)
    mk.debug = set(debug)
    mk.declare()
    with mk.es:
        mk.setup_engines()
        mk.setup_psum()
        mk.setup_consts()
        mk.cast_weights()
        for st in stages:
            with ExitStack() as es:
                if st in "ABC":
                    mk.tl_alloc(es)
                    {"A": mk.stage_A, "B": mk.stage_B, "C": mk.stage_C}[st](es)
                elif st == "M":
                    mk.stage_ssd(es)
                elif st == "N":
                    mk.stage_attn(es)
                mk.barrier()
        mk.barrier()
    return mk


_CACHE = {}


def kernel(**inputs):
    x_prompt = np.ascontiguousarray(inputs["x_prompt"], dtype=np.float32)
    x_sample = np.ascontiguousarray(inputs["x_sample"], dtype=np.float32)
    NB, S, _ = x_prompt.shape
    SB, SS, _ = x_sample.shape
    assert (NB, S, SB, SS) == (4, 4096, 32, 2048)
    if "mk" not in _CACHE:
        _CACHE["mk"] = build()
    mk = _CACHE["mk"]
    consts, oh = make_consts()
    in_maps = []
    plan = []
    for c in range(8):
        if c < 4:
            samp = [3 * c, 3 * c + 1, 3 * c + 2]
            xs = np.concatenate([x_prompt[c]] + [x_sample[i] for i in samp], axis=0)
            flag = 1.0
        else:
            samp = [12 + 5 * (c - 4) + i for i in range(5)]
            xs = np.concatenate([x_sample[i] for i in samp], axis=0)
            flag = 0.0
        plan.append(samp)
        m = {"x": np.ascontiguousarray(xs), "flag": np.full((1, 1), flag, np.float32), "consts": consts, "bucket_oh": oh}
        for k in WEIGHT_KEYS:
            m[k] = np.ascontiguousarray(inputs[k], dtype=np.float32)
        in_maps.append(m)
    res = run_bass_kernel_spmd(mk.nc, in_maps, core_ids=list(range(8)))
    y_prompt = np.empty_like(x_prompt)
    y_sample = np.empty_like(x_sample)
    for c in range(8):
        y = np.asarray(res.results[c]["y"], dtype=np.float32)
        off = 0
        if c < 4:
            y_prompt[c] = y[0:4096]
            off = 4096
        for i in plan[c]:
            y_sample[i] = y[off:off + 2048]
            off += 2048
    return (y_prompt, y_sample)

MKF = MK4
```

```python
import math
from contextlib import ExitStack
import numpy as np
import concourse.bass as bass
import concourse.mybir as mybir
from concourse.bass_utils import run_bass_kernel_spmd

F32 = mybir.dt.float32
BF16 = mybir.dt.bfloat16
AF = mybir.ActivationFunctionType
ALU = mybir.AluOpType

D = 1024
DC = 8
DFF = 2816
FC = 22
DIN = 2048
NHS = 32
HD = 64
NG = 4
NST = 128
CONVD = 3072
SSM_IN = 5184
EPS = 1e-6
NBUCK = 32
LAMBDA_INIT = 0.8 - 0.6 * math.exp(-0.3 * 1)
SCALE = 64 ** -0.5
NEGBIG = -30000.0
FV = 1280


def _bucket(rel):
    half = NBUCK // 2
    max_exact = half // 2
    ret = np.where(rel > 0, half, 0)
    n = np.abs(rel)
    nf = np.maximum(n, 1).astype(np.float32)
    large = max_exact + (np.log(nf / np.float32(max_exact)) / np.float32(math.log(128 / max_exact)) * np.float32(half - max_exact)).astype(np.int32)
    large = np.minimum(large, half - 1)
    return ret + np.where(n < max_exact, n, large)


def make_consts():
    c = {}
    i = np.arange(128)
    c["ident"] = np.eye(128, dtype=np.float32)
    c["antiid"] = np.eye(128, dtype=np.float32)[::-1].copy()
    c["trif"] = (i[:, None] <= i[None, :]).astype(np.float32)
    c["trib"] = (i[:, None] >= i[None, :]).astype(np.float32)
    c["maskf"] = np.where(i[None, :] >= i[:, None], 0.0, NEGBIG).astype(np.float32)
    c["maskb"] = np.where(i[None, :] <= i[:, None], 0.0, NEGBIG).astype(np.float32)
    rel = np.arange(-FV // 2, FV // 2)
    b = _bucket(rel)
    oh = np.zeros((NBUCK, FV), np.float32)
    oh[b, np.arange(FV)] = 1.0
    c["bucket_oh"] = oh
    return np.concatenate([c["ident"], c["antiid"], c["trif"], c["trib"], c["maskf"], c["maskb"]], axis=1), oh


class Cell:
    __slots__ = ("w", "r", "aw")

    def __init__(self):
        self.w = None
        self.r = {}
        self.aw = {}


class Sem:
    __slots__ = ("h", "id")

    def __init__(self, h, i):
        self.h = h
        self.id = i


class Eng:
    def __init__(self, name, eng, sem, is_pe=False):
        self.name = name
        self.eng = eng
        self.sem = sem
        self.count = 0
        self.seen = {}
        self.is_pe = is_pe
        self.pend_r = []
        self.pend_w = []


class Slot:
    def __init__(self, sem):
        self.sem = sem
        self.val = 0


def cells_of(x):
    if isinstance(x, Cell):
        return [x]
    out = []
    for y in x:
        out.extend(cells_of(y))
    return out


class MK:
    def __init__(self, NU=5, UL=2048, TT=1024, debug_stage=None):
        self.NU, self.UL, self.TT = NU, UL, TT
        self.SW = 512
        assert TT % 512 == 0 and UL % TT == 0
        self.TS = TT // 512
        self.NTOK = NU * UL
        self.NT = self.NTOK // TT
        self.NCH = UL // 128
        self.debug_stage = debug_stage
        self.nc = bass.Bass("TRN2", target_bir_lowering=False)
        self.es = ExitStack()
        self.n_inst = 0
        self._uid = 0

    def uid(self, p):
        self._uid += 1
        return f"{p}{self._uid}"

    def sb(self, es, name, shape, dt):
        return es.enter_context(self.nc.sbuf_tensor(self.uid(name), list(shape), dt))

    def dram(self, name, shape, dt, kind="Internal"):
        if name in getattr(self, "debug", ()):
            kind = "ExternalOutput"
        return self.nc.dram_tensor(name, list(shape), dt, kind=kind).ap()

    def setup_engines(self):
        nc = self.nc
        self.sems = []

        def mksem(name):
            h = self.es.enter_context(nc.semaphore(name))
            s = Sem(h, len(self.sems))
            self.sems.append(s)
            return s
        self.E = {
            "pe": Eng("pe", nc.tensor, mksem("s_pe"), is_pe=True),
            "act": Eng("act", nc.scalar, mksem("s_act")),
            "dve": Eng("dve", nc.vector, mksem("s_dve")),
            "pool": Eng("pool", nc.gpsimd, mksem("s_pool")),
            "sp": Eng("sp", nc.sync, mksem("s_sp")),
        }
        NS = 16
        self.slots = {q: [Slot(mksem(f"d_{q}{i}")) for i in range(NS)] for q in ("sp", "pool")}
        self.slot_i = {"sp": 0, "pool": 0}

    def _waits(self, e, reads, writes, is_dma=False, accum=()):
        need = {}

        def add(tok, raw):
            s, v = tok
            if (not is_dma) and s is e.sem:
                if e.is_pe:
                    return
            if need.get(s.id, (None, 0))[1] < v:
                need[s.id] = (s, v)
        for c in reads:
            if c.w is not None:
                add(c.w, True)
            for sid, tok in c.aw.items():
                add(tok, True)
        for c in writes:
            if c.w is not None:
                add(c.w, False)
            for sid, tok in c.aw.items():
                add(tok, False)
            for sid, tok in c.r.items():
                add(tok, False)
        for c in accum:
            if c.w is not None:
                add(c.w, False)
            for sid, tok in c.r.items():
                add(tok, False)
        for sid, (s, v) in need.items():
            if e.seen.get(sid, 0) < v:
                e.eng.wait_ge(s.h, v)
                e.seen[sid] = v
                self.n_inst += 1

    def emit(self, en, make, reads=(), writes=(), inc=True):
        e = self.E[en]
        reads = cells_of(reads)
        writes = cells_of(writes)
        self._waits(e, reads, writes)
        ins = make(e.eng)
        self.n_inst += 1
        if not inc:
            e.pend_r.extend(reads)
            e.pend_w.extend(writes)
            return ins
        e.count += 1
        ins.then_inc(e.sem.h, 1)
        tok = (e.sem, e.count)
        for c in e.pend_r + reads:
            c.r[e.sem.id] = tok
        for c in e.pend_w + writes:
            c.w = tok
            c.r = {}
            c.aw = {}
        e.pend_r = []
        e.pend_w = []
        return ins

    def dma(self, q, out, in_, reads=(), writes=(), accum=(), **kw):
        e = self.E[q]
        reads = cells_of(reads)
        writes = cells_of(writes)
        accum = cells_of(accum)
        i = self.slot_i[q]
        self.slot_i[q] = i + 1
        sl = self.slots[q][i % len(self.slots[q])]
        if sl.val > 0 and e.seen.get(sl.sem.id, 0) < sl.val:
            e.eng.wait_ge(sl.sem.h, sl.val)
            e.seen[sl.sem.id] = sl.val
        self._waits(e, reads, writes, is_dma=True, accum=accum)
        ins = e.eng.dma_start(out=out, in_=in_, **kw)
        self.n_inst += 1
        sl.val += 16
        ins.then_inc(sl.sem.h, 16)
        tok = (sl.sem, sl.val)
        for c in reads:
            c.r[sl.sem.id] = tok
        for c in writes:
            c.w = tok
            c.r = {}
            c.aw = {}
        for c in accum:
            c.aw[sl.sem.id] = tok
        return ins

    def barrier(self):
        toks = []
        for e in self.E.values():
            assert not e.pend_r and not e.pend_w
            if e.count > 0:
                toks.append((e.sem, e.count))
        for q in self.slots:
            for sl in self.slots[q]:
                if sl.val > 0:
                    toks.append((sl.sem, sl.val))
        for e in self.E.values():
            for s, v in toks:
                if s is e.sem:
                    continue
                if e.seen.get(s.id, 0) < v:
                    e.eng.wait_ge(s.h, v)
                    e.seen[s.id] = v
                    self.n_inst += 1

    def V(self, make, r=(), w=(), inc=True):
        return self.emit("dve", make, r, w, inc)

    def A(self, make, r=(), w=(), inc=True):
        return self.emit("act", make, r, w, inc)

    def G(self, make, r=(), w=(), inc=True):
        return self.emit("pool", make, r, w, inc)

    def P(self, make, r=(), w=(), inc=True):
        return self.emit("pe", make, r, w, inc)

    def mm(self, out, lhsT, rhs, start, stop, r=(), w=(), inc=True, skip=False):
        if skip:
            return self.P(lambda e: e.matmul(out, lhsT=lhsT, rhs=rhs, start=start, stop=stop, skip_group_check=True), r, w, inc)
        return self.P(lambda e: e.matmul(out, lhsT=lhsT, rhs=rhs, start=start, stop=stop), r, w, inc)

    def declare(self):
        nc = self.nc
        NTOK = self.NTOK
        ext = lambda n, s: nc.dram_tensor(n, list(s), F32, kind="ExternalInput").ap()
        self.x_in = ext("x", [NTOK, D])
        self.flag_in = ext("flag", [1, 1])
        self.consts_in = ext("consts", [128, 6 * 128])
        self.oh_in = ext("bucket_oh", [NBUCK, FV])
        self.w_in = {
            "norm_pre": ext("norm_pre", [2, 3, D]), "norm_post": ext("norm_post", [2, 3, D]),
            "ffn_w_gate": ext("ffn_w_gate", [2, 2, D, DFF]), "ffn_w_up": ext("ffn_w_up", [2, 2, D, DFF]),
            "ffn_w_down": ext("ffn_w_down", [2, 2, DFF, D]),
            "ssm_w_in": ext("ssm_w_in", [1, D, SSM_IN]), "ssm_conv_w": ext("ssm_conv_w", [1, 5, CONVD]),
            "ssm_conv_b": ext("ssm_conv_b", [1, CONVD]), "ssm_dt_bias": ext("ssm_dt_bias", [1, 2, NHS]),
            "ssm_a_log": ext("ssm_a_log", [1, 2, NHS]), "ssm_d": ext("ssm_d", [1, NHS]),
            "ssm_norm": ext("ssm_norm", [1, DIN]), "ssm_w_out": ext("ssm_w_out", [1, DIN, D]),
            "attn_w_qkv": ext("attn_w_qkv", [1, D, 3072]), "attn_lambda": ext("attn_lambda", [1, 4, 64]),
            "attn_subln": ext("attn_subln", [1, 128]), "attn_w_out": ext("attn_w_out", [1, D, D]),
            "rel_bias": ext("rel_bias", [NBUCK, 8]),
        }
        self.y_out = nc.dram_tensor("y", [NTOK, D], F32, kind="ExternalOutput").ap()
        self.wb = {}
        self.wb_cell = {}
        for li in range(2):
            for fi in range(2):
                for nm, shp in (("gate", [D, DFF]), ("up", [D, DFF]), ("down", [DFF, D])):
                    k = f"{nm}{li}{fi}"
                    self.wb[k] = self.dram("wb_" + k, shp, BF16)
                    self.wb_cell[k] = Cell()
        for k, shp in (("ssm_in", [D, SSM_IN]), ("ssm_out", [DIN, D]), ("qkv", [D, 3072]), ("attn_out", [D, D])):
            self.wb[k] = self.dram("wb_" + k, shp, BF16)
            self.wb_cell[k] = Cell()
        NT, TT = self.NT, self.TT
        self.xres = self.dram("xres", [NT, 128, DC * TT], F32)
        self.xres_c = [Cell() for _ in range(NT)]
        self.xbc = self.dram("xbc", [24, 128, NTOK], BF16)
        self.zsc = self.dram("zsc", [NTOK, DIN], BF16)
        self.dtr = self.dram("dtr", [NTOK, 64], F32)
        self.ssm_c = [Cell() for _ in range(NT)]
        self.yT = self.dram("yT", [16, 128, NTOK], BF16)
        self.yT_c = [Cell() for _ in range(self.NU)]
        self.qT = self.dram("qT", [8, 128, NTOK], BF16)
        self.kT = self.dram("kT", [8, 128, NTOK], BF16)
        self.vtm = self.dram("vtm", [NTOK, D], BF16)
        self.qkv_c = [Cell() for _ in range(NT)]
        self.oT = self.dram("oT", [8, 128, NTOK], BF16)
        self.oT_c = [Cell() for _ in range(self.NU)]
        self.fvec = self.dram("fvec", [8, FV], F32)
        self.fvec_c = Cell()
        self.dbg_out = None

    def setup_consts(self):
        es = self.es
        nc = self.nc
        sb = lambda n, s, d: self.sb(es, n, s, d)
        self.cF = sb("cF", [128, 6 * 128], F32)
        self.cF_c = Cell()
        self.cB = sb("cB", [128, 6 * 128], BF16)
        self.cB_c = Cell()
        self.onesB = sb("onesB", [128, 128], BF16)
        self.onesB_c = Cell()
        self.dma("sp", self.cF[:], self.consts_in[:, :], writes=[self.cF_c])
        self.V(lambda e: e.tensor_copy(out=self.cB[:], in_=self.cF[:]), [self.cF_c], [self.cB_c])
        self.V(lambda e: e.memset(self.onesB[:], 1.0), [], [self.onesB_c])
        self.identF = self.cF[:, 0:128]
        self.antiF = self.cF[:, 128:256]
        self.identB = self.cB[:, 0:128]
        self.trifB = self.cB[:, 256:384]
        self.tribB = self.cB[:, 384:512]
        self.maskfB = self.cB[:, 512:640]
        self.maskbB = self.cB[:, 640:768]
        self.gpre = sb("gpre", [128, 6, DC], F32)
        self.gpost = sb("gpost", [128, 6, DC], F32)
        self.gposth = sb("gposth", [128, 6, DC], F32)
        self.g_c = Cell()
        self.dma("sp", self.gpre[:], self.w_in["norm_pre"].rearrange("l j (c p) -> p (l j) c", p=128),
                 writes=[self.g_c], allow_slow_non_contiguous=True)
        self.dma("sp", self.gpost[:], self.w_in["norm_post"].rearrange("l j (c p) -> p (l j) c", p=128),
                 writes=[self.g_c], allow_slow_non_contiguous=True)
        self.V(lambda e: e.tensor_scalar(out=self.gposth[:], in0=self.gpost[:], scalar1=0.5, scalar2=None, op0=ALU.mult),
               [self.g_c], [self.g_c])
        self.flagc = sb("flagc", [128, 1], F32)
        self.maskc = sb("maskc", [128, 1], F32)
        self.flag_c = Cell()
        self.dma("sp", self.flagc[:], self.flag_in.partition_broadcast(128), writes=[self.flag_c])
        self.V(lambda e: e.tensor_scalar(out=self.maskc[:], in0=self.flagc[:], scalar1=-NEGBIG, scalar2=NEGBIG,
                                         op0=ALU.mult, op1=ALU.add), [self.flag_c], [self.flag_c])
        self.epsc = sb("epsc", [128, 1], F32)
        self.eps_c = Cell()
        self.V(lambda e: e.memset(self.epsc[:], EPS), [], [self.eps_c])

    def cast_weights(self):
        def cast(dst, src, cell, rows, cols, nsplit):
            rs = rows // nsplit
            for i in range(nsplit):
                self.dma("pool", dst[i * rs:(i + 1) * rs, :], src[i * rs:(i + 1) * rs, :], accum=[cell])
        order = []
        for li in range(2):
            for fi in range(2):
                order.append((li, fi))
        w = self.w_in
        def ffn(li, fi):
            cast(self.wb[f"gate{li}{fi}"], w["ffn_w_gate"][li, fi], self.wb_cell[f"gate{li}{fi}"], D, DFF, 4)
            cast(self.wb[f"up{li}{fi}"], w["ffn_w_up"][li, fi], self.wb_cell[f"up{li}{fi}"], D, DFF, 4)
            cast(self.wb[f"down{li}{fi}"], w["ffn_w_down"][li, fi], self.wb_cell[f"down{li}{fi}"], DFF, D, 4)
        ffn(0, 0)
        cast(self.wb["ssm_in"], w["ssm_w_in"][0], self.wb_cell["ssm_in"], D, SSM_IN, 8)
        cast(self.wb["ssm_out"], w["ssm_w_out"][0], self.wb_cell["ssm_out"], DIN, D, 2)
        ffn(0, 1)
        ffn(1, 0)
        cast(self.wb["qkv"], w["attn_w_qkv"][0], self.wb_cell["qkv"], D, 3072, 4)
        cast(self.wb["attn_out"], w["attn_w_out"][0], self.wb_cell["attn_out"], D, D, 1)
        ffn(1, 1)


class Bank:
    def __init__(self, ap, cell):
        self.ap = ap
        self.cell = cell


class Pair:
    def __init__(self, ap, c0, c1):
        self.ap = ap
        self.c0 = c0
        self.c1 = c1


def _grid(*dims):
    if len(dims) == 1:
        return [Cell() for _ in range(dims[0])]
    return [_grid(*dims[1:]) for _ in range(dims[0])]


class MK2(MK):
    def setup_psum(self):
        nc = self.nc
        es = self.es
        self.PA = es.enter_context(nc.psum_tensor("PA", [128, 1024], F32))
        self.PB = es.enter_context(nc.psum_tensor("PB", [128, 1024], F32))
        self.PC = es.enter_context(nc.psum_tensor("PC", [128, 1024], F32))
        self.PD0 = es.enter_context(nc.psum_tensor("PD0", [128, 512], F32))
        self.PT = es.enter_context(nc.psum_tensor("PT", [128, 1024], BF16))
        self.pairs = []
        self.banks = []
        for t in (self.PA, self.PB, self.PC):
            c0, c1 = Cell(), Cell()
            self.pairs.append(Pair(t, c0, c1))
            self.banks.append(Bank(t[:, 0:512], c0))
            self.banks.append(Bank(t[:, 512:1024], c1))
        self.bankD = Bank(self.PD0[:, :], Cell())
        self.banks.append(self.bankD)
        self.PT_c = Cell()
        self.rotc = {}

    def rot(self, name, n):
        i = self.rotc.get(name, 0)
        self.rotc[name] = i + 1
        return i % n

    def next_bank(self):
        return self.banks[self.rot("bank", len(self.banks))]

    def next_pair(self):
        p = self.pairs[self.rot("pair", 3)]
        return p

    def tl_alloc(self, es):
        TT, TS = self.TT, self.TS
        sb = lambda n, s, d: self.sb(es, n, s, d)
        self.xT = sb("xT", [128, DC, TT], F32)
        self.xT_c = _grid(DC, TS)
        self.uT = sb("uT", [128, DC, TT], BF16)
        self.uT_c = _grid(DC, TS)
        self.hT = sb("hT", [128, FC, TT], BF16)
        self.hT_c = _grid(FC, TS)
        self.hout = sb("hout", [128, DC, TT], F32)
        self.hout_c = _grid(DC, TS)
        self.sq = sb("sq", [128, DC, 512], BF16)
        self.sq_c = _grid(DC)
        self.NW = 3
        self.wpool = [sb(f"wp{i}", [128, 5632], BF16) for i in range(self.NW)]
        self.wpool_c = _grid(self.NW)
        self.sg = sb("sg", [128, 3, 512], F32)
        self.sg_c = _grid(3)
        self.lnv = sb("lnv", [128, 512], F32)
        self.lnv_c = Cell()
        self.rstd = sb("rstd", [128, 2, 512], F32)
        self.rstd_c = _grid(2)
        self.tmpn = sb("tmpn", [128, 2, 512], F32)
        self.tmpn_c = _grid(2)
        self.xstage = sb("xstage", [128, 2, D], F32)
        self.xstage_c = _grid(2)
        self.zst = sb("zst", [128, 3, 512], BF16)
        self.zst_c = _grid(3)
        self.dst = sb("dst", [128, 2, 64], F32)
        self.dst_c = _grid(2)

    def tsl(self, ts):
        return slice(ts * 512, (ts + 1) * 512)

    def wload(self, W, wcell, KC, col0, pw):
        i = self.rot("w", self.NW)
        buf = self.wpool[i]
        view = buf[:, 0:KC * pw].rearrange("p (k n) -> p k n", k=KC)
        src = W.rearrange("(k p) n -> p k n", p=128)[:, :, col0:col0 + pw]
        self.dma("sp", view, src, reads=[wcell], writes=[self.wpool_c[i]])
        return view, self.wpool_c[i]

    def rms_stats(self, srcs, nfeat):
        C = len(srcs)
        for c, (ap, cl) in enumerate(srcs):
            self.A(lambda e, ap=ap, c=c: e.activation(out=self.sq[:, c, :], in_=ap, func=AF.Square), cl, [self.sq_c[c]])
        bank = self.next_bank()
        for c in range(C):
            self.mm(bank.ap, self.onesB[:], self.sq[:, c, :], c == 0, c == C - 1,
                    [self.onesB_c, self.sq_c[c]], [bank.cell], inc=(c == C - 1))
        j = self.rot("rs", 2)
        self.A(lambda e: e.activation(out=self.lnv[:], in_=bank.ap, func=AF.Ln, scale=1.0 / nfeat, bias=self.epsc[:]),
               [bank.cell, self.eps_c], [self.lnv_c])
        self.A(lambda e: e.activation(out=self.rstd[:, j, :], in_=self.lnv[:], func=AF.Exp, scale=-0.5),
               [self.lnv_c], [self.rstd_c[j]])
        return self.rstd[:, j, :], self.rstd_c[j]

    def prenorm(self, n):
        for ts in range(self.TS):
            sl = self.tsl(ts)
            srcs = [(self.xT[:, c, sl], [self.xT_c[c][ts]]) for c in range(DC)]
            rs, rc = self.rms_stats(srcs, D)
            for c in range(DC):
                self.V(lambda e, c=c: e.scalar_tensor_tensor(out=self.uT[:, c, sl], in0=self.xT[:, c, sl],
                                                             scalar=self.gpre[:, n, c:c + 1], in1=rs,
                                                             op0=ALU.mult, op1=ALU.mult),
                       [self.xT_c[c][ts], self.g_c, rc], [self.uT_c[c][ts]])

    def post_res(self, n, half):
        g = self.gposth if half else self.gpost
        for ts in range(self.TS):
            sl = self.tsl(ts)
            srcs = [(self.hout[:, c, sl], [self.hout_c[c][ts]]) for c in range(DC)]
            rs, rc = self.rms_stats(srcs, D)
            for c in range(DC):
                j = self.rot("tmpn", 2)
                self.V(lambda e, c=c, j=j: e.scalar_tensor_tensor(out=self.tmpn[:, j, :], in0=self.hout[:, c, sl],
                                                                  scalar=g[:, n, c:c + 1], in1=rs,
                                                                  op0=ALU.mult, op1=ALU.mult),
                       [self.hout_c[c][ts], self.g_c, rc], [self.tmpn_c[j]])
                self.G(lambda e, c=c, j=j: e.tensor_tensor(out=self.xT[:, c, sl], in0=self.xT[:, c, sl],
                                                           in1=self.tmpn[:, j, :], op=ALU.add),
                       [self.xT_c[c][ts], self.tmpn_c[j]], [self.xT_c[c][ts]])

    def linear_fm(self, src, src_c, KC, W, wcell, col0, ncols, consumer):
        PW = 512 if KC <= 8 else 256
        for pc0 in range(0, ncols, PW):
            pw = min(PW, ncols - pc0)
            wbuf, wc = self.wload(W, wcell, KC, col0 + pc0, pw)
            for ml in range(pw // 128):
                m = pc0 // 128 + ml
                for ts in range(self.TS):
                    bank = self.next_bank()
                    for kc in range(KC):
                        self.mm(bank.ap, wbuf[:, kc, ml * 128:(ml + 1) * 128], src[:, kc, self.tsl(ts)],
                                kc == 0, kc == KC - 1, [wc, src_c[kc][ts]], [bank.cell], inc=(kc == KC - 1))
                    consumer(m, ts, bank)

    def linear_tm(self, src, src_c, KC, W, wcell, col0, ncols, consumer):
        for pc0 in range(0, ncols, 512):
            pw = min(512, ncols - pc0)
            wbuf, wc = self.wload(W, wcell, KC, col0 + pc0, pw)
            for tb in range(self.TT // 128):
                ts = (tb * 128) // 512
                bank = self.next_bank()
                for kc in range(KC):
                    self.mm(bank.ap[:, 0:pw], src[:, kc, tb * 128:(tb + 1) * 128], wbuf[:, kc, :],
                            kc == 0, kc == KC - 1, [wc, src_c[kc][ts]], [bank.cell], inc=(kc == KC - 1))
                consumer(tb, pc0, pw, bank)

    def ffn(self, li, fi):
        TS = self.TS
        n = li * 3 + (0 if fi == 0 else 2)
        self.prenorm(n)
        Wg, Wu, Wd = self.wb[f"gate{li}{fi}"], self.wb[f"up{li}{fi}"], self.wb[f"down{li}{fi}"]
        cg, cu, cd = self.wb_cell[f"gate{li}{fi}"], self.wb_cell[f"up{li}{fi}"], self.wb_cell[f"down{li}{fi}"]
        for p0 in range(0, FC, 4):
            nf = min(4, FC - p0)
            pw = nf * 128
            gbuf, gc = self.wload(Wg, cg, DC, p0 * 128, pw)
            ubuf, uc = self.wload(Wu, cu, DC, p0 * 128, pw)
            for fl in range(nf):
                f = p0 + fl
                for ts in range(TS):
                    sl = self.tsl(ts)
                    pr = self.next_pair()
                    for kc in range(DC):
                        self.mm(pr.ap[:, 0:512], gbuf[:, kc, fl * 128:(fl + 1) * 128], self.uT[:, kc, sl],
                                kc == 0, kc == DC - 1, [gc, self.uT_c[kc][ts]], [pr.c0], inc=(kc == DC - 1))
                    for kc in range(DC):
                        self.mm(pr.ap[:, 512:1024], ubuf[:, kc, fl * 128:(fl + 1) * 128], self.uT[:, kc, sl],
                                kc == 0, kc == DC - 1, [uc, self.uT_c[kc][ts]], [pr.c1], inc=(kc == DC - 1))
                    j = self.rot("sg", 3)
                    self.A(lambda e, j=j: e.activation(out=self.sg[:, j, :], in_=pr.ap[:, 0:512], func=AF.Silu),
                           [pr.c0], [self.sg_c[j]])
                    self.V(lambda e, j=j, f=f: e.tensor_tensor(out=self.hT[:, f, sl], in0=pr.ap[:, 512:1024],
                                                               in1=self.sg[:, j, :], op=ALU.mult),
                           [pr.c1, self.sg_c[j]], [self.hT_c[f][ts]])

        def cons(m, ts, bank):
            self.V(lambda e: e.tensor_copy(out=self.hout[:, m, self.tsl(ts)], in_=bank.ap),
                   [bank.cell], [self.hout_c[m][ts]])
        self.linear_fm(self.hT, self.hT_c, FC, Wd, cd, 0, D, cons)
        self.post_res(n, half=True)

    def load_x_tile(self, t):
        TT = self.TT
        for tb in range(TT // 128):
            ts = (tb * 128) // 512
            k = self.rot("xs", 2)
            r0 = t * TT + tb * 128
            self.dma("sp", self.xstage[:, k, :], self.x_in[r0:r0 + 128, :], writes=[self.xstage_c[k]])
            for hb in range(2):
                bank = self.next_bank()
                for cl in range(4):
                    c = hb * 4 + cl
                    self.P(lambda e, c=c, cl=cl: e.transpose(bank.ap[:, cl * 128:(cl + 1) * 128],
                                                              self.xstage[:, k, c * 128:(c + 1) * 128], self.identF),
                           [self.xstage_c[k], self.cF_c], [bank.cell], inc=(cl == 3))
                outv = self.xT[:, hb * 4:(hb + 1) * 4, tb * 128:(tb + 1) * 128]
                inv = bank.ap.rearrange("p (c n) -> p c n", c=4)
                wc = [self.xT_c[hb * 4 + cl][ts] for cl in range(4)]
                if hb == 0:
                    self.A(lambda e: e.activation(out=outv, in_=inv, func=AF.Copy), [bank.cell], wc)
                else:
                    self.V(lambda e: e.tensor_copy(out=outv, in_=inv), [bank.cell], wc)

    def store_y_tile(self, t):
        TT = self.TT
        for tb in range(TT // 128):
            ts = (tb * 128) // 512
            k = self.rot("xs", 2)
            for hb in range(2):
                bank = self.next_bank()
                for cl in range(4):
                    c = hb * 4 + cl
                    self.P(lambda e, c=c, cl=cl: e.transpose(bank.ap[:, cl * 128:(cl + 1) * 128],
                                                              self.xT[:, c, tb * 128:(tb + 1) * 128], self.identF),
                           [self.xT_c[c][ts], self.cF_c], [bank.cell], inc=(cl == 3))
                if hb == 0:
                    self.A(lambda e: e.activation(out=self.xstage[:, k, 0:512], in_=bank.ap, func=AF.Copy),
                           [bank.cell], [self.xstage_c[k]])
                else:
                    self.V(lambda e: e.tensor_copy(out=self.xstage[:, k, 512:1024], in_=bank.ap),
                           [bank.cell], [self.xstage_c[k]])
            r0 = t * TT + tb * 128
            self.dma("pool", self.y_out[r0:r0 + 128, :], self.xstage[:, k, :], reads=[self.xstage_c[k]])

    def all_xT_cells(self):
        return cells_of(self.xT_c)

    def store_xres(self, t):
        self.dma("pool", self.xres[t].rearrange("p (c n) -> p c n", c=DC), self.xT[:],
                 reads=self.all_xT_cells(), writes=[self.xres_c[t]])

    def load_xres(self, t):
        self.dma("sp", self.xT[:], self.xres[t].rearrange("p (c n) -> p c n", c=DC),
                 reads=[self.xres_c[t]], writes=self.all_xT_cells())

    def stage_A(self, es):
        TT = self.TT
        for t in range(self.NT):
            tok0 = t * TT
            self.load_x_tile(t)
            self.ffn(0, 0)
            self.prenorm(1)
            W, wc = self.wb["ssm_in"], self.wb_cell["ssm_in"]

            def cons_xbc(m, ts, bank):
                j = self.rot("zst", 3)
                self.V(lambda e: e.tensor_copy(out=self.zst[:, j, :], in_=bank.ap), [bank.cell], [self.zst_c[j]])
                a = tok0 + ts * 512
                self.dma("pool", self.xbc[m][:, a:a + 512], self.zst[:, j, :], reads=[self.zst_c[j]], accum=[self.ssm_c[t]])
            self.linear_fm(self.uT, self.uT_c, DC, W, wc, DIN, CONVD, cons_xbc)

            def cons_z(tb, pc0, pw, bank):
                j = self.rot("zst", 3)
                self.A(lambda e: e.activation(out=self.zst[:, j, 0:pw], in_=bank.ap[:, 0:pw], func=AF.Silu), [bank.cell], [self.zst_c[j]])
                r0 = tok0 + tb * 128
                self.dma("pool", self.zsc[r0:r0 + 128, pc0:pc0 + pw], self.zst[:, j, 0:pw], reads=[self.zst_c[j]], accum=[self.ssm_c[t]])
            self.linear_tm(self.uT, self.uT_c, DC, W, wc, 0, DIN, cons_z)

            def cons_dt(tb, pc0, pw, bank):
                j = self.rot("dst", 2)
                self.V(lambda e: e.tensor_copy(out=self.dst[:, j, :], in_=bank.ap[:, 0:64]), [bank.cell], [self.dst_c[j]])
                r0 = tok0 + tb * 128
                self.dma("pool", self.dtr[r0:r0 + 128, :], self.dst[:, j, :], reads=[self.dst_c[j]], accum=[self.ssm_c[t]])
            self.linear_tm(self.uT, self.uT_c, DC, W, wc, DIN + CONVD, 64, cons_dt)
            self.store_xres(t)

    def mixer_out(self, t, src_dram, src_cell, KC, wkey, n):
        TT = self.TT
        tok0 = t * TT
        u = tok0 // self.UL
        view = self.hT[:, 0:KC, :]
        self.dma("sp", view, src_dram.rearrange("c p n -> p c n")[:, :, tok0:tok0 + TT],
                 reads=[src_cell[u]], writes=[self.hT_c[c] for c in range(KC)])

        def cons(m, ts, bank):
            self.V(lambda e: e.tensor_copy(out=self.hout[:, m, self.tsl(ts)], in_=bank.ap),
                   [bank.cell], [self.hout_c[m][ts]])
        self.linear_fm(self.hT, self.hT_c, KC, self.wb[wkey], self.wb_cell[wkey], 0, D, cons)
        self.post_res(n, half=False)

    def stage_B(self, es):
        TT = self.TT
        for t in range(self.NT):
            tok0 = t * TT
            self.load_xres(t)
            self.mixer_out(t, self.yT, self.yT_c, 16, "ssm_out", 1)
            self.ffn(0, 1)
            self.ffn(1, 0)
            self.prenorm(4)
            W, wc = self.wb["qkv"], self.wb_cell["qkv"]

            def cons_qk(m, ts, bank):
                j = self.rot("zst", 3)
                self.A(lambda e: e.activation(out=self.zst[:, j, :], in_=bank.ap, func=AF.Copy), [bank.cell], [self.zst_c[j]])
                a = tok0 + ts * 512
                dstT = self.qT[m] if m < 8 else self.kT[m - 8]
                self.dma("pool", dstT[:, a:a + 512], self.zst[:, j, :], reads=[self.zst_c[j]], accum=[self.qkv_c[t]])
            self.linear_fm(self.uT, self.uT_c, DC, W, wc, 0, 2048, cons_qk)

            def cons_v(tb, pc0, pw, bank):
                j = self.rot("zst", 3)
                self.V(lambda e: e.tensor_copy(out=self.zst[:, j, 0:pw], in_=bank.ap[:, 0:pw]), [bank.cell], [self.zst_c[j]])
                r0 = tok0 + tb * 128
                self.dma("pool", self.vtm[r0:r0 + 128, pc0:pc0 + pw], self.zst[:, j, 0:pw], reads=[self.zst_c[j]], accum=[self.qkv_c[t]])
            self.linear_tm(self.uT, self.uT_c, DC, W, wc, 2048, 1024, cons_v)
            self.store_xres(t)

    def stage_C(self, es):
        for t in range(self.NT):
            self.load_xres(t)
            self.mixer_out(t, self.oT, self.oT_c, 8, "attn_out", 4)
            self.ffn(1, 1)
            self.store_y_tile(t)


class MK3(MK2):
    def seq_groups(self):
        sgs = [[0, 1]] if self.NU >= 2 else [[0]]
        sgs += [[u] for u in range(2, self.NU)]
        return sgs

    def stage_ssd(self, es):
        UL, NCH = self.UL, self.NCH
        NCm = 2 * NCH
        CSEG = min(1024, UL)
        NBS = CSEG // 128
        sb = lambda n, s, d: self.sb(es, n, s, d)
        w = self.w_in
        a_bc = sb("a_bc", [128, 64], F32)
        dtb_bc = sb("dtb_bc", [128, 64], F32)
        D_bc = sb("D_bc", [128, 32], F32)
        convw = sb("convw", [128, 5, 24], F32)
        convb = sb("convb", [128, 24], F32)
        ng_bc = sb("ng_bc", [128, 512], F32)
        ng_c = Cell()
        mask4 = sb("mask4", [128, 2, 512], BF16)
        diagW = sb("diagW", [128, 6, 5, 128], BF16)
        sc = Cell()
        dg_c = Cell()
        self.dma("sp", a_bc[:], w["ssm_a_log"][0].rearrange("d h -> (d h)").partition_broadcast(128), writes=[sc])
        self.dma("sp", dtb_bc[:], w["ssm_dt_bias"][0].rearrange("d h -> (d h)").partition_broadcast(128), writes=[sc])
        self.dma("sp", D_bc[:], w["ssm_d"][0].partition_broadcast(128), writes=[sc])
        for k5 in range(5):
            self.dma("sp", convw[:, k5, :], w["ssm_conv_w"][0][k5].rearrange("(c p) -> p c", p=128), writes=[sc], allow_slow_non_contiguous=True)
        self.dma("sp", convb[:], w["ssm_conv_b"][0].rearrange("(c p) -> p c", p=128), writes=[sc], allow_slow_non_contiguous=True)
        self.A(lambda e: e.activation(out=a_bc[:], in_=a_bc[:], func=AF.Exp), [sc], [sc])
        self.V(lambda e: e.tensor_scalar(out=a_bc[:], in0=a_bc[:], scalar1=-1.0, scalar2=None, op0=ALU.mult), [sc], [sc])
        for d, mk_ in enumerate((self.maskfB, self.maskbB)):
            self.V(lambda e, d=d, mk_=mk_: e.tensor_copy(out=mask4[:, d, :].rearrange("p (h i) -> p h i", h=4),
                                                       in_=mk_.unsqueeze(1).broadcast_to([128, 4, 128])), [self.cB_c, sc], [sc])
        xg = sb("xg", [128, NCm, 512], BF16); xg_c = _grid(NCm)
        Bg = sb("Bg", [128, NCm, 128], BF16); Bg_c = _grid(NCm)
        BT = sb("BT", [128, NCm * 128], BF16); BT_c = _grid(NCm)
        CT = sb("CT", [128, NCm * 128], BF16); CT_c = _grid(NCm)
        hpb = sb("hpb", [128, NCm, 512], BF16); hpb_c = _grid(NCm)
        mkdt = lambda n, d_: sb(n, [128, 2, NCm, 8], d_)
        dtraw = mkdt("dtraw", F32); tA = mkdt("tA", F32); tB = dtraw
        dtv = mkdt("dtv", F32); dav = mkdt("dav", F32); csv = mkdt("csv", F32); csl = mkdt("csl", F32)
        Ef = mkdt("Ef", F32); Dec = mkdt("Dec", F32); CDt = mkdt("CDt", F32)
        da16 = mkdt("da16", BF16); nda16 = mkdt("nda16", BF16)
        dts_c = Cell()
        xin = sb("xin", [128, 2, CSEG + 4], BF16); xin_c = _grid(2)
        cvo = sb("cvo", [128, 2, CSEG], BF16); cvo_c = _grid(2)
        cbT = sb("cbT", [128, 3, 128], F32); cbT_c = _grid(3)
        ex = sb("ex", [128, 8, 512], BF16); ex_c = _grid(8)
        WT = sb("WT", [128, 8, 512], BF16); WT_c = _grid(8)
        xdt = sb("xdt", [128, 3, 2, 512], BF16); xdt_c = _grid(3)
        xdec = sb("xdec", [128, 2, 512], BF16); xdec_c = _grid(2)
        xdecm = sb("xdecm", [128, 3, 512], BF16); xdecm_c = _grid(3)
        Hf = sb("Hf", [128, 512], F32); Hf_c = Cell()
        Hb = sb("Hb", [128, 512], F32); Hb_c = Cell()
        Hf16 = sb("Hf16", [128, 512], BF16); Hf16_c = Cell()
        tH = sb("tH", [128, 2, 512], F32); tH_c = _grid(2)
        t1 = sb("t1", [128, 2, 512], F32); t1_c = _grid(2)
        t2 = sb("t2", [128, 2, 512], F32); t2_c = _grid(2)
        t3 = sb("t3", [128, 5, 512], F32); t3_c = _grid(5)
        ysb = sb("ysb", [128, 3, 512], F32); ysb_c = _grid(3)
        zs = sb("zs", [128, 3, 512], BF16); zs_c = _grid(3)
        yz = sb("yz", [128, 2, 512], F32); yz_c = _grid(2)
        ssq = sb("ssq", [128, 3, 2], F32); ssq_c = _grid(3)
        yn = sb("yn", [128, 2, 512], BF16); yn_c = _grid(2)
        yst = sb("yst", [128, 2, 512], BF16); yst_c = _grid(2)
        _ptc = Cell()
        PTh = [_ptc, _ptc]

        def bc8(ap8):
            return ap8.unsqueeze(2).broadcast_to([128, 8, 64])

        def v3(ap512):
            return ap512.rearrange("p (h d) -> p h d", h=8)

        TTn = self.TT
        for g in range(NG):
            chs = [4 * g, 4 * g + 1, 4 * g + 2, 4 * g + 3, 16 + g, 20 + g]
            self.dma("sp", ng_bc[:], w["ssm_norm"][0][g * 512:(g + 1) * 512].partition_broadcast(128), writes=[ng_c])
            for cl, ch in enumerate(chs):
                for k5 in range(5):
                    self.G(lambda e, cl=cl, ch=ch, k5=k5: e.tensor_scalar(out=diagW[:, cl, k5, :], in0=self.identF, scalar1=convw[:, k5, ch:ch + 1],
                                                                          scalar2=None, op0=ALU.mult), [self.cF_c, sc], [dg_c])
            for sg in self.seq_groups():
                NC = len(sg) * NCH
                tok0 = sg[0] * UL
                pair = len(sg) == 2
                sgcells = [self.ssm_c[(tok0 // TTn) + i] for i in range(NC * 128 // TTn)]
                for d in range(2):
                    src = self.dtr[tok0:tok0 + NC * 128, d * 32 + 8 * g:d * 32 + 8 * g + 8].rearrange("(c p) h -> p c h", p=128)
                    self.dma("sp", dtraw[:, d, 0:NC, :], src, reads=sgcells, writes=[dts_c])
                S_ = lambda t_: t_[:, :, 0:NC, :]
                bsl = dtb_bc[:].rearrange("p (d h) -> p d h", d=2)[:, :, 8 * g:8 * g + 8].unsqueeze(2).broadcast_to([128, 2, NC, 8])
                asl = a_bc[:].rearrange("p (d h) -> p d h", d=2)[:, :, 8 * g:8 * g + 8].unsqueeze(2).broadcast_to([128, 2, NC, 8])
                self.V(lambda e: e.tensor_tensor(out=S_(tA), in0=S_(dtraw), in1=bsl, op=ALU.add), [dts_c, sc], [dts_c])
                self.A(lambda e: e.activation(out=S_(tB), in_=S_(tA), func=AF.Abs), [dts_c], [dts_c])
                self.A(lambda e: e.activation(out=S_(tB), in_=S_(tB), func=AF.Exp, scale=-1.0), [dts_c], [dts_c])
                self.A(lambda e: e.activation(out=S_(tB), in_=S_(tB), func=AF.Ln, bias=1.0), [dts_c], [dts_c])
                self.V(lambda e: e.scalar_tensor_tensor(out=S_(dtv), in0=S_(tA), scalar=0.0, in1=S_(tB), op0=ALU.max, op1=ALU.add), [dts_c], [dts_c])
                self.V(lambda e: e.tensor_tensor(out=S_(dav), in0=S_(dtv), in1=asl, op=ALU.mult), [dts_c, sc], [dts_c])
                self.V(lambda e: e.tensor_copy(out=S_(da16), in_=S_(dav)), [dts_c], [dts_c])
                self.V(lambda e: e.tensor_scalar(out=S_(nda16), in0=S_(dav), scalar1=-1.0, scalar2=None, op0=ALU.mult), [dts_c], [dts_c])
                bk1 = self.next_bank()
                bk2 = self.next_bank()
                for c in range(NC):
                    for d in range(2):
                        tri = self.trifB if d == 0 else self.tribB
                        o = (d * NC + c) * 8
                        self.mm(bk1.ap[:, o:o + 8], tri, da16[:, d, c, :], True, True, [self.cB_c, dts_c], [bk1.cell], inc=False)
                        self.mm(bk2.ap[:, o:o + 8], self.onesB[:], da16[:, d, c, :], True, True, [self.onesB_c, dts_c], [bk1.cell, bk2.cell], inc=(c == NC - 1 and d == 1))
                vb = lambda bk: bk.ap[:, 0:2 * NC * 8].rearrange("p (d c h) -> p d c h", d=2, c=NC)
                self.V(lambda e: e.tensor_copy(out=S_(csv), in_=vb(bk1)), [bk1.cell], [dts_c])
                self.V(lambda e: e.tensor_copy(out=S_(csl), in_=vb(bk2)), [bk2.cell], [dts_c])
                self.A(lambda e: e.activation(out=S_(Ef), in_=S_(csv), func=AF.Exp), [dts_c], [dts_c])
                self.A(lambda e: e.activation(out=S_(CDt), in_=S_(csl), func=AF.Exp), [dts_c], [dts_c])
                self.V(lambda e: e.tensor_tensor(out=S_(tA), in0=S_(csl), in1=S_(csv), op=ALU.subtract), [dts_c], [dts_c])
                self.A(lambda e: e.activation(out=S_(tA), in_=S_(tA), func=AF.Exp), [dts_c], [dts_c])
                self.V(lambda e: e.tensor_tensor(out=S_(Dec), in0=S_(tA), in1=S_(dtv), op=ALU.mult), [dts_c], [dts_c])
                for ui, u in enumerate(sg):
                    utok = u * UL
                    co = ui * NCH
                    nseg = UL // CSEG
                    for seg in range(nseg):
                        s0 = utok + seg * CSEG
                        cb0 = co + seg * NBS
                        lh = "real" if seg > 0 else ("flag" if (pair and ui == 1) else "zero")
                        rh = "real" if seg < nseg - 1 else ("flag" if (pair and ui == 0) else "zero")
                        for cl, ch in enumerate(chs):
                            k = self.rot("xin", 2)
                            a0 = s0 - (0 if lh == "zero" else 2)
                            a1 = s0 + CSEG + (0 if rh == "zero" else 2)
                            o0 = 0 if lh != "zero" else 2
                            self.dma("sp", xin[:, k, o0:o0 + (a1 - a0)], self.xbc[ch][:, a0:a1], reads=sgcells, writes=[xin_c[k]])
                            if lh == "zero":
                                self.G(lambda e: e.memset(xin[:, k, 0:2], 0.0), [], [xin_c[k]])
                            elif lh == "flag":
                                self.G(lambda e: e.tensor_scalar(out=xin[:, k, 0:2], in0=xin[:, k, 0:2], scalar1=self.flagc[:, 0:1], scalar2=None, op0=ALU.mult),
                                       [xin_c[k], self.flag_c], [xin_c[k]])
                            if rh == "zero":
                                self.G(lambda e: e.memset(xin[:, k, CSEG + 2:CSEG + 4], 0.0), [], [xin_c[k]])
                            elif rh == "flag":
                                self.G(lambda e: e.tensor_scalar(out=xin[:, k, CSEG + 2:CSEG + 4], in0=xin[:, k, CSEG + 2:CSEG + 4], scalar1=self.flagc[:, 0:1],
                                                                 scalar2=None, op0=ALU.mult), [xin_c[k], self.flag_c], [xin_c[k]])
                            if ch >= 16:
                                dstT, dst_cells = (BT, BT_c) if ch < 20 else (CT, CT_c)
                            else:
                                kk = self.rot("cvo", 2)
                            for b5 in range(CSEG // 512):
                                bk = self.next_bank()
                                for k5 in range(5):
                                    self.mm(bk.ap, diagW[:, cl, k5, :], xin[:, k, b5 * 512 + k5:b5 * 512 + k5 + 512], k5 == 0, k5 == 4,
                                            [dg_c, xin_c[k]], [bk.cell], inc=(k5 == 4))
                                if ch >= 16:
                                    c4 = cb0 + b5 * 4
                                    self.A(lambda e: e.activation(out=dstT[:, c4 * 128:c4 * 128 + 512], in_=bk.ap, func=AF.Silu, bias=convb[:, ch:ch + 1]),
                                           [bk.cell, sc], dst_cells[c4:c4 + 4])
                                else:
                                    self.A(lambda e: e.activation(out=cvo[:, kk, b5 * 512:(b5 + 1) * 512], in_=bk.ap, func=AF.Silu, bias=convb[:, ch:ch + 1]),
                                           [bk.cell, sc], [cvo_c[kk]])
                            if 16 <= ch < 20:
                                for b in range(NBS):
                                    self.P(lambda e, b=b: e.transpose(self.PT[:, b * 128:(b + 1) * 128], BT[:, (cb0 + b) * 128:(cb0 + b + 1) * 128], self.identB),
                                           [BT_c[cb0 + b], self.cB_c], PTh, inc=(b == NBS - 1))
                                self.V(lambda e: e.tensor_copy(out=Bg[:, cb0:cb0 + NBS, :], in_=self.PT[:, 0:NBS * 128].rearrange("p (b n) -> p b n", b=NBS)),
                                       PTh, Bg_c[cb0:cb0 + NBS])
                            elif ch < 16:
                                xi = ch - 4 * g
                                for b in range(NBS):
                                    self.P(lambda e, b=b: e.transpose(self.PT[:, b * 128:(b + 1) * 128], cvo[:, kk, b * 128:(b + 1) * 128], self.identB),
                                           [cvo_c[kk], self.cB_c], PTh, inc=(b == NBS - 1))
                                self.V(lambda e: e.tensor_copy(out=xg[:, cb0:cb0 + NBS, xi * 128:(xi + 1) * 128], in_=self.PT[:, 0:NBS * 128].rearrange("p (b n) -> p b n", b=NBS)),
                                       PTh, xg_c[cb0:cb0 + NBS])
                self.V(lambda e: e.memset(Hb[:], 0.0), [], [Hb_c])

                def pre_states(c):
                    k = self.rot("xdec", 2)
                    self.G(lambda e: e.tensor_tensor(out=v3(xdec[:, k, :]), in0=v3(xg[:, c, :]), in1=bc8(Dec[:, 1, c, :]), op=ALU.mult),
                           [xg_c[c], dts_c], [xdec_c[k]])
                    bk = self.next_bank()
                    self.mm(bk.ap, Bg[:, c, :], xdec[:, k, :], True, True, [Bg_c[c], xdec_c[k]], [bk.cell])
                    return bk

                def pre_rec(c, bk):
                    if pair and c == NCH - 1:
                        self.V(lambda e: e.tensor_scalar(out=Hb[:], in0=Hb[:], scalar1=self.flagc[:, 0:1], scalar2=None, op0=ALU.mult), [Hb_c, self.flag_c], [Hb_c])
                    self.A(lambda e: e.activation(out=hpb[:, c, :], in_=Hb[:], func=AF.Copy), [Hb_c], [hpb_c[c]])
                    j = self.rot("tH", 2)
                    self.V(lambda e: e.tensor_tensor(out=v3(tH[:, j, :]), in0=v3(Hb[:]), in1=bc8(CDt[:, 1, c, :]), op=ALU.mult), [Hb_c, dts_c], [tH_c[j]])
                    self.V(lambda e: e.tensor_tensor(out=Hb[:], in0=bk.ap, in1=tH[:, j, :], op=ALU.add), [bk.cell, tH_c[j]], [Hb_c])
                prev = None
                for c in range(NC - 1, -1, -1):
                    bk = pre_states(c)
                    if prev is not None:
                        pre_rec(*prev)
                    prev = (c, bk)
                pre_rec(*prev)
                self.V(lambda e: e.memset(Hf[:], 0.0), [], [Hf_c])
                self.V(lambda e: e.memset(Hf16[:], 0.0), [], [Hf16_c])
                stt = {}

                def Sa(c):
                    tc = slice(c * 128, (c + 1) * 128)
                    d_ = stt[c] = {}
                    bcb = self.next_bank()
                    self.mm(bcb.ap[:, 0:128], BT[:, tc], CT[:, tc], True, True, [BT_c[c], CT_c[c]], [bcb.cell])
                    kc_ = d_["cb"] = self.rot("cbT", 3)
                    self.A(lambda e: e.activation(out=cbT[:, kc_, :], in_=bcb.ap[:, 0:128], func=AF.Copy), [bcb.cell], [cbT_c[kc_]])
                    kx = d_["xdt"] = self.rot("xdt", 3)
                    self.V(lambda e: e.tensor_tensor(out=xdt[:, kx, :, :].rearrange("p d (h e) -> p d h e", h=8),
                                                     in0=v3(xg[:, c, :]).unsqueeze(1).broadcast_to([128, 2, 8, 64]),
                                                     in1=dtv[:, :, c, :].unsqueeze(3).broadcast_to([128, 2, 8, 64]), op=ALU.mult),
                           [xg_c[c], dts_c], [xdt_c[kx]])
                    kd = d_["xdec"] = self.rot("xdecm", 3)
                    self.G(lambda e: e.tensor_tensor(out=v3(xdecm[:, kd, :]), in0=v3(xg[:, c, :]), in1=bc8(Dec[:, 0, c, :]), op=ALU.mult), [xg_c[c], dts_c], [xdecm_c[kd]])
                    k3 = d_["t3"] = self.rot("t3", 5)
                    self.G(lambda e: e.tensor_tensor(out=v3(t3[:, k3, :]), in0=v3(xg[:, c, :]), in1=bc8(D_bc[:, 8 * g:8 * g + 8]), op=ALU.mult), [xg_c[c], sc], [t3_c[k3]])
                    d_["ex"] = []
                    for bi in range(4):
                        d = bi // 2
                        h0 = (bi % 2) * 4
                        tri = self.trifB if d == 0 else self.tribB
                        bs = self.next_bank()
                        self.mm(bs.ap, self.identB, mask4[:, d, :], True, False, [self.cB_c, sc], [bs.cell], inc=False)
                        self.mm(bs.ap, tri, nda16[:, d, c, h0:h0 + 4].unsqueeze(2).broadcast_to([128, 4, 128]), False, False, [self.cB_c, dts_c], [bs.cell], inc=False)
                        for hl in range(4):
                            self.mm(bs.ap[:, hl * 128:(hl + 1) * 128], da16[:, d, c, h0 + hl:h0 + hl + 1].broadcast_to([128, 128]), tri, False, hl == 3,
                                    [self.cB_c, dts_c], [bs.cell], inc=(hl == 3))
                        ke = self.rot("ex", 8)
                        self.A(lambda e: e.activation(out=ex[:, ke, :], in_=bs.ap, func=AF.Exp), [bs.cell], [ex_c[ke]])
                        d_["ex"].append(ke)

                def Sb(c):
                    d_ = stt[c]
                    d_["wt"] = []
                    for bi in range(4):
                        ke = d_["ex"][bi]
                        kw_ = self.rot("WT", 8)
                        self.V(lambda e: e.tensor_tensor(out=WT[:, kw_, :].rearrange("p (h i) -> p h i", h=4), in0=ex[:, ke, :].rearrange("p (h i) -> p h i", h=4),
                                                         in1=cbT[:, d_["cb"], :].unsqueeze(1).broadcast_to([128, 4, 128]), op=ALU.mult), [ex_c[ke], cbT_c[d_["cb"]]], [WT_c[kw_]])
                        d_["wt"].append(kw_)

                def Sc(c):
                    d_ = stt[c]
                    tc = slice(c * 128, (c + 1) * 128)
                    kx, kd, wts = d_["xdt"], d_["xdec"], d_["wt"]
                    if pair and c == NCH:
                        self.V(lambda e: e.tensor_scalar(out=Hf[:], in0=Hf[:], scalar1=self.flagc[:, 0:1], scalar2=None, op0=ALU.mult), [Hf_c, self.flag_c], [Hf_c])
                        self.A(lambda e: e.activation(out=Hf16[:], in_=Hf[:], func=AF.Copy), [Hf_c], [Hf16_c])
                    bof = self.next_bank()
                    self.mm(bof.ap, CT[:, tc], Hf16[:], True, True, [CT_c[c], Hf16_c], [bof.cell])
                    bst = self.next_bank()
                    self.mm(bst.ap, Bg[:, c, :], xdecm[:, kd, :], True, True, [Bg_c[c], xdecm_c[kd]], [bst.cell])
                    j = self.rot("tH", 2)
                    self.V(lambda e: e.tensor_tensor(out=v3(tH[:, j, :]), in0=v3(Hf[:]), in1=bc8(CDt[:, 0, c, :]), op=ALU.mult), [Hf_c, dts_c], [tH_c[j]])
                    self.V(lambda e: e.tensor_tensor(out=Hf[:], in0=bst.ap, in1=tH[:, j, :], op=ALU.add), [bst.cell, tH_c[j]], [Hf_c])
                    self.A(lambda e: e.activation(out=Hf16[:], in_=Hf[:], func=AF.Copy), [Hf_c], [Hf16_c])
                    bob = self.next_bank()
                    self.mm(bob.ap, CT[:, tc], hpb[:, c, :], True, True, [CT_c[c], hpb_c[c]], [bob.cell])
                    by = self.next_bank()
                    for h in range(8):
                        for d in range(2):
                            kw_ = wts[d * 2 + h // 4]
                            hl = h % 4
                            self.mm(by.ap[:, h * 64:(h + 1) * 64], WT[:, kw_, hl * 128:(hl + 1) * 128], xdt[:, kx, d, h * 64:(h + 1) * 64], d == 0, d == 1,
                                    [WT_c[kw_], xdt_c[kx]], [by.cell], inc=(h == 7 and d == 1))
                    kt = d_["t1"] = self.rot("t1", 2)
                    self.V(lambda e: e.tensor_tensor(out=v3(t1[:, kt, :]), in0=v3(bof.ap), in1=bc8(Ef[:, 0, c, :]), op=ALU.mult), [bof.cell, dts_c], [t1_c[kt]])
                    self.V(lambda e: e.tensor_tensor(out=v3(t2[:, kt, :]), in0=v3(bob.ap), in1=bc8(Ef[:, 1, c, :]), op=ALU.mult), [bob.cell, dts_c], [t2_c[kt]])
                    ky = d_["ysb"] = self.rot("ysb", 3)
                    self.A(lambda e: e.activation(out=ysb[:, ky, :], in_=by.ap, func=AF.Copy), [by.cell], [ysb_c[ky]])
                    kz = d_["zs"] = self.rot("zs", 3)
                    r0 = tok0 + c * 128
                    self.dma("sp", zs[:, kz, :], self.zsc[r0:r0 + 128, g * 512:(g + 1) * 512], reads=[self.ssm_c[r0 // TTn]], writes=[zs_c[kz]])

                def Sd(c):
                    d_ = stt[c]
                    kt, k3 = d_["t1"], d_["t3"]
                    self.G(lambda e: e.tensor_tensor(out=t1[:, kt, :], in0=t1[:, kt, :], in1=t2[:, kt, :], op=ALU.add), [t1_c[kt], t2_c[kt]], [t1_c[kt]])
                    self.G(lambda e: e.tensor_tensor(out=t3[:, k3, :], in0=t3[:, k3, :], in1=t1[:, kt, :], op=ALU.add), [t1_c[kt], t3_c[k3]], [t3_c[k3]])

                def Se(c):
                    d_ = stt[c]
                    ky, k3, kz = d_["ysb"], d_["t3"], d_["zs"]
                    self.V(lambda e: e.tensor_tensor(out=ysb[:, ky, :], in0=ysb[:, ky, :], in1=t3[:, k3, :], op=ALU.add), [ysb_c[ky], t3_c[k3]], [ysb_c[ky]])
                    kq = d_["yz"] = self.rot("yz", 2)
                    self.V(lambda e: e.tensor_tensor(out=yz[:, kq, :], in0=ysb[:, ky, :], in1=zs[:, kz, :], op=ALU.mult), [ysb_c[ky], zs_c[kz]], [yz_c[kq]])
                    kss = d_["ssq"] = self.rot("ssq", 3)
                    self.A(lambda e: e.activation(out=ysb[:, ky, :], in_=yz[:, kq, :], func=AF.Square, accum_out=ssq[:, kss, 0:1]), [yz_c[kq]], [ysb_c[ky], ssq_c[kss]])

                def Sf(c):
                    d_ = stt[c]
                    kq, kss = d_["yz"], d_["ssq"]
                    self.A(lambda e: e.activation(out=ssq[:, kss, 1:2], in_=ssq[:, kss, 0:1], func=AF.Ln, scale=1.0 / 512, bias=self.epsc[:]), [ssq_c[kss], self.eps_c], [ssq_c[kss]])
                    self.A(lambda e: e.activation(out=ssq[:, kss, 1:2], in_=ssq[:, kss, 1:2], func=AF.Exp, scale=-0.5), [ssq_c[kss]], [ssq_c[kss]])
                    kn = d_["yn"] = self.rot("yn", 2)
                    self.V(lambda e: e.scalar_tensor_tensor(out=yn[:, kn, :], in0=yz[:, kq, :], scalar=ssq[:, kss, 1:2], in1=ng_bc[:],
                                                            op0=ALU.mult, op1=ALU.mult), [yz_c[kq], ssq_c[kss], ng_c], [yn_c[kn]])

                def Sg(c):
                    d_ = stt.pop(c)
                    kn = d_["yn"]
                    for k4 in range(4):
                        self.P(lambda e, k4=k4: e.transpose(self.PT[:, k4 * 128:(k4 + 1) * 128], yn[:, kn, k4 * 128:(k4 + 1) * 128], self.identB),
                               [yn_c[kn], self.cB_c], [_ptc], inc=(k4 == 3))
                    ks = self.rot("yst", 2)
                    self.A(lambda e: e.activation(out=yst[:, ks, :], in_=self.PT[:, 0:512], func=AF.Copy), [_ptc], [yst_c[ks]])
                    r0 = tok0 + c * 128
                    u = r0 // UL
                    self.dma("pool", self.yT[4 * g:4 * g + 4].rearrange("k p n -> p k n")[:, :, r0:r0 + 128], yst[:, ks, :].rearrange("p (k n) -> p k n", k=4),
                             reads=[yst_c[ks]], accum=[self.yT_c[u]])

                order = [(Sc, 2), (Sa, 0), (Sb, 1), (Sd, 3), (Se, 4), (Sf, 5), (Sg, 6)]
                for i in range(NC + 6):
                    for fn, lag in order:
                        c = i - lag
                        if 0 <= c < NC:
                            fn(c)


class MK4(MK3):
    def stage_attn(self, es):
        UL = self.UL
        Sm = 2 * UL if self.NU >= 2 else UL
        NBm = Sm // 128
        sb = lambda n, s, d: self.sb(es, n, s, d)
        w = self.w_in
        sc = Cell()
        relb = sb("relb", [128, NBUCK * 8], F32)
        self.dma("sp", relb[:], w["rel_bias"].rearrange("b h -> (b h)").partition_broadcast(128), writes=[sc])
        cLx = sb("cLx", [128, 8], F32)
        cRx = sb("cRx", [128, 8], F32)
        self.V(lambda e: e.tensor_scalar(out=cLx[:], in0=relb[:, 15 * 8:16 * 8], scalar1=self.maskc[:, 0:1], scalar2=None, op0=ALU.add), [sc, self.flag_c], [sc])
        self.V(lambda e: e.tensor_scalar(out=cRx[:], in0=relb[:, 31 * 8:32 * 8], scalar1=self.maskc[:, 0:1], scalar2=None, op0=ALU.add), [sc, self.flag_c], [sc])
        zeroc = sb("zeroc", [128, 1], F32)
        self.V(lambda e: e.memset(zeroc[:], 0.0), [], [sc])
        lamb = sb("lamb", [128, 4, 64], F32)
        self.dma("sp", lamb[:], w["attn_lambda"][0].rearrange("a d -> (a d)").partition_broadcast(128), writes=[sc])
        lt = sb("lt", [128, 2, 64], F32)
        ls = sb("ls", [128, 4], F32)
        self.V(lambda e: e.tensor_tensor(out=lt[:, 0, :], in0=lamb[:, 0, :], in1=lamb[:, 1, :], op=ALU.mult), [sc], [sc])
        self.V(lambda e: e.tensor_tensor(out=lt[:, 1, :], in0=lamb[:, 2, :], in1=lamb[:, 3, :], op=ALU.mult), [sc], [sc])
        self.V(lambda e: e.reduce_sum(out=ls[:, 0:2], in_=lt[:], axis=mybir.AxisListType.X), [sc], [sc])
        self.A(lambda e: e.activation(out=ls[:, 0:2], in_=ls[:, 0:2], func=AF.Exp), [sc], [sc])
        self.V(lambda e: e.tensor_tensor(out=ls[:, 2:3], in0=ls[:, 1:2], in1=ls[:, 0:1], op=ALU.subtract), [sc], [sc])
        self.V(lambda e: e.tensor_scalar(out=ls[:, 3:4], in0=ls[:, 2:3], scalar1=-LAMBDA_INIT, scalar2=None, op0=ALU.add), [sc], [sc])
        neglam = ls[:, 3:4]
        gsub = sb("gsub", [128, 128], F32)
        self.dma("sp", gsub[:], w["attn_subln"][0].partition_broadcast(128), writes=[sc])
        self.V(lambda e: e.tensor_scalar(out=gsub[:], in0=gsub[:], scalar1=1.0 - LAMBDA_INIT, scalar2=None, op0=ALU.mult), [sc], [sc])
        tab = sb("tab", [NBUCK, 8], F32)
        ohs = sb("ohs", [NBUCK, FV], F32)
        fsb = sb("fsb", [8, FV], F32)
        self.dma("sp", tab[:], w["rel_bias"][:, :], writes=[sc])
        self.dma("sp", ohs[:], self.oh_in[:, :], writes=[sc])
        for i0 in range(0, FV, 512):
            n = min(512, FV - i0)
            bk = self.next_bank()
            self.mm(bk.ap[0:8, 0:n], tab[:], ohs[:, i0:i0 + n], True, True, [sc], [bk.cell])
            self.V(lambda e: e.tensor_copy(out=fsb[:, i0:i0 + n], in_=bk.ap[0:8, 0:n]), [bk.cell], [sc])
        self.dma("pool", self.fvec[:, :], fsb[:], reads=[sc], writes=[self.fvec_c])
        QT = sb("QT", [128, 2, Sm], BF16); QT_c = _grid(2)
        KT = sb("KT", [128, 2, Sm], BF16); KT_c = _grid(2)
        Vh = sb("Vh", [128, 2, NBm, 129], BF16); Vh_c = _grid(2)
        self.V(lambda e: e.memset(Vh[:, :, :, 128:129], 1.0), [], Vh_c)
        hk = sb("hk", [128, 2, 9 * 128], F32); hk_c = _grid(2)
        TBr = sb("TBr", [128, 2, 9 * 128], F32); TBr_c = _grid(2)
        et = sb("et", [128, 3, 2, 512], BF16); et_c = _grid(3)
        accs = sb("accs", [128, 2, 8, 129], F32); accs_c = _grid(2)
        rr = sb("rr", [128, 2, 8], F32); rr_c = _grid(2)
        nl = sb("nl", [128, 2, 4], F32); nl_c = _grid(2)
        o0 = sb("o0", [128, 2, 512], F32); o0_c = _grid(2)
        o1 = sb("o1", [128, 2, 512], F32); o1_c = _grid(2)
        sqt = sb("sqt", [128, 512], F32); sqt_c = Cell()
        ss = sb("ss", [128, 2, 8], F32); ss_c = _grid(2)
        on16 = sb("on16", [128, 2, 512], BF16); on_c = _grid(2)
        ost = sb("ost", [128, 2, Sm], BF16); ost_c = _grid(2)
        _ptc = Cell()
        PTh = [_ptc, _ptc]
        acc_banks = [self.banks[4], self.banks[5], self.bankD]
        lg_pairs = [self.pairs[0], self.pairs[1]]

        def acc_ap(a):
            b, sl = divmod(a, 3)
            return acc_banks[b].ap[:, sl * 129:(sl + 1) * 129], acc_banks[b].cell

        sgs = self.seq_groups()
        for h in range(8):
            kh = self.rot("hk", 2)
            hap = bass.AP(tensor=self.fvec.tensor, offset=h * FV + (FV // 2 - 4 * 128 - 127), ap=[[1, 128], [1, 9 * 128]])
            self.dma("sp", hk[:, kh, :], hap, reads=[self.fvec_c], writes=[hk_c[kh]])
            for b0 in range(0, 9, 4):
                nb = min(4, 9 - b0)
                bk = self.next_bank()
                for i in range(b0, b0 + nb):
                    dl = 4 - i
                    self.mm(bk.ap[:, (i - b0) * 128:(i - b0 + 1) * 128], hk[:, kh, (dl + 4) * 128:(dl + 5) * 128], self.antiF, True, True,
                            [hk_c[kh], self.cF_c], [bk.cell], inc=(i == b0 + nb - 1))
                self.V(lambda e: e.tensor_scalar(out=TBr[:, kh, b0 * 128:(b0 + nb) * 128], in0=bk.ap[:, 0:nb * 128], scalar1=8.0, scalar2=None, op0=ALU.mult), [bk.cell], [TBr_c[kh]])
            for sg in sgs:
                S = len(sg) * UL
                NB = S // 128
                tok0 = sg[0] * UL
                kq = self.rot("qkv", 2)
                tcells = [self.qkv_c[(tok0 // self.TT) + i] for i in range(S // self.TT)]
                self.dma("sp", QT[:, kq, 0:S], self.qT[h][:, tok0:tok0 + S], reads=tcells, writes=[QT_c[kq]])
                self.dma("sp", KT[:, kq, 0:S], self.kT[h][:, tok0:tok0 + S], reads=tcells, writes=[KT_c[kq]])
                self.dma("sp", Vh[:, kq, 0:NB, 0:128], self.vtm[tok0:tok0 + S, h * 128:(h + 1) * 128].rearrange("(b p) d -> p b d", p=128),
                         reads=tcells, writes=[Vh_c[kq]])
                ko = self.rot("ost", 2)
                def front(qc, kb):
                    uq = (qc * 512) // UL
                    uk = (kb * 128) // UL
                    dl = kb - 4 * qc
                    near = -1 <= dl <= 4
                    pr = lg_pairs[self.rot("lgp", 2)]
                    for m in range(2):
                        self.mm(pr.ap[:, m * 512:(m + 1) * 512], KT[64 * m:64 * m + 64, kq, kb * 128:(kb + 1) * 128],
                                QT[64 * m:64 * m + 64, kq, qc * 512:(qc + 1) * 512], True, True, [KT_c[kq], QT_c[kq]], [pr.c0, pr.c1], inc=(m == 1))
                    if near:
                        bview = TBr[:, kh, (4 - dl) * 128:(8 - dl) * 128]
                        self.V(lambda e: e.tensor_tensor(out=pr.ap.rearrange("p (m q) -> p m q", m=2), in0=pr.ap.rearrange("p (m q) -> p m q", m=2),
                                                         in1=bview.unsqueeze(1).broadcast_to([128, 2, 512]), op=ALU.add),
                               [pr.c0, pr.c1, TBr_c[kh]], [pr.c0, pr.c1])
                        bcol = self.maskc[:, 0:1] if uk != uq else zeroc[:, 0:1]
                    elif dl < -1:
                        bcol = cLx[:, h:h + 1] if uk != uq else relb[:, 15 * 8 + h:15 * 8 + h + 1]
                    else:
                        bcol = cRx[:, h:h + 1] if uk != uq else relb[:, 31 * 8 + h:31 * 8 + h + 1]
                    ke = self.rot("et", 3)
                    self.A(lambda e: e.activation(out=et[:, ke, :, :], in_=pr.ap.rearrange("p (m q) -> p m q", m=2), func=AF.Exp, scale=SCALE, bias=bcol),
                           [pr.c0, pr.c1, sc, self.flag_c], [et_c[ke]])
                    return ke

                def back(qc, kb, ke):
                    for m in range(2):
                        for qb in range(4):
                            ap_, cell_ = acc_ap(m * 4 + qb)
                            self.mm(ap_, et[:, ke, m, qb * 128:(qb + 1) * 128], Vh[:, kq, kb, :], kb == 0 and (m * 4 + qb) % 3 == 0, kb == NB - 1,
                                    [et_c[ke], Vh_c[kq]], [cell_], inc=(m == 1 and qb == 3), skip=True)
                    if kb == NB - 1:
                        finalize(qc)

                def finalize(qc):
                    ka = self.rot("accs", 2)
                    for b in range(3):
                        ns = 3 if b < 2 else 2
                        self.A(lambda e, b=b, ns=ns: e.activation(out=accs[:, ka, 3 * b:3 * b + ns, :].rearrange("p a d -> p (a d)"), in_=acc_banks[b].ap[:, 0:ns * 129], func=AF.Copy),
                               [acc_banks[b].cell], [accs_c[ka]])
                    self.V(lambda e: e.reciprocal(out=rr[:, ka, :], in_=accs[:, ka, :, 128]), [accs_c[ka]], [rr_c[ka]])
                    self.V(lambda e: e.tensor_scalar(out=nl[:, ka, :], in0=rr[:, ka, 4:8], scalar1=neglam, scalar2=None, op0=ALU.mult), [rr_c[ka], sc], [nl_c[ka]])
                    v4 = lambda ap: ap.rearrange("p (a d) -> p a d", a=4)
                    self.V(lambda e: e.tensor_tensor(out=v4(o0[:, ka, :]), in0=accs[:, ka, 0:4, 0:128], in1=rr[:, ka, 0:4].unsqueeze(2).broadcast_to([128, 4, 128]), op=ALU.mult),
                           [accs_c[ka], rr_c[ka]], [o0_c[ka]])
                    self.V(lambda e: e.tensor_tensor(out=v4(o1[:, ka, :]), in0=accs[:, ka, 4:8, 0:128], in1=nl[:, ka, :].unsqueeze(2).broadcast_to([128, 4, 128]), op=ALU.mult),
                           [accs_c[ka], nl_c[ka]], [o1_c[ka]])
                    self.G(lambda e: e.tensor_tensor(out=o0[:, ka, :], in0=o0[:, ka, :], in1=o1[:, ka, :], op=ALU.add), [o0_c[ka], o1_c[ka]], [o0_c[ka]])
                    self.G(lambda e: e.tensor_tensor(out=sqt[:], in0=o0[:, ka, :], in1=o0[:, ka, :], op=ALU.mult), [o0_c[ka]], [sqt_c])
                    self.V(lambda e: e.reduce_sum(out=ss[:, ka, 0:4], in_=v4(sqt[:]), axis=mybir.AxisListType.X), [sqt_c], [ss_c[ka]])
                    self.A(lambda e: e.activation(out=ss[:, ka, 4:8], in_=ss[:, ka, 0:4], func=AF.Ln, scale=1.0 / 128, bias=self.epsc[:]), [ss_c[ka], self.eps_c], [ss_c[ka]])
                    self.A(lambda e: e.activation(out=ss[:, ka, 4:8], in_=ss[:, ka, 4:8], func=AF.Exp, scale=-0.5), [ss_c[ka]], [ss_c[ka]])
                    self.V(lambda e: e.tensor_tensor(out=v4(o1[:, ka, :]), in0=v4(o0[:, ka, :]), in1=ss[:, ka, 4:8].unsqueeze(2).broadcast_to([128, 4, 128]), op=ALU.mult),
                           [o0_c[ka], ss_c[ka]], [o1_c[ka]])
                    self.V(lambda e: e.tensor_tensor(out=v4(on16[:, ka, :]), in0=v4(o1[:, ka, :]), in1=gsub[:].unsqueeze(1).broadcast_to([128, 4, 128]), op=ALU.mult),
                           [o1_c[ka], sc], [on_c[ka]])
                    ph = self.rot("PTh", 2)
                    for qb in range(4):
                        self.P(lambda e, qb=qb: e.transpose(self.PT[:, ph * 512 + qb * 128:ph * 512 + (qb + 1) * 128], on16[:, ka, qb * 128:(qb + 1) * 128], self.identB),
                               [on_c[ka], self.cB_c], [PTh[ph]], inc=(qb == 3))
                    self.A(lambda e: e.activation(out=ost[:, ko, qc * 512:(qc + 1) * 512], in_=self.PT[:, ph * 512:(ph + 1) * 512], func=AF.Copy), [PTh[ph]], [ost_c[ko]])

                its = [(qc, kb) for qc in range(S // 512) for kb in range(NB)]
                pend = None
                for it in its:
                    ke = front(*it)
                    if pend is not None:
                        back(*pend)
                    pend = (it[0], it[1], ke)
                back(*pend)
                self.dma("pool", self.oT[h][:, tok0:tok0 + S], ost[:, ko, 0:S], reads=[ost_c[ko]], accum=[self.oT_c[u] for u in sg])


WEIGHT_KEYS = ["norm_pre", "norm_post", "ffn_w_gate", "ffn_w_up", "ffn_w_down", "ssm_w_in", "ssm_conv_w",
               "ssm_conv_b", "ssm_dt_bias", "ssm_a_log", "ssm_d", "ssm_norm", "ssm_w_out", "attn_w_qkv",
               "attn_lambda", "attn_subln", "attn_w_out", "rel_bias"]


def build(NU=5, UL=2048, TT=1024, stages="AMBNC", debug=()):
    mk = MKF(NU, UL, TT)
    mk.debug = set(debug)
    mk.declare()
    with mk.es:
        mk.setup_engines()
        mk.setup_psum()
        mk.setup_consts()
        mk.cast_weights()
        for st in stages:
            with ExitStack() as es:
                if st in "ABC":
                    mk.tl_alloc(es)
                    {"A": mk.stage_A, "B": mk.stage_B, "C": mk.stage_C}[st](es)
                elif st == "M":
                    mk.stage_ssd(es)
                elif st == "N":
                    mk.stage_attn(es)
                mk.barrier()
        mk.barrier()
    return mk


_CACHE = {}


def kernel(**inputs):
    x_prompt = np.ascontiguousarray(inputs["x_prompt"], dtype=np.float32)
    x_sample = np.ascontiguousarray(inputs["x_sample"], dtype=np.float32)
    NB, S, _ = x_prompt.shape
    SB, SS, _ = x_sample.shape
    assert (NB, S, SB, SS) == (4, 4096, 32, 2048)
    if "mk" not in _CACHE:
        _CACHE["mk"] = build()
    mk = _CACHE["mk"]
    consts, oh = make_consts()
    in_maps = []
    plan = []
    for c in range(8):
        if c < 4:
            samp = [3 * c, 3 * c + 1, 3 * c + 2]
            xs = np.concatenate([x_prompt[c]] + [x_sample[i] for i in samp], axis=0)
            flag = 1.0
        else:
            samp = [12 + 5 * (c - 4) + i for i in range(5)]
            xs = np.concatenate([x_sample[i] for i in samp], axis=0)
            flag = 0.0
        plan.append(samp)
        m = {"x": np.ascontiguousarray(xs), "flag": np.full((1, 1), flag, np.float32), "consts": consts, "bucket_oh": oh}
        for k in WEIGHT_KEYS:
            m[k] = np.ascontiguousarray(inputs[k], dtype=np.float32)
        in_maps.append(m)
    res = run_bass_kernel_spmd(mk.nc, in_maps, core_ids=list(range(8)))
    y_prompt = np.empty_like(x_prompt)
    y_sample = np.empty_like(x_sample)
    for c in range(8):
        y = np.asarray(res.results[c]["y"], dtype=np.float32)
        off = 0
        if c < 4:
            y_prompt[c] = y[0:4096]
            off = 4096
        for i in plan[c]:
            y_sample[i] = y[off:off + 2048]
            off += 2048
    return (y_prompt, y_sample)

MKF = MK4
```

```python
import math
from contextlib import ExitStack
import numpy as np
import concourse.bass as bass
import concourse.mybir as mybir
from concourse.bass_utils import run_bass_kernel_spmd

F32 = mybir.dt.float32
BF16 = mybir.dt.bfloat16
AF = mybir.ActivationFunctionType
ALU = mybir.AluOpType

D = 1024
DC = 8
DFF = 2816
FC = 22
DIN = 2048
NHS = 32
HD = 64
NG = 4
NST = 128
CONVD = 3072
SSM_IN = 5184
EPS = 1e-6
NBUCK = 32
LAMBDA_INIT = 0.8 - 0.6 * math.exp(-0.3 * 1)
SCALE = 64 ** -0.5
NEGBIG = -30000.0
FV = 1280


def _bucket(rel):
    half = NBUCK // 2
    max_exact = half // 2
    ret = np.where(rel > 0, half, 0)
    n = np.abs(rel)
    nf = np.maximum(n, 1).astype(np.float32)
    large = max_exact + (np.log(nf / np.float32(max_exact)) / np.float32(math.log(128 / max_exact)) * np.float32(half - max_exact)).astype(np.int32)
    large = np.minimum(large, half - 1)
    return ret + np.where(n < max_exact, n, large)


def make_consts():
    c = {}
    i = np.arange(128)
    c["ident"] = np.eye(128, dtype=np.float32)
    c["antiid"] = np.eye(128, dtype=np.float32)[::-1].copy()
    c["trif"] = (i[:, None] <= i[None, :]).astype(np.float32)
    c["trib"] = (i[:, None] >= i[None, :]).astype(np.float32)
    c["maskf"] = np.where(i[None, :] >= i[:, None], 0.0, NEGBIG).astype(np.float32)
    c["maskb"] = np.where(i[None, :] <= i[:, None], 0.0, NEGBIG).astype(np.float32)
    rel = np.arange(-FV // 2, FV // 2)
    b = _bucket(rel)
    oh = np.zeros((NBUCK, FV), np.float32)
    oh[b, np.arange(FV)] = 1.0
    c["bucket_oh"] = oh
    return np.concatenate([c["ident"], c["antiid"], c["trif"], c["trib"], c["maskf"], c["maskb"]], axis=1), oh


class Cell:
    __slots__ = ("w", "r", "aw")

    def __init__(self):
        self.w = None
        self.r = {}
        self.aw = {}


class Sem:
    __slots__ = ("h", "id")

    def __init__(self, h, i):
        self.h = h
        self.id = i


class Eng:
    def __init__(self, name, eng, sem, is_pe=False):
        self.name = name
        self.eng = eng
        self.sem = sem
        self.count = 0
        self.seen = {}
        self.is_pe = is_pe
        self.pend_r = []
        self.pend_w = []


class Slot:
    def __init__(self, sem):
        self.sem = sem
        self.val = 0


def cells_of(x):
    if isinstance(x, Cell):
        return [x]
    out = []
    for y in x:
        out.extend(cells_of(y))
    return out


class MK:
    def __init__(self, NU=5, UL=2048, TT=1024, debug_stage=None):
        self.NU, self.UL, self.TT = NU, UL, TT
        self.SW = 512
        assert TT % 512 == 0 and UL % TT == 0
        self.TS = TT // 512
        self.NTOK = NU * UL
        self.NT = self.NTOK // TT
        self.NCH = UL // 128
        self.debug_stage = debug_stage
        self.nc = bass.Bass("TRN2", target_bir_lowering=False)
        self.es = ExitStack()
        self.n_inst = 0
        self._uid = 0

    def uid(self, p):
        self._uid += 1
        return f"{p}{self._uid}"

    def sb(self, es, name, shape, dt):
        return es.enter_context(self.nc.sbuf_tensor(self.uid(name), list(shape), dt))

    def dram(self, name, shape, dt, kind="Internal"):
        if name in getattr(self, "debug", ()):
            kind = "ExternalOutput"
        return self.nc.dram_tensor(name, list(shape), dt, kind=kind).ap()

    def setup_engines(self):
        nc = self.nc
        self.sems = []

        def mksem(name):
            h = self.es.enter_context(nc.semaphore(name))
            s = Sem(h, len(self.sems))
            self.sems.append(s)
            return s
        self.E = {
            "pe": Eng("pe", nc.tensor, mksem("s_pe"), is_pe=True),
            "act": Eng("act", nc.scalar, mksem("s_act")),
            "dve": Eng("dve", nc.vector, mksem("s_dve")),
            "pool": Eng("pool", nc.gpsimd, mksem("s_pool")),
            "sp": Eng("sp", nc.sync, mksem("s_sp")),
        }
        NS = 16
        self.slots = {q: [Slot(mksem(f"d_{q}{i}")) for i in range(NS)] for q in ("sp", "pool")}
        self.slot_i = {"sp": 0, "pool": 0}

    def _waits(self, e, reads, writes, is_dma=False, accum=()):
        need = {}

        def add(tok, raw):
            s, v = tok
            if (not is_dma) and s is e.sem:
                if e.is_pe:
                    return
            if need.get(s.id, (None, 0))[1] < v:
                need[s.id] = (s, v)
        for c in reads:
            if c.w is not None:
                add(c.w, True)
            for sid, tok in c.aw.items():
                add(tok, True)
        for c in writes:
            if c.w is not None:
                add(c.w, False)
            for sid, tok in c.aw.items():
                add(tok, False)
            for sid, tok in c.r.items():
                add(tok, False)
        for c in accum:
            if c.w is not None:
                add(c.w, False)
            for sid, tok in c.r.items():
                add(tok, False)
        for sid, (s, v) in need.items():
            if e.seen.get(sid, 0) < v:
                e.eng.wait_ge(s.h, v)
                e.seen[sid] = v
                self.n_inst += 1

    def emit(self, en, make, reads=(), writes=(), inc=True):
        e = self.E[en]
        reads = cells_of(reads)
        writes = cells_of(writes)
        self._waits(e, reads, writes)
        ins = make(e.eng)
        self.n_inst += 1
        if not inc:
            e.pend_r.extend(reads)
            e.pend_w.extend(writes)
            return ins
        e.count += 1
        ins.then_inc(e.sem.h, 1)
        tok = (e.sem, e.count)
        for c in e.pend_r + reads:
            c.r[e.sem.id] = tok
        for c in e.pend_w + writes:
            c.w = tok
            c.r = {}
            c.aw = {}
        e.pend_r = []
        e.pend_w = []
        return ins

    def dma(self, q, out, in_, reads=(), writes=(), accum=(), **kw):
        e = self.E[q]
        reads = cells_of(reads)
        writes = cells_of(writes)
        accum = cells_of(accum)
        i = self.slot_i[q]
        self.slot_i[q] = i + 1
        sl = self.slots[q][i % len(self.slots[q])]
        if sl.val > 0 and e.seen.get(sl.sem.id, 0) < sl.val:
            e.eng.wait_ge(sl.sem.h, sl.val)
            e.seen[sl.sem.id] = sl.val
        self._waits(e, reads, writes, is_dma=True, accum=accum)
        ins = e.eng.dma_start(out=out, in_=in_, **kw)
        self.n_inst += 1
        sl.val += 16
        ins.then_inc(sl.sem.h, 16)
        tok = (sl.sem, sl.val)
        for c in reads:
            c.r[sl.sem.id] = tok
        for c in writes:
            c.w = tok
            c.r = {}
            c.aw = {}
        for c in accum:
            c.aw[sl.sem.id] = tok
        return ins

    def barrier(self):
        toks = []
        for e in self.E.values():
            assert not e.pend_r and not e.pend_w
            if e.count > 0:
                toks.append((e.sem, e.count))
        for q in self.slots:
            for sl in self.slots[q]:
                if sl.val > 0:
                    toks.append((sl.sem, sl.val))
        for e in self.E.values():
            for s, v in toks:
                if s is e.sem:
                    continue
                if e.seen.get(s.id, 0) < v:
                    e.eng.wait_ge(s.h, v)
                    e.seen[s.id] = v
                    self.n_inst += 1

    def V(self, make, r=(), w=(), inc=True):
        return self.emit("dve", make, r, w, inc)

    def A(self, make, r=(), w=(), inc=True):
        return self.emit("act", make, r, w, inc)

    def G(self, make, r=(), w=(), inc=True):
        return self.emit("pool", make, r, w, inc)

    def P(self, make, r=(), w=(), inc=True):
        return self.emit("pe", make, r, w, inc)

    def mm(self, out, lhsT, rhs, start, stop, r=(), w=(), inc=True, skip=False):
        if skip:
            return self.P(lambda e: e.matmul(out, lhsT=lhsT, rhs=rhs, start=start, stop=stop, skip_group_check=True), r, w, inc)
        return self.P(lambda e: e.matmul(out, lhsT=lhsT, rhs=rhs, start=start, stop=stop), r, w, inc)

    def declare(self):
        nc = self.nc
        NTOK = self.NTOK
        ext = lambda n, s: nc.dram_tensor(n, list(s), F32, kind="ExternalInput").ap()
        self.x_in = ext("x", [NTOK, D])
        self.flag_in = ext("flag", [1, 1])
        self.consts_in = ext("consts", [128, 6 * 128])
        self.oh_in = ext("bucket_oh", [NBUCK, FV])
        self.w_in = {
            "norm_pre": ext("norm_pre", [2, 3, D]), "norm_post": ext("norm_post", [2, 3, D]),
            "ffn_w_gate": ext("ffn_w_gate", [2, 2, D, DFF]), "ffn_w_up": ext("ffn_w_up", [2, 2, D, DFF]),
            "ffn_w_down": ext("ffn_w_down", [2, 2, DFF, D]),
            "ssm_w_in": ext("ssm_w_in", [1, D, SSM_IN]), "ssm_conv_w": ext("ssm_conv_w", [1, 5, CONVD]),
            "ssm_conv_b": ext("ssm_conv_b", [1, CONVD]), "ssm_dt_bias": ext("ssm_dt_bias", [1, 2, NHS]),
            "ssm_a_log": ext("ssm_a_log", [1, 2, NHS]), "ssm_d": ext("ssm_d", [1, NHS]),
            "ssm_norm": ext("ssm_norm", [1, DIN]), "ssm_w_out": ext("ssm_w_out", [1, DIN, D]),
            "attn_w_qkv": ext("attn_w_qkv", [1, D, 3072]), "attn_lambda": ext("attn_lambda", [1, 4, 64]),
            "attn_subln": ext("attn_subln", [1, 128]), "attn_w_out": ext("attn_w_out", [1, D, D]),
            "rel_bias": ext("rel_bias", [NBUCK, 8]),
        }
        self.y_out = nc.dram_tensor("y", [NTOK, D], F32, kind="ExternalOutput").ap()
        self.wb = {}
        self.wb_cell = {}
        for li in range(2):
            for fi in range(2):
                for nm, shp in (("gate", [D, DFF]), ("up", [D, DFF]), ("down", [DFF, D])):
                    k = f"{nm}{li}{fi}"
                    self.wb[k] = self.dram("wb_" + k, shp, BF16)
                    self.wb_cell[k] = Cell()
        for k, shp in (("ssm_in", [D, SSM_IN]), ("ssm_out", [DIN, D]), ("qkv", [D, 3072]), ("attn_out", [D, D])):
            self.wb[k] = self.dram("wb_" + k, shp, BF16)
            self.wb_cell[k] = Cell()
        NT, TT = self.NT, self.TT
        self.xres = self.dram("xres", [NT, 128, DC * TT], F32)
        self.xres_c = [Cell() for _ in range(NT)]
        self.xbc = self.dram("xbc", [24, 128, NTOK], BF16)
        self.zsc = self.dram("zsc", [NTOK, DIN], BF16)
        self.dtr = self.dram("dtr", [NTOK, 64], F32)
        self.ssm_c = [Cell() for _ in range(NT)]
        self.yT = self.dram("yT", [16, 128, NTOK], BF16)
        self.yT_c = [Cell() for _ in range(self.NU)]
        self.qT = self.dram("qT", [8, 128, NTOK], BF16)
        self.kT = self.dram("kT", [8, 128, NTOK], BF16)
        self.vtm = self.dram("vtm", [NTOK, D], BF16)
        self.qkv_c = [Cell() for _ in range(NT)]
        self.oT = self.dram("oT", [8, 128, NTOK], BF16)
        self.oT_c = [Cell() for _ in range(self.NU)]
        self.fvec = self.dram("fvec", [8, FV], F32)
        self.fvec_c = Cell()
        self.dbg_out = None

    def setup_consts(self):
        es = self.es
        nc = self.nc
        sb = lambda n, s, d: self.sb(es, n, s, d)
        self.cF = sb("cF", [128, 6 * 128], F32)
        self.cF_c = Cell()
        self.cB = sb("cB", [128, 6 * 128], BF16)
        self.cB_c = Cell()
        self.onesB = sb("onesB", [128, 128], BF16)
        self.onesB_c = Cell()
        self.dma("sp", self.cF[:], self.consts_in[:, :], writes=[self.cF_c])
        self.V(lambda e: e.tensor_copy(out=self.cB[:], in_=self.cF[:]), [self.cF_c], [self.cB_c])
        self.V(lambda e: e.memset(self.onesB[:], 1.0), [], [self.onesB_c])
        self.identF = self.cF[:, 0:128]
        self.antiF = self.cF[:, 128:256]
        self.identB = self.cB[:, 0:128]
        self.trifB = self.cB[:, 256:384]
        self.tribB = self.cB[:, 384:512]
        self.maskfB = self.cB[:, 512:640]
        self.maskbB = self.cB[:, 640:768]
        self.gpre = sb("gpre", [128, 6, DC], F32)
        self.gpost = sb("gpost", [128, 6, DC], F32)
        self.gposth = sb("gposth", [128, 6, DC], F32)
        self.g_c = Cell()
        self.dma("sp", self.gpre[:], self.w_in["norm_pre"].rearrange("l j (c p) -> p (l j) c", p=128),
                 writes=[self.g_c], allow_slow_non_contiguous=True)
        self.dma("sp", self.gpost[:], self.w_in["norm_post"].rearrange("l j (c p) -> p (l j) c", p=128),
                 writes=[self.g_c], allow_slow_non_contiguous=True)
        self.V(lambda e: e.tensor_scalar(out=self.gposth[:], in0=self.gpost[:], scalar1=0.5, scalar2=None, op0=ALU.mult),
               [self.g_c], [self.g_c])
        self.flagc = sb("flagc", [128, 1], F32)
        self.maskc = sb("maskc", [128, 1], F32)
        self.flag_c = Cell()
        self.dma("sp", self.flagc[:], self.flag_in.partition_broadcast(128), writes=[self.flag_c])
        self.V(lambda e: e.tensor_scalar(out=self.maskc[:], in0=self.flagc[:], scalar1=-NEGBIG, scalar2=NEGBIG,
                                         op0=ALU.mult, op1=ALU.add), [self.flag_c], [self.flag_c])
        self.epsc = sb("epsc", [128, 1], F32)
        self.eps_c = Cell()
        self.V(lambda e: e.memset(self.epsc[:], EPS), [], [self.eps_c])

    def cast_weights(self):
        def cast(dst, src, cell, rows, cols, nsplit):
            rs = rows // nsplit
            for i in range(nsplit):
                self.dma("pool", dst[i * rs:(i + 1) * rs, :], src[i * rs:(i + 1) * rs, :], accum=[cell])
        order = []
        for li in range(2):
            for fi in range(2):
                order.append((li, fi))
        w = self.w_in
        def ffn(li, fi):
            cast(self.wb[f"gate{li}{fi}"], w["ffn_w_gate"][li, fi], self.wb_cell[f"gate{li}{fi}"], D, DFF, 4)
            cast(self.wb[f"up{li}{fi}"], w["ffn_w_up"][li, fi], self.wb_cell[f"up{li}{fi}"], D, DFF, 4)
            cast(self.wb[f"down{li}{fi}"], w["ffn_w_down"][li, fi], self.wb_cell[f"down{li}{fi}"], DFF, D, 4)
        ffn(0, 0)
        cast(self.wb["ssm_in"], w["ssm_w_in"][0], self.wb_cell["ssm_in"], D, SSM_IN, 8)
        cast(self.wb["ssm_out"], w["ssm_w_out"][0], self.wb_cell["ssm_out"], DIN, D, 2)
        ffn(0, 1)
        ffn(1, 0)
        cast(self.wb["qkv"], w["attn_w_qkv"][0], self.wb_cell["qkv"], D, 3072, 4)
        cast(self.wb["attn_out"], w["attn_w_out"][0], self.wb_cell["attn_out"], D, D, 1)
        ffn(1, 1)


class Bank:
    def __init__(self, ap, cell):
        self.ap = ap
        self.cell = cell


class Pair:
    def __init__(self, ap, c0, c1):
        self.ap = ap
        self.c0 = c0
        self.c1 = c1


def _grid(*dims):
    if len(dims) == 1:
        return [Cell() for _ in range(dims[0])]
    return [_grid(*dims[1:]) for _ in range(dims[0])]


class MK2(MK):
    def setup_psum(self):
        nc = self.nc
        es = self.es
        self.PA = es.enter_context(nc.psum_tensor("PA", [128, 1024], F32))
        self.PB = es.enter_context(nc.psum_tensor("PB", [128, 1024], F32))
        self.PC = es.enter_context(nc.psum_tensor("PC", [128, 1024], F32))
        self.PD0 = es.enter_context(nc.psum_tensor("PD0", [128, 512], F32))
        self.PT = es.enter_context(nc.psum_tensor("PT", [128, 1024], BF16))
        self.pairs = []
        self.banks = []
        for t in (self.PA, self.PB, self.PC):
            c0, c1 = Cell(), Cell()
            self.pairs.append(Pair(t, c0, c1))
            self.banks.append(Bank(t[:, 0:512], c0))
            self.banks.append(Bank(t[:, 512:1024], c1))
        self.bankD = Bank(self.PD0[:, :], Cell())
        self.banks.append(self.bankD)
        self.PT_c = Cell()
        self.rotc = {}

    def rot(self, name, n):
        i = self.rotc.get(name, 0)
        self.rotc[name] = i + 1
        return i % n

    def next_bank(self):
        return self.banks[self.rot("bank", len(self.banks))]

    def next_pair(self):
        p = self.pairs[self.rot("pair", 3)]
        return p

    def tl_alloc(self, es):
        TT, TS = self.TT, self.TS
        sb = lambda n, s, d: self.sb(es, n, s, d)
        self.xT = sb("xT", [128, DC, TT], F32)
        self.xT_c = _grid(DC, TS)
        self.uT = sb("uT", [128, DC, TT], BF16)
        self.uT_c = _grid(DC, TS)
        self.hT = sb("hT", [128, FC, TT], BF16)
        self.hT_c = _grid(FC, TS)
        self.hout = sb("hout", [128, DC, TT], F32)
        self.hout_c = _grid(DC, TS)
        self.sq = sb("sq", [128, DC, 512], BF16)
        self.sq_c = _grid(DC)
        self.NW = 3
        self.wpool = [sb(f"wp{i}", [128, 5632], BF16) for i in range(self.NW)]
        self.wpool_c = _grid(self.NW)
        self.sg = sb("sg", [128, 3, 512], F32)
        self.sg_c = _grid(3)
        self.lnv = sb("lnv", [128, 512], F32)
        self.lnv_c = Cell()
        self.rstd = sb("rstd", [128, 2, 512], F32)
        self.rstd_c = _grid(2)
        self.tmpn = sb("tmpn", [128, 3, 512], F32)
        self.tmpn_c = _grid(3)
        self.xstage = sb("xstage", [128, 2, D], F32)
        self.xstage_c = _grid(2)
        self.zst = sb("zst", [128, 3, 512], BF16)
        self.zst_c = _grid(3)
        self.dst = sb("dst", [128, 2, 64], F32)
        self.dst_c = _grid(2)

    def tsl(self, ts):
        return slice(ts * 512, (ts + 1) * 512)

    def wload(self, W, wcell, KC, col0, pw):
        i = self.rot("w", self.NW)
        buf = self.wpool[i]
        view = buf[:, 0:KC * pw].rearrange("p (k n) -> p k n", k=KC)
        src = W.rearrange("(k p) n -> p k n", p=128)[:, :, col0:col0 + pw]
        self.dma("sp", view, src, reads=[wcell], writes=[self.wpool_c[i]])
        return view, self.wpool_c[i]

    def rms_stats(self, srcs, nfeat):
        C = len(srcs)
        for c, (ap, cl) in enumerate(srcs):
            self.A(lambda e, ap=ap, c=c: e.activation(out=self.sq[:, c, :], in_=ap, func=AF.Square), cl, [self.sq_c[c]])
        bank = self.next_bank()
        for c in range(C):
            self.mm(bank.ap, self.onesB[:], self.sq[:, c, :], c == 0, c == C - 1,
                    [self.onesB_c, self.sq_c[c]], [bank.cell], inc=(c == C - 1))
        j = self.rot("rs", 2)
        self.A(lambda e: e.activation(out=self.lnv[:], in_=bank.ap, func=AF.Ln, scale=1.0 / nfeat, bias=self.epsc[:]),
               [bank.cell, self.eps_c], [self.lnv_c])
        self.A(lambda e: e.activation(out=self.rstd[:, j, :], in_=self.lnv[:], func=AF.Exp, scale=-0.5),
               [self.lnv_c], [self.rstd_c[j]])
        return self.rstd[:, j, :], self.rstd_c[j]

    def prenorm(self, n):
        for ts in range(self.TS):
            sl = self.tsl(ts)
            srcs = [(self.xT[:, c, sl], [self.xT_c[c][ts]]) for c in range(DC)]
            rs, rc = self.rms_stats(srcs, D)
            for c in range(DC):
                self.V(lambda e, c=c: e.scalar_tensor_tensor(out=self.uT[:, c, sl], in0=self.xT[:, c, sl],
                                                             scalar=self.gpre[:, n, c:c + 1], in1=rs,
                                                             op0=ALU.mult, op1=ALU.mult),
                       [self.xT_c[c][ts], self.g_c, rc], [self.uT_c[c][ts]])

    def post_res(self, n, half):
        g = self.gposth if half else self.gpost
        for ts in range(self.TS):
            sl = self.tsl(ts)
            srcs = [(self.hout[:, c, sl], [self.hout_c[c][ts]]) for c in range(DC)]
            rs, rc = self.rms_stats(srcs, D)
            for c in range(DC):
                j = self.rot("tmpn", 3)
                self.V(lambda e, c=c, j=j: e.scalar_tensor_tensor(out=self.tmpn[:, j, :], in0=self.hout[:, c, sl],
                                                                  scalar=g[:, n, c:c + 1], in1=rs,
                                                                  op0=ALU.mult, op1=ALU.mult),
                       [self.hout_c[c][ts], self.g_c, rc], [self.tmpn_c[j]])
                (self.G if c % 2 == 0 else self.V)(lambda e, c=c, j=j: e.tensor_tensor(out=self.xT[:, c, sl], in0=self.xT[:, c, sl],
                                                                                  in1=self.tmpn[:, j, :], op=ALU.add),
                                                   [self.xT_c[c][ts], self.tmpn_c[j]], [self.xT_c[c][ts]])

    def linear_fm(self, src, src_c, KC, W, wcell, col0, ncols, consumer):
        PW = 512 if KC <= 8 else 256
        for pc0 in range(0, ncols, PW):
            pw = min(PW, ncols - pc0)
            wbuf, wc = self.wload(W, wcell, KC, col0 + pc0, pw)
            for ts in range(self.TS):
                for ml in range(pw // 128):
                    m = pc0 // 128 + ml
                    bank = self.next_bank()
                    for kc in range(KC):
                        self.mm(bank.ap, wbuf[:, kc, ml * 128:(ml + 1) * 128], src[:, kc, self.tsl(ts)],
                                kc == 0, kc == KC - 1, [wc, src_c[kc][ts]], [bank.cell], inc=(kc == KC - 1))
                    consumer(m, ts, bank)

    def linear_tm(self, src, src_c, KC, W, wcell, col0, ncols, consumer):
        for pc0 in range(0, ncols, 512):
            pw = min(512, ncols - pc0)
            wbuf, wc = self.wload(W, wcell, KC, col0 + pc0, pw)
            for tb in range(self.TT // 128):
                ts = (tb * 128) // 512
                bank = self.next_bank()
                for kc in range(KC):
                    self.mm(bank.ap[:, 0:pw], src[:, kc, tb * 128:(tb + 1) * 128], wbuf[:, kc, :],
                            kc == 0, kc == KC - 1, [wc, src_c[kc][ts]], [bank.cell], inc=(kc == KC - 1))
                consumer(tb, pc0, pw, bank)

    def ffn(self, li, fi):
        TS = self.TS
        n = li * 3 + (0 if fi == 0 else 2)
        self.prenorm(n)
        Wg, Wu, Wd = self.wb[f"gate{li}{fi}"], self.wb[f"up{li}{fi}"], self.wb[f"down{li}{fi}"]
        cg, cu, cd = self.wb_cell[f"gate{li}{fi}"], self.wb_cell[f"up{li}{fi}"], self.wb_cell[f"down{li}{fi}"]
        for p0 in range(0, FC, 4):
            nf = min(4, FC - p0)
            pw = nf * 128
            gbuf, gc = self.wload(Wg, cg, DC, p0 * 128, pw)
            ubuf, uc = self.wload(Wu, cu, DC, p0 * 128, pw)
            for ts in range(TS):
                sl = self.tsl(ts)
                for fl in range(nf):
                    f = p0 + fl
                    pr = self.next_pair()
                    for kc in range(DC):
                        self.mm(pr.ap[:, 0:512], gbuf[:, kc, fl * 128:(fl + 1) * 128], self.uT[:, kc, sl],
                                kc == 0, kc == DC - 1, [gc, self.uT_c[kc][ts]], [pr.c0], inc=(kc == DC - 1))
                    for kc in range(DC):
                        self.mm(pr.ap[:, 512:1024], ubuf[:, kc, fl * 128:(fl + 1) * 128], self.uT[:, kc, sl],
                                kc == 0, kc == DC - 1, [uc, self.uT_c[kc][ts]], [pr.c1], inc=(kc == DC - 1))
                    j = self.rot("sg", 3)
                    self.A(lambda e, j=j: e.activation(out=self.sg[:, j, :], in_=pr.ap[:, 0:512], func=AF.Silu),
                           [pr.c0], [self.sg_c[j]])
                    self.V(lambda e, j=j, f=f: e.tensor_tensor(out=self.hT[:, f, sl], in0=pr.ap[:, 512:1024],
                                                               in1=self.sg[:, j, :], op=ALU.mult),
                           [pr.c1, self.sg_c[j]], [self.hT_c[f][ts]])

        def cons(m, ts, bank):
            self.V(lambda e: e.tensor_copy(out=self.hout[:, m, self.tsl(ts)], in_=bank.ap),
                   [bank.cell], [self.hout_c[m][ts]])
        self.linear_fm(self.hT, self.hT_c, FC, Wd, cd, 0, D, cons)
        self.post_res(n, half=True)

    def load_x_tile(self, t):
        TT = self.TT
        for tb in range(TT // 128):
            ts = (tb * 128) // 512
            k = self.rot("xs", 2)
            r0 = t * TT + tb * 128
            self.dma("sp", self.xstage[:, k, :], self.x_in[r0:r0 + 128, :], writes=[self.xstage_c[k]])
            for hb in range(2):
                bank = self.next_bank()
                for cl in range(4):
                    c = hb * 4 + cl
                    self.P(lambda e, c=c, cl=cl: e.transpose(bank.ap[:, cl * 128:(cl + 1) * 128],
                                                              self.xstage[:, k, c * 128:(c + 1) * 128], self.identF),
                           [self.xstage_c[k], self.cF_c], [bank.cell], inc=(cl == 3))
                outv = self.xT[:, hb * 4:(hb + 1) * 4, tb * 128:(tb + 1) * 128]
                inv = bank.ap.rearrange("p (c n) -> p c n", c=4)
                wc = [self.xT_c[hb * 4 + cl][ts] for cl in range(4)]
                if hb == 0:
                    self.A(lambda e: e.activation(out=outv, in_=inv, func=AF.Copy), [bank.cell], wc)
                else:
                    self.V(lambda e: e.tensor_copy(out=outv, in_=inv), [bank.cell], wc)

    def store_y_tile(self, t):
        TT = self.TT
        for tb in range(TT // 128):
            ts = (tb * 128) // 512
            k = self.rot("xs", 2)
            for hb in range(2):
                bank = self.next_bank()
                for cl in range(4):
                    c = hb * 4 + cl
                    self.P(lambda e, c=c, cl=cl: e.transpose(bank.ap[:, cl * 128:(cl + 1) * 128],
                                                              self.xT[:, c, tb * 128:(tb + 1) * 128], self.identF),
                           [self.xT_c[c][ts], self.cF_c], [bank.cell], inc=(cl == 3))
                if hb == 0:
                    self.A(lambda e: e.activation(out=self.xstage[:, k, 0:512], in_=bank.ap, func=AF.Copy),
                           [bank.cell], [self.xstage_c[k]])
                else:
                    self.V(lambda e: e.tensor_copy(out=self.xstage[:, k, 512:1024], in_=bank.ap),
                           [bank.cell], [self.xstage_c[k]])
            r0 = t * TT + tb * 128
            self.dma("pool", self.y_out[r0:r0 + 128, :], self.xstage[:, k, :], reads=[self.xstage_c[k]])

    def all_xT_cells(self):
        return cells_of(self.xT_c)

    def store_xres(self, t):
        self.dma("pool", self.xres[t].rearrange("p (c n) -> p c n", c=DC), self.xT[:],
                 reads=self.all_xT_cells(), writes=[self.xres_c[t]])

    def load_xres(self, t):
        self.dma("sp", self.xT[:], self.xres[t].rearrange("p (c n) -> p c n", c=DC),
                 reads=[self.xres_c[t]], writes=self.all_xT_cells())

    def stage_A(self, es):
        TT = self.TT
        for t in range(self.NT):
            tok0 = t * TT
            self.load_x_tile(t)
            self.ffn(0, 0)
            self.prenorm(1)
            W, wc = self.wb["ssm_in"], self.wb_cell["ssm_in"]

            def cons_xbc(m, ts, bank):
                j = self.rot("zst", 3)
                self.V(lambda e: e.tensor_copy(out=self.zst[:, j, :], in_=bank.ap), [bank.cell], [self.zst_c[j]])
                a = tok0 + ts * 512
                self.dma("pool", self.xbc[m][:, a:a + 512], self.zst[:, j, :], reads=[self.zst_c[j]], accum=[self.ssm_c[t]])
            self.linear_fm(self.uT, self.uT_c, DC, W, wc, DIN, CONVD, cons_xbc)

            def cons_z(tb, pc0, pw, bank):
                j = self.rot("zst", 3)
                self.A(lambda e: e.activation(out=self.zst[:, j, 0:pw], in_=bank.ap[:, 0:pw], func=AF.Silu), [bank.cell], [self.zst_c[j]])
                r0 = tok0 + tb * 128
                self.dma("pool", self.zsc[r0:r0 + 128, pc0:pc0 + pw], self.zst[:, j, 0:pw], reads=[self.zst_c[j]], accum=[self.ssm_c[t]])
            self.linear_tm(self.uT, self.uT_c, DC, W, wc, 0, DIN, cons_z)

            def cons_dt(tb, pc0, pw, bank):
                j = self.rot("dst", 2)
                self.V(lambda e: e.tensor_copy(out=self.dst[:, j, :], in_=bank.ap[:, 0:64]), [bank.cell], [self.dst_c[j]])
                r0 = tok0 + tb * 128
                self.dma("pool", self.dtr[r0:r0 + 128, :], self.dst[:, j, :], reads=[self.dst_c[j]], accum=[self.ssm_c[t]])
            self.linear_tm(self.uT, self.uT_c, DC, W, wc, DIN + CONVD, 64, cons_dt)
            self.store_xres(t)

    def mixer_out(self, t, src_dram, src_cell, KC, wkey, n):
        TT = self.TT
        tok0 = t * TT
        u = tok0 // self.UL
        view = self.hT[:, 0:KC, :]
        self.dma("sp", view, src_dram.rearrange("c p n -> p c n")[:, :, tok0:tok0 + TT],
                 reads=[src_cell[u]], writes=[self.hT_c[c] for c in range(KC)])

        def cons(m, ts, bank):
            self.V(lambda e: e.tensor_copy(out=self.hout[:, m, self.tsl(ts)], in_=bank.ap),
                   [bank.cell], [self.hout_c[m][ts]])
        self.linear_fm(self.hT, self.hT_c, KC, self.wb[wkey], self.wb_cell[wkey], 0, D, cons)
        self.post_res(n, half=False)

    def stage_B(self, es):
        TT = self.TT
        for t in range(self.NT):
            tok0 = t * TT
            self.load_xres(t)
            self.mixer_out(t, self.yT, self.yT_c, 16, "ssm_out", 1)
            self.ffn(0, 1)
            self.ffn(1, 0)
            self.prenorm(4)
            W, wc = self.wb["qkv"], self.wb_cell["qkv"]

            def cons_qk(m, ts, bank):
                j = self.rot("zst", 3)
                self.A(lambda e: e.activation(out=self.zst[:, j, :], in_=bank.ap, func=AF.Copy), [bank.cell], [self.zst_c[j]])
                a = tok0 + ts * 512
                dstT = self.qT[m] if m < 8 else self.kT[m - 8]
                self.dma("pool", dstT[:, a:a + 512], self.zst[:, j, :], reads=[self.zst_c[j]], accum=[self.qkv_c[t]])
            self.linear_fm(self.uT, self.uT_c, DC, W, wc, 0, 2048, cons_qk)

            def cons_v(tb, pc0, pw, bank):
                j = self.rot("zst", 3)
                self.V(lambda e: e.tensor_copy(out=self.zst[:, j, 0:pw], in_=bank.ap[:, 0:pw]), [bank.cell], [self.zst_c[j]])
                r0 = tok0 + tb * 128
                self.dma("pool", self.vtm[r0:r0 + 128, pc0:pc0 + pw], self.zst[:, j, 0:pw], reads=[self.zst_c[j]], accum=[self.qkv_c[t]])
            self.linear_tm(self.uT, self.uT_c, DC, W, wc, 2048, 1024, cons_v)
            self.store_xres(t)

    def stage_C(self, es):
        for t in range(self.NT):
            self.load_xres(t)
            self.mixer_out(t, self.oT, self.oT_c, 8, "attn_out", 4)
            self.ffn(1, 1)
            self.store_y_tile(t)


class MK3(MK2):
    def seq_groups(self):
        sgs = [[0, 1]] if self.NU >= 2 else [[0]]
        sgs += [[u] for u in range(2, self.NU)]
        return sgs

    def stage_ssd(self, es):
        UL, NCH = self.UL, self.NCH
        NCm = 2 * NCH
        CSEG = min(1024, UL)
        NBS = CSEG // 128
        sb = lambda n, s, d: self.sb(es, n, s, d)
        w = self.w_in
        a_bc = sb("a_bc", [128, 64], F32)
        dtb_bc = sb("dtb_bc", [128, 64], F32)
        D_bc = sb("D_bc", [128, 32], F32)
        convw = sb("convw", [128, 5, 24], F32)
        convb = sb("convb", [128, 24], F32)
        ng_bc = sb("ng_bc", [128, 512], F32)
        ng_c = Cell()
        mask4 = sb("mask4", [128, 2, 512], BF16)
        diagW = sb("diagW", [128, 6, 5, 128], BF16)
        sc = Cell()
        dg_c = Cell()
        self.dma("sp", a_bc[:], w["ssm_a_log"][0].rearrange("d h -> (d h)").partition_broadcast(128), writes=[sc])
        self.dma("sp", dtb_bc[:], w["ssm_dt_bias"][0].rearrange("d h -> (d h)").partition_broadcast(128), writes=[sc])
        self.dma("sp", D_bc[:], w["ssm_d"][0].partition_broadcast(128), writes=[sc])
        for k5 in range(5):
            self.dma("sp", convw[:, k5, :], w["ssm_conv_w"][0][k5].rearrange("(c p) -> p c", p=128), writes=[sc], allow_slow_non_contiguous=True)
        self.dma("sp", convb[:], w["ssm_conv_b"][0].rearrange("(c p) -> p c", p=128), writes=[sc], allow_slow_non_contiguous=True)
        self.A(lambda e: e.activation(out=a_bc[:], in_=a_bc[:], func=AF.Exp), [sc], [sc])
        self.V(lambda e: e.tensor_scalar(out=a_bc[:], in0=a_bc[:], scalar1=-1.0, scalar2=None, op0=ALU.mult), [sc], [sc])
        for d, mk_ in enumerate((self.maskfB, self.maskbB)):
            self.V(lambda e, d=d, mk_=mk_: e.tensor_copy(out=mask4[:, d, :].rearrange("p (h i) -> p h i", h=4),
                                                       in_=mk_.unsqueeze(1).broadcast_to([128, 4, 128])), [self.cB_c, sc], [sc])
        xg = sb("xg", [128, NCm, 512], BF16); xg_c = _grid(NCm)
        Bg = sb("Bg", [128, NCm, 128], BF16); Bg_c = _grid(NCm)
        BT = sb("BT", [128, NCm * 128], BF16); BT_c = _grid(NCm)
        CT = sb("CT", [128, NCm * 128], BF16); CT_c = _grid(NCm)
        hpb = sb("hpb", [128, NCm, 512], BF16); hpb_c = _grid(NCm)
        mkdt = lambda n, d_: sb(n, [128, 2, NCm, 8], d_)
        dtraw = mkdt("dtraw", F32); tA = mkdt("tA", F32); tB = dtraw
        dtv = mkdt("dtv", F32); dav = mkdt("dav", F32); csv = mkdt("csv", F32); csl = mkdt("csl", F32)
        Ef = mkdt("Ef", F32); Dec = mkdt("Dec", F32); CDt = mkdt("CDt", F32)
        da16 = mkdt("da16", BF16); nda16 = mkdt("nda16", BF16)
        dts_c = Cell()
        xin = sb("xin", [128, 2, CSEG + 4], BF16); xin_c = _grid(2)
        cvo = sb("cvo", [128, 2, CSEG], BF16); cvo_c = _grid(2)
        cbT = sb("cbT", [128, 3, 128], BF16); cbT_c = _grid(3)
        ex = sb("ex", [128, 8, 512], BF16); ex_c = _grid(8)
        WT = sb("WT", [128, 8, 512], BF16); WT_c = _grid(8)
        xdt = sb("xdt", [128, 3, 2, 512], BF16); xdt_c = _grid(3)
        xdec = sb("xdec", [128, 2, 512], BF16); xdec_c = _grid(2)
        xdecm = sb("xdecm", [128, 3, 512], BF16); xdecm_c = _grid(3)
        Hf = sb("Hf", [128, 512], F32); Hf_c = Cell()
        Hb = sb("Hb", [128, 512], F32); Hb_c = Cell()
        Hf16 = sb("Hf16", [128, 512], BF16); Hf16_c = Cell()
        tH = sb("tH", [128, 2, 512], F32); tH_c = _grid(2)
        t1 = sb("t1", [128, 2, 512], F32); t1_c = _grid(2)
        t2 = sb("t2", [128, 2, 512], F32); t2_c = _grid(2)
        t3 = sb("t3", [128, 5, 512], F32); t3_c = _grid(5)
        ysb = sb("ysb", [128, 3, 512], F32); ysb_c = _grid(3)
        zs = sb("zs", [128, 3, 512], BF16); zs_c = _grid(3)
        yz = sb("yz", [128, 2, 512], F32); yz_c = _grid(2)
        ssq = sb("ssq", [128, 3, 2], F32); ssq_c = _grid(3)
        yn = sb("yn", [128, 2, 512], BF16); yn_c = _grid(2)
        yst = sb("yst", [128, 2, 512], BF16); yst_c = _grid(2)
        _ptc = Cell()
        PTh = [_ptc, _ptc]

        def bc8(ap8):
            return ap8.unsqueeze(2).broadcast_to([128, 8, 64])

        def v3(ap512):
            return ap512.rearrange("p (h d) -> p h d", h=8)

        TTn = self.TT
        for g in range(NG):
            chs = [4 * g, 4 * g + 1, 4 * g + 2, 4 * g + 3, 16 + g, 20 + g]
            self.dma("sp", ng_bc[:], w["ssm_norm"][0][g * 512:(g + 1) * 512].partition_broadcast(128), writes=[ng_c])
            for cl, ch in enumerate(chs):
                for k5 in range(5):
                    self.G(lambda e, cl=cl, ch=ch, k5=k5: e.tensor_scalar(out=diagW[:, cl, k5, :], in0=self.identF, scalar1=convw[:, k5, ch:ch + 1],
                                                                          scalar2=None, op0=ALU.mult), [self.cF_c, sc], [dg_c])
            for sg in self.seq_groups():
                NC = len(sg) * NCH
                tok0 = sg[0] * UL
                pair = len(sg) == 2
                sgcells = [self.ssm_c[(tok0 // TTn) + i] for i in range(NC * 128 // TTn)]
                for d in range(2):
                    src = self.dtr[tok0:tok0 + NC * 128, d * 32 + 8 * g:d * 32 + 8 * g + 8].rearrange("(c p) h -> p c h", p=128)
                    self.dma("sp", dtraw[:, d, 0:NC, :], src, reads=sgcells, writes=[dts_c])
                S_ = lambda t_: t_[:, :, 0:NC, :]
                bsl = dtb_bc[:].rearrange("p (d h) -> p d h", d=2)[:, :, 8 * g:8 * g + 8].unsqueeze(2).broadcast_to([128, 2, NC, 8])
                asl = a_bc[:].rearrange("p (d h) -> p d h", d=2)[:, :, 8 * g:8 * g + 8].unsqueeze(2).broadcast_to([128, 2, NC, 8])
                self.V(lambda e: e.tensor_tensor(out=S_(tA), in0=S_(dtraw), in1=bsl, op=ALU.add), [dts_c, sc], [dts_c])
                self.A(lambda e: e.activation(out=S_(tB), in_=S_(tA), func=AF.Abs), [dts_c], [dts_c])
                self.A(lambda e: e.activation(out=S_(tB), in_=S_(tB), func=AF.Exp, scale=-1.0), [dts_c], [dts_c])
                self.A(lambda e: e.activation(out=S_(tB), in_=S_(tB), func=AF.Ln, bias=1.0), [dts_c], [dts_c])
                self.V(lambda e: e.scalar_tensor_tensor(out=S_(dtv), in0=S_(tA), scalar=0.0, in1=S_(tB), op0=ALU.max, op1=ALU.add), [dts_c], [dts_c])
                self.V(lambda e: e.tensor_tensor(out=S_(dav), in0=S_(dtv), in1=asl, op=ALU.mult), [dts_c, sc], [dts_c])
                self.V(lambda e: e.tensor_copy(out=S_(da16), in_=S_(dav)), [dts_c], [dts_c])
                self.V(lambda e: e.tensor_scalar(out=S_(nda16), in0=S_(dav), scalar1=-1.0, scalar2=None, op0=ALU.mult), [dts_c], [dts_c])
                bk1 = self.next_bank()
                bk2 = self.next_bank()
                for c in range(NC):
                    for d in range(2):
                        tri = self.trifB if d == 0 else self.tribB
                        o = (d * NC + c) * 8
                        self.mm(bk1.ap[:, o:o + 8], tri, da16[:, d, c, :], True, True, [self.cB_c, dts_c], [bk1.cell], inc=False)
                        self.mm(bk2.ap[:, o:o + 8], self.onesB[:], da16[:, d, c, :], True, True, [self.onesB_c, dts_c], [bk1.cell, bk2.cell], inc=(c == NC - 1 and d == 1))
                vb = lambda bk: bk.ap[:, 0:2 * NC * 8].rearrange("p (d c h) -> p d c h", d=2, c=NC)
                self.V(lambda e: e.tensor_copy(out=S_(csv), in_=vb(bk1)), [bk1.cell], [dts_c])
                self.V(lambda e: e.tensor_copy(out=S_(csl), in_=vb(bk2)), [bk2.cell], [dts_c])
                self.A(lambda e: e.activation(out=S_(Ef), in_=S_(csv), func=AF.Exp), [dts_c], [dts_c])
                self.A(lambda e: e.activation(out=S_(CDt), in_=S_(csl), func=AF.Exp), [dts_c], [dts_c])
                self.V(lambda e: e.tensor_tensor(out=S_(tA), in0=S_(csl), in1=S_(csv), op=ALU.subtract), [dts_c], [dts_c])
                self.A(lambda e: e.activation(out=S_(tA), in_=S_(tA), func=AF.Exp), [dts_c], [dts_c])
                self.V(lambda e: e.tensor_tensor(out=S_(Dec), in0=S_(tA), in1=S_(dtv), op=ALU.mult), [dts_c], [dts_c])
                for ui, u in enumerate(sg):
                    utok = u * UL
                    co = ui * NCH
                    nseg = UL // CSEG
                    for seg in range(nseg):
                        s0 = utok + seg * CSEG
                        cb0 = co + seg * NBS
                        lh = "real" if seg > 0 else ("flag" if (pair and ui == 1) else "zero")
                        rh = "real" if seg < nseg - 1 else ("flag" if (pair and ui == 0) else "zero")
                        for cl, ch in enumerate(chs):
                            k = self.rot("xin", 2)
                            a0 = s0 - (0 if lh == "zero" else 2)
                            a1 = s0 + CSEG + (0 if rh == "zero" else 2)
                            o0 = 0 if lh != "zero" else 2
                            self.dma("sp", xin[:, k, o0:o0 + (a1 - a0)], self.xbc[ch][:, a0:a1], reads=sgcells, writes=[xin_c[k]])
                            if lh == "zero":
                                self.G(lambda e: e.memset(xin[:, k, 0:2], 0.0), [], [xin_c[k]])
                            elif lh == "flag":
                                self.G(lambda e: e.tensor_scalar(out=xin[:, k, 0:2], in0=xin[:, k, 0:2], scalar1=self.flagc[:, 0:1], scalar2=None, op0=ALU.mult),
                                       [xin_c[k], self.flag_c], [xin_c[k]])
                            if rh == "zero":
                                self.G(lambda e: e.memset(xin[:, k, CSEG + 2:CSEG + 4], 0.0), [], [xin_c[k]])
                            elif rh == "flag":
                                self.G(lambda e: e.tensor_scalar(out=xin[:, k, CSEG + 2:CSEG + 4], in0=xin[:, k, CSEG + 2:CSEG + 4], scalar1=self.flagc[:, 0:1],
                                                                 scalar2=None, op0=ALU.mult), [xin_c[k], self.flag_c], [xin_c[k]])
                            if ch >= 16:
                                dstT, dst_cells = (BT, BT_c) if ch < 20 else (CT, CT_c)
                            else:
                                kk = self.rot("cvo", 2)
                            for b5 in range(CSEG // 512):
                                bk = self.next_bank()
                                for k5 in range(5):
                                    self.mm(bk.ap, diagW[:, cl, k5, :], xin[:, k, b5 * 512 + k5:b5 * 512 + k5 + 512], k5 == 0, k5 == 4,
                                            [dg_c, xin_c[k]], [bk.cell], inc=(k5 == 4))
                                if ch >= 16:
                                    c4 = cb0 + b5 * 4
                                    self.A(lambda e: e.activation(out=dstT[:, c4 * 128:c4 * 128 + 512], in_=bk.ap, func=AF.Silu, bias=convb[:, ch:ch + 1]),
                                           [bk.cell, sc], dst_cells[c4:c4 + 4])
                                else:
                                    self.A(lambda e: e.activation(out=cvo[:, kk, b5 * 512:(b5 + 1) * 512], in_=bk.ap, func=AF.Silu, bias=convb[:, ch:ch + 1]),
                                           [bk.cell, sc], [cvo_c[kk]])
                            if 16 <= ch < 20:
                                for b in range(NBS):
                                    self.P(lambda e, b=b: e.transpose(self.PT[:, b * 128:(b + 1) * 128], BT[:, (cb0 + b) * 128:(cb0 + b + 1) * 128], self.identB),
                                           [BT_c[cb0 + b], self.cB_c], PTh, inc=(b == NBS - 1))
                                self.V(lambda e: e.tensor_copy(out=Bg[:, cb0:cb0 + NBS, :], in_=self.PT[:, 0:NBS * 128].rearrange("p (b n) -> p b n", b=NBS)),
                                       PTh, Bg_c[cb0:cb0 + NBS])
                            elif ch < 16:
                                xi = ch - 4 * g
                                for b in range(NBS):
                                    self.P(lambda e, b=b: e.transpose(self.PT[:, b * 128:(b + 1) * 128], cvo[:, kk, b * 128:(b + 1) * 128], self.identB),
                                           [cvo_c[kk], self.cB_c], PTh, inc=(b == NBS - 1))
                                self.V(lambda e: e.tensor_copy(out=xg[:, cb0:cb0 + NBS, xi * 128:(xi + 1) * 128], in_=self.PT[:, 0:NBS * 128].rearrange("p (b n) -> p b n", b=NBS)),
                                       PTh, xg_c[cb0:cb0 + NBS])
                self.V(lambda e: e.memset(Hb[:], 0.0), [], [Hb_c])

                def pre_states(c):
                    k = self.rot("xdec", 2)
                    self.G(lambda e: e.tensor_tensor(out=v3(xdec[:, k, :]), in0=v3(xg[:, c, :]), in1=bc8(Dec[:, 1, c, :]), op=ALU.mult),
                           [xg_c[c], dts_c], [xdec_c[k]])
                    bk = self.next_bank()
                    self.mm(bk.ap, Bg[:, c, :], xdec[:, k, :], True, True, [Bg_c[c], xdec_c[k]], [bk.cell])
                    return bk

                def pre_rec(c, bk):
                    if pair and c == NCH - 1:
                        self.V(lambda e: e.tensor_scalar(out=Hb[:], in0=Hb[:], scalar1=self.flagc[:, 0:1], scalar2=None, op0=ALU.mult), [Hb_c, self.flag_c], [Hb_c])
                    self.A(lambda e: e.activation(out=hpb[:, c, :], in_=Hb[:], func=AF.Copy), [Hb_c], [hpb_c[c]])
                    j = self.rot("tH", 2)
                    self.V(lambda e: e.tensor_tensor(out=v3(tH[:, j, :]), in0=v3(Hb[:]), in1=bc8(CDt[:, 1, c, :]), op=ALU.mult), [Hb_c, dts_c], [tH_c[j]])
                    self.V(lambda e: e.tensor_tensor(out=Hb[:], in0=bk.ap, in1=tH[:, j, :], op=ALU.add), [bk.cell, tH_c[j]], [Hb_c])
                prev = None
                for c in range(NC - 1, -1, -1):
                    bk = pre_states(c)
                    if prev is not None:
                        pre_rec(*prev)
                    prev = (c, bk)
                pre_rec(*prev)
                self.V(lambda e: e.memset(Hf[:], 0.0), [], [Hf_c])
                self.V(lambda e: e.memset(Hf16[:], 0.0), [], [Hf16_c])
                stt = {}

                def Sa(c):
                    tc = slice(c * 128, (c + 1) * 128)
                    d_ = stt[c] = {}
                    bcb = self.next_bank()
                    self.mm(bcb.ap[:, 0:128], BT[:, tc], CT[:, tc], True, True, [BT_c[c], CT_c[c]], [bcb.cell])
                    kc_ = d_["cb"] = self.rot("cbT", 3)
                    self.A(lambda e: e.activation(out=cbT[:, kc_, :], in_=bcb.ap[:, 0:128], func=AF.Copy), [bcb.cell], [cbT_c[kc_]])
                    kx = d_["xdt"] = self.rot("xdt", 3)
                    self.V(lambda e: e.tensor_tensor(out=xdt[:, kx, :, :].rearrange("p d (h e) -> p d h e", h=8),
                                                     in0=v3(xg[:, c, :]).unsqueeze(1).broadcast_to([128, 2, 8, 64]),
                                                     in1=dtv[:, :, c, :].unsqueeze(3).broadcast_to([128, 2, 8, 64]), op=ALU.mult),
                           [xg_c[c], dts_c], [xdt_c[kx]])
                    kd = d_["xdec"] = self.rot("xdecm", 3)
                    self.G(lambda e: e.tensor_tensor(out=v3(xdecm[:, kd, :]), in0=v3(xg[:, c, :]), in1=bc8(Dec[:, 0, c, :]), op=ALU.mult), [xg_c[c], dts_c], [xdecm_c[kd]])
                    k3 = d_["t3"] = self.rot("t3", 5)
                    self.G(lambda e: e.tensor_tensor(out=v3(t3[:, k3, :]), in0=v3(xg[:, c, :]), in1=bc8(D_bc[:, 8 * g:8 * g + 8]), op=ALU.mult), [xg_c[c], sc], [t3_c[k3]])
                    d_["ex"] = []
                    for bi in range(4):
                        d = bi // 2
                        h0 = (bi % 2) * 4
                        tri = self.trifB if d == 0 else self.tribB
                        bs = self.next_bank()
                        self.mm(bs.ap, self.identB, mask4[:, d, :], True, False, [self.cB_c, sc], [bs.cell], inc=False)
                        self.mm(bs.ap, tri, nda16[:, d, c, h0:h0 + 4].unsqueeze(2).broadcast_to([128, 4, 128]), False, False, [self.cB_c, dts_c], [bs.cell], inc=False)
                        for hl in range(4):
                            self.mm(bs.ap[:, hl * 128:(hl + 1) * 128], da16[:, d, c, h0 + hl:h0 + hl + 1].broadcast_to([128, 128]), tri, False, hl == 3,
                                    [self.cB_c, dts_c], [bs.cell], inc=(hl == 3))
                        ke = self.rot("ex", 8)
                        self.A(lambda e: e.activation(out=ex[:, ke, :], in_=bs.ap, func=AF.Exp), [bs.cell], [ex_c[ke]])
                        d_["ex"].append(ke)

                def Sb(c):
                    d_ = stt[c]
                    d_["wt"] = []
                    for bi in range(4):
                        ke = d_["ex"][bi]
                        kw_ = self.rot("WT", 8)
                        self.V(lambda e: e.tensor_tensor(out=WT[:, kw_, :].rearrange("p (h i) -> p h i", h=4), in0=ex[:, ke, :].rearrange("p (h i) -> p h i", h=4),
                                                         in1=cbT[:, d_["cb"], :].unsqueeze(1).broadcast_to([128, 4, 128]), op=ALU.mult), [ex_c[ke], cbT_c[d_["cb"]]], [WT_c[kw_]])
                        d_["wt"].append(kw_)

                def Sc(c):
                    d_ = stt[c]
                    tc = slice(c * 128, (c + 1) * 128)
                    kx, kd, wts = d_["xdt"], d_["xdec"], d_["wt"]
                    if pair and c == NCH:
                        self.V(lambda e: e.tensor_scalar(out=Hf[:], in0=Hf[:], scalar1=self.flagc[:, 0:1], scalar2=None, op0=ALU.mult), [Hf_c, self.flag_c], [Hf_c])
                        self.A(lambda e: e.activation(out=Hf16[:], in_=Hf[:], func=AF.Copy), [Hf_c], [Hf16_c])
                    bof = self.next_bank()
                    self.mm(bof.ap, CT[:, tc], Hf16[:], True, True, [CT_c[c], Hf16_c], [bof.cell])
                    bst = self.next_bank()
                    self.mm(bst.ap, Bg[:, c, :], xdecm[:, kd, :], True, True, [Bg_c[c], xdecm_c[kd]], [bst.cell])
                    j = self.rot("tH", 2)
                    self.V(lambda e: e.tensor_tensor(out=v3(tH[:, j, :]), in0=v3(Hf[:]), in1=bc8(CDt[:, 0, c, :]), op=ALU.mult), [Hf_c, dts_c], [tH_c[j]])
                    self.V(lambda e: e.tensor_tensor(out=Hf[:], in0=bst.ap, in1=tH[:, j, :], op=ALU.add), [bst.cell, tH_c[j]], [Hf_c])
                    self.A(lambda e: e.activation(out=Hf16[:], in_=Hf[:], func=AF.Copy), [Hf_c], [Hf16_c])
                    bob = self.next_bank()
                    self.mm(bob.ap, CT[:, tc], hpb[:, c, :], True, True, [CT_c[c], hpb_c[c]], [bob.cell])
                    by = self.next_bank()
                    for h in range(8):
                        for d in range(2):
                            kw_ = wts[d * 2 + h // 4]
                            hl = h % 4
                            self.mm(by.ap[:, h * 64:(h + 1) * 64], WT[:, kw_, hl * 128:(hl + 1) * 128], xdt[:, kx, d, h * 64:(h + 1) * 64], d == 0, d == 1,
                                    [WT_c[kw_], xdt_c[kx]], [by.cell], inc=(h == 7 and d == 1))
                    kt = d_["t1"] = self.rot("t1", 2)
                    self.V(lambda e: e.tensor_tensor(out=v3(t1[:, kt, :]), in0=v3(bof.ap), in1=bc8(Ef[:, 0, c, :]), op=ALU.mult), [bof.cell, dts_c], [t1_c[kt]])
                    self.V(lambda e: e.tensor_tensor(out=v3(t2[:, kt, :]), in0=v3(bob.ap), in1=bc8(Ef[:, 1, c, :]), op=ALU.mult), [bob.cell, dts_c], [t2_c[kt]])
                    ky = d_["ysb"] = self.rot("ysb", 3)
                    self.A(lambda e: e.activation(out=ysb[:, ky, :], in_=by.ap, func=AF.Copy), [by.cell], [ysb_c[ky]])
                    kz = d_["zs"] = self.rot("zs", 3)
                    r0 = tok0 + c * 128
                    self.dma("sp", zs[:, kz, :], self.zsc[r0:r0 + 128, g * 512:(g + 1) * 512], reads=[self.ssm_c[r0 // TTn]], writes=[zs_c[kz]])

                def Sd(c):
                    d_ = stt[c]
                    kt, k3 = d_["t1"], d_["t3"]
                    self.G(lambda e: e.tensor_tensor(out=t1[:, kt, :], in0=t1[:, kt, :], in1=t2[:, kt, :], op=ALU.add), [t1_c[kt], t2_c[kt]], [t1_c[kt]])
                    self.G(lambda e: e.tensor_tensor(out=t3[:, k3, :], in0=t3[:, k3, :], in1=t1[:, kt, :], op=ALU.add), [t1_c[kt], t3_c[k3]], [t3_c[k3]])

                def Se(c):
                    d_ = stt[c]
                    ky, k3, kz = d_["ysb"], d_["t3"], d_["zs"]
                    self.V(lambda e: e.tensor_tensor(out=ysb[:, ky, :], in0=ysb[:, ky, :], in1=t3[:, k3, :], op=ALU.add), [ysb_c[ky], t3_c[k3]], [ysb_c[ky]])
                    kq = d_["yz"] = self.rot("yz", 2)
                    self.V(lambda e: e.tensor_tensor(out=yz[:, kq, :], in0=ysb[:, ky, :], in1=zs[:, kz, :], op=ALU.mult), [ysb_c[ky], zs_c[kz]], [yz_c[kq]])
                    kss = d_["ssq"] = self.rot("ssq", 3)
                    self.A(lambda e: e.activation(out=ysb[:, ky, :], in_=yz[:, kq, :], func=AF.Square, accum_out=ssq[:, kss, 0:1]), [yz_c[kq]], [ysb_c[ky], ssq_c[kss]])

                def Sf(c):
                    d_ = stt[c]
                    kq, kss = d_["yz"], d_["ssq"]
                    self.A(lambda e: e.activation(out=ssq[:, kss, 1:2], in_=ssq[:, kss, 0:1], func=AF.Ln, scale=1.0 / 512, bias=self.epsc[:]), [ssq_c[kss], self.eps_c], [ssq_c[kss]])
                    self.A(lambda e: e.activation(out=ssq[:, kss, 1:2], in_=ssq[:, kss, 1:2], func=AF.Exp, scale=-0.5), [ssq_c[kss]], [ssq_c[kss]])
                    kn = d_["yn"] = self.rot("yn", 2)
                    self.V(lambda e: e.scalar_tensor_tensor(out=yn[:, kn, :], in0=yz[:, kq, :], scalar=ssq[:, kss, 1:2], in1=ng_bc[:],
                                                            op0=ALU.mult, op1=ALU.mult), [yz_c[kq], ssq_c[kss], ng_c], [yn_c[kn]])

                def Sg(c):
                    d_ = stt.pop(c)
                    kn = d_["yn"]
                    for k4 in range(4):
                        self.P(lambda e, k4=k4: e.transpose(self.PT[:, k4 * 128:(k4 + 1) * 128], yn[:, kn, k4 * 128:(k4 + 1) * 128], self.identB),
                               [yn_c[kn], self.cB_c], [_ptc], inc=(k4 == 3))
                    ks = self.rot("yst", 2)
                    self.A(lambda e: e.activation(out=yst[:, ks, :], in_=self.PT[:, 0:512], func=AF.Copy), [_ptc], [yst_c[ks]])
                    r0 = tok0 + c * 128
                    u = r0 // UL
                    self.dma("pool", self.yT[4 * g:4 * g + 4].rearrange("k p n -> p k n")[:, :, r0:r0 + 128], yst[:, ks, :].rearrange("p (k n) -> p k n", k=4),
                             reads=[yst_c[ks]], accum=[self.yT_c[u]])

                order = [(Sc, 2), (Sa, 0), (Sb, 1), (Sd, 3), (Se, 4), (Sf, 5), (Sg, 6)]
                for i in range(NC + 6):
                    for fn, lag in order:
                        c = i - lag
                        if 0 <= c < NC:
                            fn(c)


class MK4(MK3):
    def stage_attn(self, es):
        UL = self.UL
        Sm = 2 * UL if self.NU >= 2 else UL
        NBm = Sm // 128
        sb = lambda n, s, d: self.sb(es, n, s, d)
        w = self.w_in
        sc = Cell()
        relb = sb("relb", [128, NBUCK * 8], F32)
        self.dma("sp", relb[:], w["rel_bias"].rearrange("b h -> (b h)").partition_broadcast(128), writes=[sc])
        cLx = sb("cLx", [128, 8], F32)
        cRx = sb("cRx", [128, 8], F32)
        self.V(lambda e: e.tensor_scalar(out=cLx[:], in0=relb[:, 15 * 8:16 * 8], scalar1=self.maskc[:, 0:1], scalar2=None, op0=ALU.add), [sc, self.flag_c], [sc])
        self.V(lambda e: e.tensor_scalar(out=cRx[:], in0=relb[:, 31 * 8:32 * 8], scalar1=self.maskc[:, 0:1], scalar2=None, op0=ALU.add), [sc, self.flag_c], [sc])
        zeroc = sb("zeroc", [128, 1], F32)
        self.V(lambda e: e.memset(zeroc[:], 0.0), [], [sc])
        lamb = sb("lamb", [128, 4, 64], F32)
        self.dma("sp", lamb[:], w["attn_lambda"][0].rearrange("a d -> (a d)").partition_broadcast(128), writes=[sc])
        lt = sb("lt", [128, 2, 64], F32)
        ls = sb("ls", [128, 4], F32)
        self.V(lambda e: e.tensor_tensor(out=lt[:, 0, :], in0=lamb[:, 0, :], in1=lamb[:, 1, :], op=ALU.mult), [sc], [sc])
        self.V(lambda e: e.tensor_tensor(out=lt[:, 1, :], in0=lamb[:, 2, :], in1=lamb[:, 3, :], op=ALU.mult), [sc], [sc])
        self.V(lambda e: e.reduce_sum(out=ls[:, 0:2], in_=lt[:], axis=mybir.AxisListType.X), [sc], [sc])
        self.A(lambda e: e.activation(out=ls[:, 0:2], in_=ls[:, 0:2], func=AF.Exp), [sc], [sc])
        self.V(lambda e: e.tensor_tensor(out=ls[:, 2:3], in0=ls[:, 1:2], in1=ls[:, 0:1], op=ALU.subtract), [sc], [sc])
        self.V(lambda e: e.tensor_scalar(out=ls[:, 3:4], in0=ls[:, 2:3], scalar1=-LAMBDA_INIT, scalar2=None, op0=ALU.add), [sc], [sc])
        neglam = ls[:, 3:4]
        gsub = sb("gsub", [128, 128], F32)
        self.dma("sp", gsub[:], w["attn_subln"][0].partition_broadcast(128), writes=[sc])
        self.V(lambda e: e.tensor_scalar(out=gsub[:], in0=gsub[:], scalar1=1.0 - LAMBDA_INIT, scalar2=None, op0=ALU.mult), [sc], [sc])
        tab = sb("tab", [NBUCK, 8], F32)
        ohs = sb("ohs", [NBUCK, FV], F32)
        fsb = sb("fsb", [8, FV], F32)
        self.dma("sp", tab[:], w["rel_bias"][:, :], writes=[sc])
        self.dma("sp", ohs[:], self.oh_in[:, :], writes=[sc])
        for i0 in range(0, FV, 512):
            n = min(512, FV - i0)
            bk = self.next_bank()
            self.mm(bk.ap[0:8, 0:n], tab[:], ohs[:, i0:i0 + n], True, True, [sc], [bk.cell])
            self.V(lambda e: e.tensor_copy(out=fsb[:, i0:i0 + n], in_=bk.ap[0:8, 0:n]), [bk.cell], [sc])
        self.dma("pool", self.fvec[:, :], fsb[:], reads=[sc], writes=[self.fvec_c])
        QT = sb("QT", [128, 2, Sm], BF16); QT_c = _grid(2)
        KT = sb("KT", [128, 2, Sm], BF16); KT_c = _grid(2)
        Vh = sb("Vh", [128, 2, NBm, 129], BF16); Vh_c = _grid(2)
        self.V(lambda e: e.memset(Vh[:, :, :, 128:129], 1.0), [], Vh_c)
        hk = sb("hk", [128, 2, 9 * 128], F32); hk_c = _grid(2)
        TBr = sb("TBr", [128, 2, 9 * 128], F32); TBr_c = _grid(2)
        et = sb("et", [128, 3, 2, 512], BF16); et_c = _grid(3)
        accs = sb("accs", [128, 2, 8, 129], F32); accs_c = _grid(2)
        rr = sb("rr", [128, 2, 8], F32); rr_c = _grid(2)
        nl = sb("nl", [128, 2, 4], F32); nl_c = _grid(2)
        o0 = sb("o0", [128, 2, 512], F32); o0_c = _grid(2)
        o1 = sb("o1", [128, 2, 512], F32); o1_c = _grid(2)
        sqt = sb("sqt", [128, 512], F32); sqt_c = Cell()
        ss = sb("ss", [128, 2, 8], F32); ss_c = _grid(2)
        on16 = sb("on16", [128, 2, 512], BF16); on_c = _grid(2)
        ost = sb("ost", [128, 2, Sm], BF16); ost_c = _grid(2)
        _ptc = Cell()
        PTh = [_ptc, _ptc]
        acc_banks = [self.banks[4], self.banks[5], self.bankD]
        lg_pairs = [self.pairs[0], self.pairs[1]]

        def acc_ap(a):
            b, sl = divmod(a, 3)
            return acc_banks[b].ap[:, sl * 129:(sl + 1) * 129], acc_banks[b].cell

        sgs = self.seq_groups()
        for h in range(8):
            kh = self.rot("hk", 2)
            hap = bass.AP(tensor=self.fvec.tensor, offset=h * FV + (FV // 2 - 4 * 128 - 127), ap=[[1, 128], [1, 9 * 128]])
            self.dma("sp", hk[:, kh, :], hap, reads=[self.fvec_c], writes=[hk_c[kh]])
            for b0 in range(0, 9, 4):
                nb = min(4, 9 - b0)
                bk = self.next_bank()
                for i in range(b0, b0 + nb):
                    dl = 4 - i
                    self.mm(bk.ap[:, (i - b0) * 128:(i - b0 + 1) * 128], hk[:, kh, (dl + 4) * 128:(dl + 5) * 128], self.antiF, True, True,
                            [hk_c[kh], self.cF_c], [bk.cell], inc=(i == b0 + nb - 1))
                self.V(lambda e: e.tensor_scalar(out=TBr[:, kh, b0 * 128:(b0 + nb) * 128], in0=bk.ap[:, 0:nb * 128], scalar1=8.0, scalar2=None, op0=ALU.mult), [bk.cell], [TBr_c[kh]])
            for sg in sgs:
                S = len(sg) * UL
                NB = S // 128
                tok0 = sg[0] * UL
                kq = self.rot("qkv", 2)
                tcells = [self.qkv_c[(tok0 // self.TT) + i] for i in range(S // self.TT)]
                self.dma("sp", QT[:, kq, 0:S], self.qT[h][:, tok0:tok0 + S], reads=tcells, writes=[QT_c[kq]])
                self.dma("sp", KT[:, kq, 0:S], self.kT[h][:, tok0:tok0 + S], reads=tcells, writes=[KT_c[kq]])
                self.dma("sp", Vh[:, kq, 0:NB, 0:128], self.vtm[tok0:tok0 + S, h * 128:(h + 1) * 128].rearrange("(b p) d -> p b d", p=128),
                         reads=tcells, writes=[Vh_c[kq]])
                ko = self.rot("ost", 2)
                def front(qc, kb):
                    uq = (qc * 512) // UL
                    uk = (kb * 128) // UL
                    dl = kb - 4 * qc
                    near = -1 <= dl <= 4
                    pr = lg_pairs[self.rot("lgp", 2)]
                    for m in range(2):
                        self.mm(pr.ap[:, m * 512:(m + 1) * 512], KT[64 * m:64 * m + 64, kq, kb * 128:(kb + 1) * 128],
                                QT[64 * m:64 * m + 64, kq, qc * 512:(qc + 1) * 512], True, True, [KT_c[kq], QT_c[kq]], [pr.c0, pr.c1], inc=(m == 1))
                    if near:
                        bview = TBr[:, kh, (4 - dl) * 128:(8 - dl) * 128]
                        self.V(lambda e: e.tensor_tensor(out=pr.ap.rearrange("p (m q) -> p m q", m=2), in0=pr.ap.rearrange("p (m q) -> p m q", m=2),
                                                         in1=bview.unsqueeze(1).broadcast_to([128, 2, 512]), op=ALU.add),
                               [pr.c0, pr.c1, TBr_c[kh]], [pr.c0, pr.c1])
                        bcol = self.maskc[:, 0:1] if uk != uq else zeroc[:, 0:1]
                    elif dl < -1:
                        bcol = cLx[:, h:h + 1] if uk != uq else relb[:, 15 * 8 + h:15 * 8 + h + 1]
                    else:
                        bcol = cRx[:, h:h + 1] if uk != uq else relb[:, 31 * 8 + h:31 * 8 + h + 1]
                    ke = self.rot("et", 3)
                    self.A(lambda e: e.activation(out=et[:, ke, :, :], in_=pr.ap.rearrange("p (m q) -> p m q", m=2), func=AF.Exp, scale=SCALE, bias=bcol),
                           [pr.c0, pr.c1, sc, self.flag_c], [et_c[ke]])
                    return ke

                def back(qc, kb, ke):
                    for m in range(2):
                        for qb in range(4):
                            ap_, cell_ = acc_ap(m * 4 + qb)
                            self.mm(ap_, et[:, ke, m, qb * 128:(qb + 1) * 128], Vh[:, kq, kb, :], kb == 0 and (m * 4 + qb) % 3 == 0, kb == NB - 1,
                                    [et_c[ke], Vh_c[kq]], [cell_], inc=(m == 1 and qb == 3), skip=True)
                    if kb == NB - 1:
                        finalize(qc)

                def finalize(qc):
                    ka = self.rot("accs", 2)
                    for b in range(3):
                        ns = 3 if b < 2 else 2
                        self.A(lambda e, b=b, ns=ns: e.activation(out=accs[:, ka, 3 * b:3 * b + ns, :].rearrange("p a d -> p (a d)"), in_=acc_banks[b].ap[:, 0:ns * 129], func=AF.Copy),
                               [acc_banks[b].cell], [accs_c[ka]])
                    self.V(lambda e: e.reciprocal(out=rr[:, ka, :], in_=accs[:, ka, :, 128]), [accs_c[ka]], [rr_c[ka]])
                    self.V(lambda e: e.tensor_scalar(out=nl[:, ka, :], in0=rr[:, ka, 4:8], scalar1=neglam, scalar2=None, op0=ALU.mult), [rr_c[ka], sc], [nl_c[ka]])
                    v4 = lambda ap: ap.rearrange("p (a d) -> p a d", a=4)
                    self.V(lambda e: e.tensor_tensor(out=v4(o0[:, ka, :]), in0=accs[:, ka, 0:4, 0:128], in1=rr[:, ka, 0:4].unsqueeze(2).broadcast_to([128, 4, 128]), op=ALU.mult),
                           [accs_c[ka], rr_c[ka]], [o0_c[ka]])
                    self.V(lambda e: e.tensor_tensor(out=v4(o1[:, ka, :]), in0=accs[:, ka, 4:8, 0:128], in1=nl[:, ka, :].unsqueeze(2).broadcast_to([128, 4, 128]), op=ALU.mult),
                           [accs_c[ka], nl_c[ka]], [o1_c[ka]])
                    self.G(lambda e: e.tensor_tensor(out=o0[:, ka, :], in0=o0[:, ka, :], in1=o1[:, ka, :], op=ALU.add), [o0_c[ka], o1_c[ka]], [o0_c[ka]])
                    self.G(lambda e: e.tensor_tensor(out=sqt[:], in0=o0[:, ka, :], in1=o0[:, ka, :], op=ALU.mult), [o0_c[ka]], [sqt_c])
                    self.V(lambda e: e.reduce_sum(out=ss[:, ka, 0:4], in_=v4(sqt[:]), axis=mybir.AxisListType.X), [sqt_c], [ss_c[ka]])
                    self.A(lambda e: e.activation(out=ss[:, ka, 4:8], in_=ss[:, ka, 0:4], func=AF.Ln, scale=1.0 / 128, bias=self.epsc[:]), [ss_c[ka], self.eps_c], [ss_c[ka]])
                    self.A(lambda e: e.activation(out=ss[:, ka, 4:8], in_=ss[:, ka, 4:8], func=AF.Exp, scale=-0.5), [ss_c[ka]], [ss_c[ka]])
                    self.V(lambda e: e.tensor_tensor(out=v4(o1[:, ka, :]), in0=v4(o0[:, ka, :]), in1=ss[:, ka, 4:8].unsqueeze(2).broadcast_to([128, 4, 128]), op=ALU.mult),
                           [o0_c[ka], ss_c[ka]], [o1_c[ka]])
                    self.V(lambda e: e.tensor_tensor(out=v4(on16[:, ka, :]), in0=v4(o1[:, ka, :]), in1=gsub[:].unsqueeze(1).broadcast_to([128, 4, 128]), op=ALU.mult),
                           [o1_c[ka], sc], [on_c[ka]])
                    ph = self.rot("PTh", 2)
                    for qb in range(4):
                        self.P(lambda e, qb=qb: e.transpose(self.PT[:, ph * 512 + qb * 128:ph * 512 + (qb + 1) * 128], on16[:, ka, qb * 128:(qb + 1) * 128], self.identB),
                               [on_c[ka], self.cB_c], [PTh[ph]], inc=(qb == 3))
                    self.A(lambda e: e.activation(out=ost[:, ko, qc * 512:(qc + 1) * 512], in_=self.PT[:, ph * 512:(ph + 1) * 512], func=AF.Copy), [PTh[ph]], [ost_c[ko]])

                its = [(qc, kb) for qc in range(S // 512) for kb in range(NB)]
                pend = None
                for it in its:
                    ke = front(*it)
                    if pend is not None:
                        back(*pend)
                    pend = (it[0], it[1], ke)
                back(*pend)
                self.dma("pool", self.oT[h][:, tok0:tok0 + S], ost[:, ko, 0:S], reads=[ost_c[ko]], accum=[self.oT_c[u] for u in sg])


WEIGHT_KEYS = ["norm_pre", "norm_post", "ffn_w_gate", "ffn_w_up", "ffn_w_down", "ssm_w_in", "ssm_conv_w",
               "ssm_conv_b", "ssm_dt_bias", "ssm_a_log", "ssm_d", "ssm_norm", "ssm_w_out", "attn_w_qkv",
               "attn_lambda", "attn_subln", "attn_w_out", "rel_bias"]


def build(NU=5, UL=2048, TT=1024, stages="AMBNC", debug=()):
    mk = MKF(NU, UL, TT)
    mk.debug = set(debug)
    mk.declare()
    with mk.es:
        mk.setup_engines()
        mk.setup_psum()
        mk.setup_consts()
        mk.cast_weights()
        for st in stages:
            with ExitStack() as es:
                if st in "ABC":
                    mk.tl_alloc(es)
                    {"A": mk.stage_A, "B": mk.stage_B, "C": mk.stage_C}[st](es)
                elif st == "M":
                    mk.stage_ssd(es)
                elif st == "N":
                    mk.stage_attn(es)
                mk.barrier()
        mk.barrier()
    return mk


_CACHE = {}


def kernel(**inputs):
    x_prompt = np.ascontiguousarray(inputs["x_prompt"], dtype=np.float32)
    x_sample = np.ascontiguousarray(inputs["x_sample"], dtype=np.float32)
    NB, S, _ = x_prompt.shape
    SB, SS, _ = x_sample.shape
    assert (NB, S, SB, SS) == (4, 4096, 32, 2048)
    if "mk" not in _CACHE:
        _CACHE["mk"] = build()
    mk = _CACHE["mk"]
    consts, oh = make_consts()
    in_maps = []
    plan = []
    for c in range(8):
        if c < 4:
            samp = [3 * c, 3 * c + 1, 3 * c + 2]
            xs = np.concatenate([x_prompt[c]] + [x_sample[i] for i in samp], axis=0)
            flag = 1.0
        else:
            samp = [12 + 5 * (c - 4) + i for i in range(5)]
            xs = np.concatenate([x_sample[i] for i in samp], axis=0)
            flag = 0.0
        plan.append(samp)
        m = {"x": np.ascontiguousarray(xs), "flag": np.full((1, 1), flag, np.float32), "consts": consts, "bucket_oh": oh}
        for k in WEIGHT_KEYS:
            m[k] = np.ascontiguousarray(inputs[k], dtype=np.float32)
        in_maps.append(m)
    res = run_bass_kernel_spmd(mk.nc, in_maps, core_ids=list(range(8)))
    y_prompt = np.empty_like(x_prompt)
    y_sample = np.empty_like(x_sample)
    for c in range(8):
        y = np.asarray(res.results[c]["y"], dtype=np.float32)
        off = 0
        if c < 4:
            y_prompt[c] = y[0:4096]
            off = 4096
        for i in plan[c]:
            y_sample[i] = y[off:off + 2048]
            off += 2048
    return (y_prompt, y_sample)

MKF = MK4
```

```python
import math
from contextlib import ExitStack
import numpy as np
import concourse.bass as bass
import concourse.mybir as mybir
from concourse.bass_utils import run_bass_kernel_spmd

F32 = mybir.dt.float32
BF16 = mybir.dt.bfloat16
AF = mybir.ActivationFunctionType
ALU = mybir.AluOpType

D = 1024
DC = 8
DFF = 2816
FC = 22
DIN = 2048
NHS = 32
HD = 64
NG = 4
NST = 128
CONVD = 3072
SSM_IN = 5184
EPS = 1e-6
NBUCK = 32
LAMBDA_INIT = 0.8 - 0.6 * math.exp(-0.3 * 1)
SCALE = 64 ** -0.5
NEGBIG = -30000.0
FV = 1280


def _bucket(rel):
    half = NBUCK // 2
    max_exact = half // 2
    ret = np.where(rel > 0, half, 0)
    n = np.abs(rel)
    nf = np.maximum(n, 1).astype(np.float32)
    large = max_exact + (np.log(nf / np.float32(max_exact)) / np.float32(math.log(128 / max_exact)) * np.float32(half - max_exact)).astype(np.int32)
    large = np.minimum(large, half - 1)
    return ret + np.where(n < max_exact, n, large)


def make_consts():
    c = {}
    i = np.arange(128)
    c["ident"] = np.eye(128, dtype=np.float32)
    c["antiid"] = np.eye(128, dtype=np.float32)[::-1].copy()
    c["trif"] = (i[:, None] <= i[None, :]).astype(np.float32)
    c["trib"] = (i[:, None] >= i[None, :]).astype(np.float32)
    c["maskf"] = np.where(i[None, :] >= i[:, None], 0.0, NEGBIG).astype(np.float32)
    c["maskb"] = np.where(i[None, :] <= i[:, None], 0.0, NEGBIG).astype(np.float32)
    rel = np.arange(-FV // 2, FV // 2)
    b = _bucket(rel)
    oh = np.zeros((NBUCK, FV), np.float32)
    oh[b, np.arange(FV)] = 1.0
    c["bucket_oh"] = oh
    return np.concatenate([c["ident"], c["antiid"], c["trif"], c["trib"], c["maskf"], c["maskb"]], axis=1), oh


class Cell:
    __slots__ = ("w", "r", "aw")

    def __init__(self):
        self.w = None
        self.r = {}
        self.aw = {}


class Sem:
    __slots__ = ("h", "id")

    def __init__(self, h, i):
        self.h = h
        self.id = i


class Eng:
    def __init__(self, name, eng, sem, is_pe=False):
        self.name = name
        self.eng = eng
        self.sem = sem
        self.count = 0
        self.seen = {}
        self.is_pe = is_pe
        self.pend_r = []
        self.pend_w = []


class Slot:
    def __init__(self, sem):
        self.sem = sem
        self.val = 0


def cells_of(x):
    if isinstance(x, Cell):
        return [x]
    out = []
    for y in x:
        out.extend(cells_of(y))
    return out


class MK:
    def __init__(self, NU=5, UL=2048, TT=1024, debug_stage=None):
        self.NU, self.UL, self.TT = NU, UL, TT
        self.SW = 512
        assert TT % 512 == 0 and UL % TT == 0
        self.TS = TT // 512
        self.NTOK = NU * UL
        self.NT = self.NTOK // TT
        self.NCH = UL // 128
        self.debug_stage = debug_stage
        self.nc = bass.Bass("TRN2", target_bir_lowering=False)
        self.es = ExitStack()
        self.n_inst = 0
        self._uid = 0

    def uid(self, p):
        self._uid += 1
        return f"{p}{self._uid}"

    def sb(self, es, name, shape, dt):
        return es.enter_context(self.nc.sbuf_tensor(self.uid(name), list(shape), dt))

    def dram(self, name, shape, dt, kind="Internal"):
        if name in getattr(self, "debug", ()):
            kind = "ExternalOutput"
        return self.nc.dram_tensor(name, list(shape), dt, kind=kind).ap()

    def setup_engines(self):
        nc = self.nc
        self.sems = []

        def mksem(name):
            h = self.es.enter_context(nc.semaphore(name))
            s = Sem(h, len(self.sems))
            self.sems.append(s)
            return s
        self.E = {
            "pe": Eng("pe", nc.tensor, mksem("s_pe"), is_pe=True),
            "act": Eng("act", nc.scalar, mksem("s_act")),
            "dve": Eng("dve", nc.vector, mksem("s_dve")),
            "pool": Eng("pool", nc.gpsimd, mksem("s_pool")),
            "sp": Eng("sp", nc.sync, mksem("s_sp")),
        }
        NS = 16
        self.slots = {q: [Slot(mksem(f"d_{q}{i}")) for i in range(NS)] for q in ("sp", "pool")}
        self.slot_i = {"sp": 0, "pool": 0}

    def _waits(self, e, reads, writes, is_dma=False, accum=()):
        need = {}

        def add(tok, raw):
            s, v = tok
            if (not is_dma) and s is e.sem:
                if e.is_pe:
                    return
            if need.get(s.id, (None, 0))[1] < v:
                need[s.id] = (s, v)
        for c in reads:
            if c.w is not None:
                add(c.w, True)
            for sid, tok in c.aw.items():
                add(tok, True)
        for c in writes:
            if c.w is not None:
                add(c.w, False)
            for sid, tok in c.aw.items():
                add(tok, False)
            for sid, tok in c.r.items():
                add(tok, False)
        for c in accum:
            if c.w is not None:
                add(c.w, False)
            for sid, tok in c.r.items():
                add(tok, False)
        for sid, (s, v) in need.items():
            if e.seen.get(sid, 0) < v:
                e.eng.wait_ge(s.h, v)
                e.seen[sid] = v
                self.n_inst += 1

    def emit(self, en, make, reads=(), writes=(), inc=True):
        e = self.E[en]
        reads = cells_of(reads)
        writes = cells_of(writes)
        self._waits(e, reads, writes)
        ins = make(e.eng)
        self.n_inst += 1
        if not inc:
            e.pend_r.extend(reads)
            e.pend_w.extend(writes)
            return ins
        e.count += 1
        ins.then_inc(e.sem.h, 1)
        tok = (e.sem, e.count)
        for c in e.pend_r + reads:
            c.r[e.sem.id] = tok
        for c in e.pend_w + writes:
            c.w = tok
            c.r = {}
            c.aw = {}
        e.pend_r = []
        e.pend_w = []
        return ins

    def dma(self, q, out, in_, reads=(), writes=(), accum=(), **kw):
        e = self.E[q]
        reads = cells_of(reads)
        writes = cells_of(writes)
        accum = cells_of(accum)
        i = self.slot_i[q]
        self.slot_i[q] = i + 1
        sl = self.slots[q][i % len(self.slots[q])]
        if sl.val > 0 and e.seen.get(sl.sem.id, 0) < sl.val:
            e.eng.wait_ge(sl.sem.h, sl.val)
            e.seen[sl.sem.id] = sl.val
        self._waits(e, reads, writes, is_dma=True, accum=accum)
        ins = e.eng.dma_start(out=out, in_=in_, **kw)
        self.n_inst += 1
        sl.val += 16
        ins.then_inc(sl.sem.h, 16)
        tok = (sl.sem, sl.val)
        for c in reads:
            c.r[sl.sem.id] = tok
        for c in writes:
            c.w = tok
            c.r = {}
            c.aw = {}
        for c in accum:
            c.aw[sl.sem.id] = tok
        return ins

    def barrier(self):
        toks = []
        for e in self.E.values():
            assert not e.pend_r and not e.pend_w
            if e.count > 0:
                toks.append((e.sem, e.count))
        for q in self.slots:
            for sl in self.slots[q]:
                if sl.val > 0:
                    toks.append((sl.sem, sl.val))
        for e in self.E.values():
            for s, v in toks:
                if s is e.sem:
                    continue
                if e.seen.get(s.id, 0) < v:
                    e.eng.wait_ge(s.h, v)
                    e.seen[s.id] = v
                    self.n_inst += 1

    def V(self, make, r=(), w=(), inc=True):
        return self.emit("dve", make, r, w, inc)

    def A(self, make, r=(), w=(), inc=True):
        return self.emit("act", make, r, w, inc)

    def G(self, make, r=(), w=(), inc=True):
        return self.emit("pool", make, r, w, inc)

    def P(self, make, r=(), w=(), inc=True):
        return self.emit("pe", make, r, w, inc)

    def mm(self, out, lhsT, rhs, start, stop, r=(), w=(), inc=True, skip=False):
        if skip:
            return self.P(lambda e: e.matmul(out, lhsT=lhsT, rhs=rhs, start=start, stop=stop, skip_group_check=True), r, w, inc)
        return self.P(lambda e: e.matmul(out, lhsT=lhsT, rhs=rhs, start=start, stop=stop), r, w, inc)

    def declare(self):
        nc = self.nc
        NTOK = self.NTOK
        ext = lambda n, s: nc.dram_tensor(n, list(s), F32, kind="ExternalInput").ap()
        self.x_in = ext("x", [NTOK, D])
        self.flag_in = ext("flag", [1, 1])
        self.consts_in = ext("consts", [128, 6 * 128])
        self.oh_in = ext("bucket_oh", [NBUCK, FV])
        self.w_in = {
            "norm_pre": ext("norm_pre", [2, 3, D]), "norm_post": ext("norm_post", [2, 3, D]),
            "ffn_w_gate": ext("ffn_w_gate", [2, 2, D, DFF]), "ffn_w_up": ext("ffn_w_up", [2, 2, D, DFF]),
            "ffn_w_down": ext("ffn_w_down", [2, 2, DFF, D]),
            "ssm_w_in": ext("ssm_w_in", [1, D, SSM_IN]), "ssm_conv_w": ext("ssm_conv_w", [1, 5, CONVD]),
            "ssm_conv_b": ext("ssm_conv_b", [1, CONVD]), "ssm_dt_bias": ext("ssm_dt_bias", [1, 2, NHS]),
            "ssm_a_log": ext("ssm_a_log", [1, 2, NHS]), "ssm_d": ext("ssm_d", [1, NHS]),
            "ssm_norm": ext("ssm_norm", [1, DIN]), "ssm_w_out": ext("ssm_w_out", [1, DIN, D]),
            "attn_w_qkv": ext("attn_w_qkv", [1, D, 3072]), "attn_lambda": ext("attn_lambda", [1, 4, 64]),
            "attn_subln": ext("attn_subln", [1, 128]), "attn_w_out": ext("attn_w_out", [1, D, D]),
            "rel_bias": ext("rel_bias", [NBUCK, 8]),
        }
        self.y_out = nc.dram_tensor("y", [NTOK, D], F32, kind="ExternalOutput").ap()
        self.wb = {}
        self.wb_cell = {}
        for li in range(2):
            for fi in range(2):
                for nm, shp in (("gate", [D, DFF]), ("up", [D, DFF]), ("down", [DFF, D])):
                    k = f"{nm}{li}{fi}"
                    self.wb[k] = self.dram("wb_" + k, shp, BF16)
                    self.wb_cell[k] = Cell()
        for k, shp in (("ssm_in", [D, SSM_IN]), ("ssm_out", [DIN, D]), ("qkv", [D, 3072]), ("attn_out", [D, D])):
            self.wb[k] = self.dram("wb_" + k, shp, BF16)
            self.wb_cell[k] = Cell()
        NT, TT = self.NT, self.TT
        self.xres = self.dram("xres", [NT, 128, DC * TT], F32)
        self.xres_c = [Cell() for _ in range(NT)]
        self.xbc = self.dram("xbc", [24, 128, NTOK], BF16)
        self.zsc = self.dram("zsc", [NTOK, DIN], BF16)
        self.dtr = self.dram("dtr", [NTOK, 64], F32)
        self.ssm_c = [Cell() for _ in range(NT)]
        self.yT = self.dram("yT", [16, 128, NTOK], BF16)
        self.yT_c = [Cell() for _ in range(self.NU)]
        self.qT = self.dram("qT", [8, 128, NTOK], BF16)
        self.kT = self.dram("kT", [8, 128, NTOK], BF16)
        self.vtm = self.dram("vtm", [NTOK, D], BF16)
        self.qkv_c = [Cell() for _ in range(NT)]
        self.oT = self.dram("oT", [8, 128, NTOK], BF16)
        self.oT_c = [Cell() for _ in range(self.NU)]
        self.fvec = self.dram("fvec", [8, FV], F32)
        self.fvec_c = Cell()
        self.dbg_out = None

    def setup_consts(self):
        es = self.es
        nc = self.nc
        sb = lambda n, s, d: self.sb(es, n, s, d)
        self.cF = sb("cF", [128, 6 * 128], F32)
        self.cF_c = Cell()
        self.cB = sb("cB", [128, 6 * 128], BF16)
        self.cB_c = Cell()
        self.onesB = sb("onesB", [128, 128], BF16)
        self.onesB_c = Cell()
        self.dma("sp", self.cF[:], self.consts_in[:, :], writes=[self.cF_c])
        self.V(lambda e: e.tensor_copy(out=self.cB[:], in_=self.cF[:]), [self.cF_c], [self.cB_c])
        self.V(lambda e: e.memset(self.onesB[:], 1.0), [], [self.onesB_c])
        self.identF = self.cF[:, 0:128]
        self.antiF = self.cF[:, 128:256]
        self.identB = self.cB[:, 0:128]
        self.trifB = self.cB[:, 256:384]
        self.tribB = self.cB[:, 384:512]
        self.maskfB = self.cB[:, 512:640]
        self.maskbB = self.cB[:, 640:768]
        self.gpre = sb("gpre", [128, 6, DC], F32)
        self.gpost = sb("gpost", [128, 6, DC], F32)
        self.gposth = sb("gposth", [128, 6, DC], F32)
        self.g_c = Cell()
        self.dma("sp", self.gpre[:], self.w_in["norm_pre"].rearrange("l j (c p) -> p (l j) c", p=128),
                 writes=[self.g_c], allow_slow_non_contiguous=True)
        self.dma("sp", self.gpost[:], self.w_in["norm_post"].rearrange("l j (c p) -> p (l j) c", p=128),
                 writes=[self.g_c], allow_slow_non_contiguous=True)
        self.V(lambda e: e.tensor_scalar(out=self.gposth[:], in0=self.gpost[:], scalar1=0.5, scalar2=None, op0=ALU.mult),
               [self.g_c], [self.g_c])
        self.flagc = sb("flagc", [128, 1], F32)
        self.maskc = sb("maskc", [128, 1], F32)
        self.flag_c = Cell()
        self.dma("sp", self.flagc[:], self.flag_in.partition_broadcast(128), writes=[self.flag_c])
        self.V(lambda e: e.tensor_scalar(out=self.maskc[:], in0=self.flagc[:], scalar1=-NEGBIG, scalar2=NEGBIG,
                                         op0=ALU.mult, op1=ALU.add), [self.flag_c], [self.flag_c])
        self.epsc = sb("epsc", [128, 1], F32)
        self.eps_c = Cell()
        self.V(lambda e: e.memset(self.epsc[:], EPS), [], [self.eps_c])

    def cast_weights(self):
        def cast(dst, src, cell, rows, cols, nsplit):
            rs = rows // nsplit
            for i in range(nsplit):
                self.dma("pool", dst[i * rs:(i + 1) * rs, :], src[i * rs:(i + 1) * rs, :], accum=[cell])
        order = []
        for li in range(2):
            for fi in range(2):
                order.append((li, fi))
        w = self.w_in
        def ffn(li, fi):
            cast(self.wb[f"gate{li}{fi}"], w["ffn_w_gate"][li, fi], self.wb_cell[f"gate{li}{fi}"], D, DFF, 4)
            cast(self.wb[f"up{li}{fi}"], w["ffn_w_up"][li, fi], self.wb_cell[f"up{li}{fi}"], D, DFF, 4)
            cast(self.wb[f"down{li}{fi}"], w["ffn_w_down"][li, fi], self.wb_cell[f"down{li}{fi}"], DFF, D, 4)
        ffn(0, 0)
        cast(self.wb["ssm_in"], w["ssm_w_in"][0], self.wb_cell["ssm_in"], D, SSM_IN, 8)
        cast(self.wb["ssm_out"], w["ssm_w_out"][0], self.wb_cell["ssm_out"], DIN, D, 2)
        ffn(0, 1)
        ffn(1, 0)
        cast(self.wb["qkv"], w["attn_w_qkv"][0], self.wb_cell["qkv"], D, 3072, 4)
        cast(self.wb["attn_out"], w["attn_w_out"][0], self.wb_cell["attn_out"], D, D, 1)
        ffn(1, 1)


class Bank:
    def __init__(self, ap, cell):
        self.ap = ap
        self.cell = cell


class Pair:
    def __init__(self, ap, c0, c1):
        self.ap = ap
        self.c0 = c0
        self.c1 = c1


def _grid(*dims):
    if len(dims) == 1:
        return [Cell() for _ in range(dims[0])]
    return [_grid(*dims[1:]) for _ in range(dims[0])]


class MK2(MK):
    def setup_psum(self):
        nc = self.nc
        es = self.es
        self.PA = es.enter_context(nc.psum_tensor("PA", [128, 1024], F32))
        self.PB = es.enter_context(nc.psum_tensor("PB", [128, 1024], F32))
        self.PC = es.enter_context(nc.psum_tensor("PC", [128, 1024], F32))
        self.PD0 = es.enter_context(nc.psum_tensor("PD0", [128, 512], F32))
        self.PT = es.enter_context(nc.psum_tensor("PT", [128, 1024], BF16))
        self.pairs = []
        self.banks = []
        for t in (self.PA, self.PB, self.PC):
            c0, c1 = Cell(), Cell()
            self.pairs.append(Pair(t, c0, c1))
            self.banks.append(Bank(t[:, 0:512], c0))
            self.banks.append(Bank(t[:, 512:1024], c1))
        self.bankD = Bank(self.PD0[:, :], Cell())
        self.banks.append(self.bankD)
        self.PT_c = Cell()
        self.rotc = {}

    def rot(self, name, n):
        i = self.rotc.get(name, 0)
        self.rotc[name] = i + 1
        return i % n

    def next_bank(self):
        return self.banks[self.rot("bank", len(self.banks))]

    def next_pair(self):
        p = self.pairs[self.rot("pair", 3)]
        return p

    def tl_alloc(self, es):
        TT, TS = self.TT, self.TS
        sb = lambda n, s, d: self.sb(es, n, s, d)
        self.xT = sb("xT", [128, DC, TT], F32)
        self.xT_c = _grid(DC, TS)
        self.uT = sb("uT", [128, DC, TT], BF16)
        self.uT_c = _grid(DC, TS)
        self.hT = sb("hT", [128, FC, TT], BF16)
        self.hT_c = _grid(FC, TS)
        self.hout = sb("hout", [128, DC, TT], F32)
        self.hout_c = _grid(DC, TS)
        self.sq = sb("sq", [128, DC, 512], BF16)
        self.sq_c = _grid(DC)
        self.NW = 3
        self.wpool = [sb(f"wp{i}", [128, 5632], BF16) for i in range(self.NW)]
        self.wpool_c = _grid(self.NW)
        self.sg = sb("sg", [128, 3, 512], F32)
        self.sg_c = _grid(3)
        self.lnv = sb("lnv", [128, 512], F32)
        self.lnv_c = Cell()
        self.rstd = sb("rstd", [128, 2, 512], F32)
        self.rstd_c = _grid(2)
        self.tmpn = sb("tmpn", [128, 3, 512], F32)
        self.tmpn_c = _grid(3)
        self.xstage = sb("xstage", [128, 2, D], F32)
        self.xstage_c = _grid(2)
        self.zst = sb("zst", [128, 3, 512], BF16)
        self.zst_c = _grid(3)
        self.dst = sb("dst", [128, 2, 64], F32)
        self.dst_c = _grid(2)

    def tsl(self, ts):
        return slice(ts * 512, (ts + 1) * 512)

    def wload(self, W, wcell, KC, col0, pw):
        i = self.rot("w", self.NW)
        buf = self.wpool[i]
        view = buf[:, 0:KC * pw].rearrange("p (k n) -> p k n", k=KC)
        src = W.rearrange("(k p) n -> p k n", p=128)[:, :, col0:col0 + pw]
        self.dma("sp", view, src, reads=[wcell], writes=[self.wpool_c[i]])
        return view, self.wpool_c[i]

    def rms_stats(self, srcs, nfeat):
        C = len(srcs)
        for c, (ap, cl) in enumerate(srcs):
            self.A(lambda e, ap=ap, c=c: e.activation(out=self.sq[:, c, :], in_=ap, func=AF.Square), cl, [self.sq_c[c]])
        bank = self.next_bank()
        for c in range(C):
            self.mm(bank.ap, self.onesB[:], self.sq[:, c, :], c == 0, c == C - 1,
                    [self.onesB_c, self.sq_c[c]], [bank.cell], inc=(c == C - 1))
        j = self.rot("rs", 2)
        self.A(lambda e: e.activation(out=self.lnv[:], in_=bank.ap, func=AF.Ln, scale=1.0 / nfeat, bias=self.epsc[:]),
               [bank.cell, self.eps_c], [self.lnv_c])
        self.A(lambda e: e.activation(out=self.rstd[:, j, :], in_=self.lnv[:], func=AF.Exp, scale=-0.5),
               [self.lnv_c], [self.rstd_c[j]])
        return self.rstd[:, j, :], self.rstd_c[j]

    def prenorm(self, n):
        for ts in range(self.TS):
            sl = self.tsl(ts)
            srcs = [(self.xT[:, c, sl], [self.xT_c[c][ts]]) for c in range(DC)]
            rs, rc = self.rms_stats(srcs, D)
            for c in range(DC):
                self.V(lambda e, c=c: e.scalar_tensor_tensor(out=self.uT[:, c, sl], in0=self.xT[:, c, sl],
                                                             scalar=self.gpre[:, n, c:c + 1], in1=rs,
                                                             op0=ALU.mult, op1=ALU.mult),
                       [self.xT_c[c][ts], self.g_c, rc], [self.uT_c[c][ts]])

    def post_res(self, n, half):
        g = self.gposth if half else self.gpost
        for ts in range(self.TS):
            sl = self.tsl(ts)
            srcs = [(self.hout[:, c, sl], [self.hout_c[c][ts]]) for c in range(DC)]
            rs, rc = self.rms_stats(srcs, D)
            for c in range(DC):
                j = self.rot("tmpn", 3)
                self.V(lambda e, c=c, j=j: e.scalar_tensor_tensor(out=self.tmpn[:, j, :], in0=self.hout[:, c, sl],
                                                                  scalar=g[:, n, c:c + 1], in1=rs,
                                                                  op0=ALU.mult, op1=ALU.mult),
                       [self.hout_c[c][ts], self.g_c, rc], [self.tmpn_c[j]])
                (self.G if c % 2 == 0 else self.V)(lambda e, c=c, j=j: e.tensor_tensor(out=self.xT[:, c, sl], in0=self.xT[:, c, sl],
                                                                                  in1=self.tmpn[:, j, :], op=ALU.add),
                                                   [self.xT_c[c][ts], self.tmpn_c[j]], [self.xT_c[c][ts]])

    def linear_fm(self, src, src_c, KC, W, wcell, col0, ncols, consumer):
        PW = 512 if KC <= 8 else 256
        for pc0 in range(0, ncols, PW):
            pw = min(PW, ncols - pc0)
            wbuf, wc = self.wload(W, wcell, KC, col0 + pc0, pw)
            for ts in range(self.TS):
                for ml in range(pw // 128):
                    m = pc0 // 128 + ml
                    bank = self.next_bank()
                    for kc in range(KC):
                        self.mm(bank.ap, wbuf[:, kc, ml * 128:(ml + 1) * 128], src[:, kc, self.tsl(ts)],
                                kc == 0, kc == KC - 1, [wc, src_c[kc][ts]], [bank.cell], inc=(kc == KC - 1))
                    consumer(m, ts, bank)

    def linear_tm(self, src, src_c, KC, W, wcell, col0, ncols, consumer):
        for pc0 in range(0, ncols, 512):
            pw = min(512, ncols - pc0)
            wbuf, wc = self.wload(W, wcell, KC, col0 + pc0, pw)
            for tb in range(self.TT // 128):
                ts = (tb * 128) // 512
                bank = self.next_bank()
                for kc in range(KC):
                    self.mm(bank.ap[:, 0:pw], src[:, kc, tb * 128:(tb + 1) * 128], wbuf[:, kc, :],
                            kc == 0, kc == KC - 1, [wc, src_c[kc][ts]], [bank.cell], inc=(kc == KC - 1))
                consumer(tb, pc0, pw, bank)

    def ffn(self, li, fi):
        TS = self.TS
        n = li * 3 + (0 if fi == 0 else 2)
        self.prenorm(n)
        Wg, Wu, Wd = self.wb[f"gate{li}{fi}"], self.wb[f"up{li}{fi}"], self.wb[f"down{li}{fi}"]
        cg, cu, cd = self.wb_cell[f"gate{li}{fi}"], self.wb_cell[f"up{li}{fi}"], self.wb_cell[f"down{li}{fi}"]
        for p0 in range(0, FC, 4):
            nf = min(4, FC - p0)
            pw = nf * 128
            gbuf, gc = self.wload(Wg, cg, DC, p0 * 128, pw)
            ubuf, uc = self.wload(Wu, cu, DC, p0 * 128, pw)
            for ts in range(TS):
                sl = self.tsl(ts)
                for fl in range(nf):
                    f = p0 + fl
                    pr = self.next_pair()
                    for kc in range(DC):
                        self.mm(pr.ap[:, 0:512], gbuf[:, kc, fl * 128:(fl + 1) * 128], self.uT[:, kc, sl],
                                kc == 0, kc == DC - 1, [gc, self.uT_c[kc][ts]], [pr.c0], inc=(kc == DC - 1))
                    for kc in range(DC):
                        self.mm(pr.ap[:, 512:1024], ubuf[:, kc, fl * 128:(fl + 1) * 128], self.uT[:, kc, sl],
                                kc == 0, kc == DC - 1, [uc, self.uT_c[kc][ts]], [pr.c1], inc=(kc == DC - 1))
                    j = self.rot("sg", 3)
                    self.A(lambda e, j=j: e.activation(out=self.sg[:, j, :], in_=pr.ap[:, 0:512], func=AF.Silu),
                           [pr.c0], [self.sg_c[j]])
                    self.V(lambda e, j=j, f=f: e.tensor_tensor(out=self.hT[:, f, sl], in0=pr.ap[:, 512:1024],
                                                               in1=self.sg[:, j, :], op=ALU.mult),
                           [pr.c1, self.sg_c[j]], [self.hT_c[f][ts]])

        def cons(m, ts, bank):
            self.V(lambda e: e.tensor_copy(out=self.hout[:, m, self.tsl(ts)], in_=bank.ap),
                   [bank.cell], [self.hout_c[m][ts]])
        self.linear_fm(self.hT, self.hT_c, FC, Wd, cd, 0, D, cons)
        self.post_res(n, half=True)

    def load_x_tile(self, t):
        TT = self.TT
        for tb in range(TT // 128):
            ts = (tb * 128) // 512
            k = self.rot("xs", 2)
            r0 = t * TT + tb * 128
            self.dma("sp", self.xstage[:, k, :], self.x_in[r0:r0 + 128, :], writes=[self.xstage_c[k]])
            for hb in range(2):
                bank = self.next_bank()
                for cl in range(4):
                    c = hb * 4 + cl
                    self.P(lambda e, c=c, cl=cl: e.transpose(bank.ap[:, cl * 128:(cl + 1) * 128],
                                                              self.xstage[:, k, c * 128:(c + 1) * 128], self.identF),
                           [self.xstage_c[k], self.cF_c], [bank.cell], inc=(cl == 3))
                outv = self.xT[:, hb * 4:(hb + 1) * 4, tb * 128:(tb + 1) * 128]
                inv = bank.ap.rearrange("p (c n) -> p c n", c=4)
                wc = [self.xT_c[hb * 4 + cl][ts] for cl in range(4)]
                if hb == 0:
                    self.A(lambda e: e.activation(out=outv, in_=inv, func=AF.Copy), [bank.cell], wc)
                else:
                    self.V(lambda e: e.tensor_copy(out=outv, in_=inv), [bank.cell], wc)

    def store_y_tile(self, t):
        TT = self.TT
        for tb in range(TT // 128):
            ts = (tb * 128) // 512
            k = self.rot("xs", 2)
            for hb in range(2):
                bank = self.next_bank()
                for cl in range(4):
                    c = hb * 4 + cl
                    self.P(lambda e, c=c, cl=cl: e.transpose(bank.ap[:, cl * 128:(cl + 1) * 128],
                                                              self.xT[:, c, tb * 128:(tb + 1) * 128], self.identF),
                           [self.xT_c[c][ts], self.cF_c], [bank.cell], inc=(cl == 3))
                if hb == 0:
                    self.A(lambda e: e.activation(out=self.xstage[:, k, 0:512], in_=bank.ap, func=AF.Copy),
                           [bank.cell], [self.xstage_c[k]])
                else:
                    self.V(lambda e: e.tensor_copy(out=self.xstage[:, k, 512:1024], in_=bank.ap),
                           [bank.cell], [self.xstage_c[k]])
            r0 = t * TT + tb * 128
            self.dma("pool", self.y_out[r0:r0 + 128, :], self.xstage[:, k, :], reads=[self.xstage_c[k]])

    def all_xT_cells(self):
        return cells_of(self.xT_c)

    def store_xres(self, t):
        self.dma("pool", self.xres[t].rearrange("p (c n) -> p c n", c=DC), self.xT[:],
                 reads=self.all_xT_cells(), writes=[self.xres_c[t]])

    def load_xres(self, t):
        self.dma("sp", self.xT[:], self.xres[t].rearrange("p (c n) -> p c n", c=DC),
                 reads=[self.xres_c[t]], writes=self.all_xT_cells())

    def stage_A(self, es):
        TT = self.TT
        for t in range(self.NT):
            tok0 = t * TT
            self.load_x_tile(t)
            self.ffn(0, 0)
            self.prenorm(1)
            W, wc = self.wb["ssm_in"], self.wb_cell["ssm_in"]

            def cons_xbc(m, ts, bank):
                j = self.rot("zst", 3)
                self.V(lambda e: e.tensor_copy(out=self.zst[:, j, :], in_=bank.ap), [bank.cell], [self.zst_c[j]])
                a = tok0 + ts * 512
                self.dma("pool", self.xbc[m][:, a:a + 512], self.zst[:, j, :], reads=[self.zst_c[j]], accum=[self.ssm_c[t]])
            self.linear_fm(self.uT, self.uT_c, DC, W, wc, DIN, CONVD, cons_xbc)

            def cons_z(tb, pc0, pw, bank):
                j = self.rot("zst", 3)
                self.A(lambda e: e.activation(out=self.zst[:, j, 0:pw], in_=bank.ap[:, 0:pw], func=AF.Silu), [bank.cell], [self.zst_c[j]])
                r0 = tok0 + tb * 128
                self.dma("pool", self.zsc[r0:r0 + 128, pc0:pc0 + pw], self.zst[:, j, 0:pw], reads=[self.zst_c[j]], accum=[self.ssm_c[t]])
            self.linear_tm(self.uT, self.uT_c, DC, W, wc, 0, DIN, cons_z)

            def cons_dt(tb, pc0, pw, bank):
                j = self.rot("dst", 2)
                self.V(lambda e: e.tensor_copy(out=self.dst[:, j, :], in_=bank.ap[:, 0:64]), [bank.cell], [self.dst_c[j]])
                r0 = tok0 + tb * 128
                self.dma("pool", self.dtr[r0:r0 + 128, :], self.dst[:, j, :], reads=[self.dst_c[j]], accum=[self.ssm_c[t]])
            self.linear_tm(self.uT, self.uT_c, DC, W, wc, DIN + CONVD, 64, cons_dt)
            self.store_xres(t)

    def mixer_out(self, t, src_dram, src_cell, KC, wkey, n):
        TT = self.TT
        tok0 = t * TT
        u = tok0 // self.UL
        view = self.hT[:, 0:KC, :]
        self.dma("sp", view, src_dram.rearrange("c p n -> p c n")[:, :, tok0:tok0 + TT],
                 reads=[src_cell[u]], writes=[self.hT_c[c] for c in range(KC)])

        def cons(m, ts, bank):
            self.V(lambda e: e.tensor_copy(out=self.hout[:, m, self.tsl(ts)], in_=bank.ap),
                   [bank.cell], [self.hout_c[m][ts]])
        self.linear_fm(self.hT, self.hT_c, KC, self.wb[wkey], self.wb_cell[wkey], 0, D, cons)
        self.post_res(n, half=False)

    def stage_B(self, es):
        TT = self.TT
        for t in range(self.NT):
            tok0 = t * TT
            self.load_xres(t)
            self.mixer_out(t, self.yT, self.yT_c, 16, "ssm_out", 1)
            self.ffn(0, 1)
            self.ffn(1, 0)
            self.prenorm(4)
            W, wc = self.wb["qkv"], self.wb_cell["qkv"]

            def cons_qk(m, ts, bank):
                j = self.rot("zst", 3)
                self.A(lambda e: e.activation(out=self.zst[:, j, :], in_=bank.ap, func=AF.Copy), [bank.cell], [self.zst_c[j]])
                a = tok0 + ts * 512
                dstT = self.qT[m] if m < 8 else self.kT[m - 8]
                self.dma("pool", dstT[:, a:a + 512], self.zst[:, j, :], reads=[self.zst_c[j]], accum=[self.qkv_c[t]])
            self.linear_fm(self.uT, self.uT_c, DC, W, wc, 0, 2048, cons_qk)

            def cons_v(tb, pc0, pw, bank):
                j = self.rot("zst", 3)
                self.V(lambda e: e.tensor_copy(out=self.zst[:, j, 0:pw], in_=bank.ap[:, 0:pw]), [bank.cell], [self.zst_c[j]])
                r0 = tok0 + tb * 128
                self.dma("pool", self.vtm[r0:r0 + 128, pc0:pc0 + pw], self.zst[:, j, 0:pw], reads=[self.zst_c[j]], accum=[self.qkv_c[t]])
            self.linear_tm(self.uT, self.uT_c, DC, W, wc, 2048, 1024, cons_v)
            self.store_xres(t)

    def stage_C(self, es):
        for t in range(self.NT):
            self.load_xres(t)
            self.mixer_out(t, self.oT, self.oT_c, 8, "attn_out", 4)
            self.ffn(1, 1)
            self.store_y_tile(t)


class MK3(MK2):
    def seq_groups(self):
        sgs = [[0, 1]] if self.NU >= 2 else [[0]]
        sgs += [[u] for u in range(2, self.NU)]
        return sgs

    def stage_ssd(self, es):
        UL, NCH = self.UL, self.NCH
        NCm = 2 * NCH
        CSEG = min(1024, UL)
        NBS = CSEG // 128
        sb = lambda n, s, d: self.sb(es, n, s, d)
        w = self.w_in
        a_bc = sb("a_bc", [128, 64], F32)
        dtb_bc = sb("dtb_bc", [128, 64], F32)
        D_bc = sb("D_bc", [128, 32], F32)
        convw = sb("convw", [128, 5, 24], F32)
        convb = sb("convb", [128, 24], F32)
        ng_bc = sb("ng_bc", [128, 512], F32)
        ng_c = Cell()
        mask4 = sb("mask4", [128, 2, 512], BF16)
        diagW = sb("diagW", [128, 6, 5, 128], BF16)
        sc = Cell()
        dg_c = Cell()
        self.dma("sp", a_bc[:], w["ssm_a_log"][0].rearrange("d h -> (d h)").partition_broadcast(128), writes=[sc])
        self.dma("sp", dtb_bc[:], w["ssm_dt_bias"][0].rearrange("d h -> (d h)").partition_broadcast(128), writes=[sc])
        self.dma("sp", D_bc[:], w["ssm_d"][0].partition_broadcast(128), writes=[sc])
        for k5 in range(5):
            self.dma("sp", convw[:, k5, :], w["ssm_conv_w"][0][k5].rearrange("(c p) -> p c", p=128), writes=[sc], allow_slow_non_contiguous=True)
        self.dma("sp", convb[:], w["ssm_conv_b"][0].rearrange("(c p) -> p c", p=128), writes=[sc], allow_slow_non_contiguous=True)
        self.A(lambda e: e.activation(out=a_bc[:], in_=a_bc[:], func=AF.Exp), [sc], [sc])
        self.V(lambda e: e.tensor_scalar(out=a_bc[:], in0=a_bc[:], scalar1=-1.0, scalar2=None, op0=ALU.mult), [sc], [sc])
        for d, mk_ in enumerate((self.maskfB, self.maskbB)):
            self.V(lambda e, d=d, mk_=mk_: e.tensor_copy(out=mask4[:, d, :].rearrange("p (h i) -> p h i", h=4),
                                                       in_=mk_.unsqueeze(1).broadcast_to([128, 4, 128])), [self.cB_c, sc], [sc])
        xg = sb("xg", [128, NCm, 512], BF16); xg_c = _grid(NCm)
        Bg = sb("Bg", [128, NCm, 128], BF16); Bg_c = _grid(NCm)
        BT = sb("BT", [128, NCm * 128], BF16); BT_c = _grid(NCm)
        CT = sb("CT", [128, NCm * 128], BF16); CT_c = _grid(NCm)
        hpb = sb("hpb", [128, NCm, 512], BF16); hpb_c = _grid(NCm)
        mkdt = lambda n, d_: sb(n, [128, 2, NCm, 8], d_)
        dtraw = mkdt("dtraw", F32); tA = mkdt("tA", F32); tB = dtraw
        dtv = mkdt("dtv", F32); dav = mkdt("dav", F32); csv = mkdt("csv", F32); csl = mkdt("csl", F32)
        Ef = mkdt("Ef", F32); Dec = mkdt("Dec", F32); CDt = mkdt("CDt", F32)
        da16 = mkdt("da16", BF16); nda16 = mkdt("nda16", BF16)
        dts_c = Cell()
        xin = sb("xin", [128, 2, CSEG + 4], BF16); xin_c = _grid(2)
        cvo = sb("cvo", [128, 2, CSEG], BF16); cvo_c = _grid(2)
        cbT = sb("cbT", [128, 3, 128], BF16); cbT_c = _grid(3)
        ex = sb("ex", [128, 8, 512], BF16); ex_c = _grid(8)
        WT = sb("WT", [128, 8, 512], BF16); WT_c = _grid(8)
        xdt = sb("xdt", [128, 3, 2, 512], BF16); xdt_c = _grid(3)
        xdec = sb("xdec", [128, 2, 512], BF16); xdec_c = _grid(2)
        xdecm = sb("xdecm", [128, 3, 512], BF16); xdecm_c = _grid(3)
        Hf = sb("Hf", [128, 512], F32); Hf_c = Cell()
        Hb = sb("Hb", [128, 512], F32); Hb_c = Cell()
        Hf16 = sb("Hf16", [128, 512], BF16); Hf16_c = Cell()
        tH = sb("tH", [128, 2, 512], F32); tH_c = _grid(2)
        t1 = sb("t1", [128, 2, 512], F32); t1_c = _grid(2)
        t2 = sb("t2", [128, 2, 512], F32); t2_c = _grid(2)
        t3 = sb("t3", [128, 5, 512], F32); t3_c = _grid(5)
        ysb = sb("ysb", [128, 3, 512], F32); ysb_c = _grid(3)
        zs = sb("zs", [128, 3, 512], BF16); zs_c = _grid(3)
        yz = sb("yz", [128, 2, 512], F32); yz_c = _grid(2)
        ssq = sb("ssq", [128, 3, 2], F32); ssq_c = _grid(3)
        yn = sb("yn", [128, 2, 512], BF16); yn_c = _grid(2)
        yst = sb("yst", [128, 2, 512], BF16); yst_c = _grid(2)
        _ptc = Cell()
        PTh = [_ptc, _ptc]

        def bc8(ap8):
            return ap8.unsqueeze(2).broadcast_to([128, 8, 64])

        def v3(ap512):
            return ap512.rearrange("p (h d) -> p h d", h=8)

        TTn = self.TT
        for g in range(NG):
            chs = [4 * g, 4 * g + 1, 4 * g + 2, 4 * g + 3, 16 + g, 20 + g]
            self.dma("sp", ng_bc[:], w["ssm_norm"][0][g * 512:(g + 1) * 512].partition_broadcast(128), writes=[ng_c])
            for cl, ch in enumerate(chs):
                for k5 in range(5):
                    self.G(lambda e, cl=cl, ch=ch, k5=k5: e.tensor_scalar(out=diagW[:, cl, k5, :], in0=self.identF, scalar1=convw[:, k5, ch:ch + 1],
                                                                          scalar2=None, op0=ALU.mult), [self.cF_c, sc], [dg_c])
            for sg in self.seq_groups():
                NC = len(sg) * NCH
                tok0 = sg[0] * UL
                pair = len(sg) == 2
                sgcells = [self.ssm_c[(tok0 // TTn) + i] for i in range(NC * 128 // TTn)]
                for d in range(2):
                    src = self.dtr[tok0:tok0 + NC * 128, d * 32 + 8 * g:d * 32 + 8 * g + 8].rearrange("(c p) h -> p c h", p=128)
                    self.dma("sp", dtraw[:, d, 0:NC, :], src, reads=sgcells, writes=[dts_c])
                S_ = lambda t_: t_[:, :, 0:NC, :]
                bsl = dtb_bc[:].rearrange("p (d h) -> p d h", d=2)[:, :, 8 * g:8 * g + 8].unsqueeze(2).broadcast_to([128, 2, NC, 8])
                asl = a_bc[:].rearrange("p (d h) -> p d h", d=2)[:, :, 8 * g:8 * g + 8].unsqueeze(2).broadcast_to([128, 2, NC, 8])
                self.V(lambda e: e.tensor_tensor(out=S_(tA), in0=S_(dtraw), in1=bsl, op=ALU.add), [dts_c, sc], [dts_c])
                self.A(lambda e: e.activation(out=S_(tB), in_=S_(tA), func=AF.Abs), [dts_c], [dts_c])
                self.A(lambda e: e.activation(out=S_(tB), in_=S_(tB), func=AF.Exp, scale=-1.0), [dts_c], [dts_c])
                self.A(lambda e: e.activation(out=S_(tB), in_=S_(tB), func=AF.Ln, bias=1.0), [dts_c], [dts_c])
                self.V(lambda e: e.scalar_tensor_tensor(out=S_(dtv), in0=S_(tA), scalar=0.0, in1=S_(tB), op0=ALU.max, op1=ALU.add), [dts_c], [dts_c])
                self.V(lambda e: e.tensor_tensor(out=S_(dav), in0=S_(dtv), in1=asl, op=ALU.mult), [dts_c, sc], [dts_c])
                self.V(lambda e: e.tensor_copy(out=S_(da16), in_=S_(dav)), [dts_c], [dts_c])
                self.V(lambda e: e.tensor_scalar(out=S_(nda16), in0=S_(dav), scalar1=-1.0, scalar2=None, op0=ALU.mult), [dts_c], [dts_c])
                bk1 = self.next_bank()
                bk2 = self.next_bank()
                for c in range(NC):
                    for d in range(2):
                        tri = self.trifB if d == 0 else self.tribB
                        o = (d * NC + c) * 8
                        self.mm(bk1.ap[:, o:o + 8], tri, da16[:, d, c, :], True, True, [self.cB_c, dts_c], [bk1.cell], inc=False)
                        self.mm(bk2.ap[:, o:o + 8], self.onesB[:], da16[:, d, c, :], True, True, [self.onesB_c, dts_c], [bk1.cell, bk2.cell], inc=(c == NC - 1 and d == 1))
                vb = lambda bk: bk.ap[:, 0:2 * NC * 8].rearrange("p (d c h) -> p d c h", d=2, c=NC)
                self.V(lambda e: e.tensor_copy(out=S_(csv), in_=vb(bk1)), [bk1.cell], [dts_c])
                self.V(lambda e: e.tensor_copy(out=S_(csl), in_=vb(bk2)), [bk2.cell], [dts_c])
                self.A(lambda e: e.activation(out=S_(Ef), in_=S_(csv), func=AF.Exp), [dts_c], [dts_c])
                self.A(lambda e: e.activation(out=S_(CDt), in_=S_(csl), func=AF.Exp), [dts_c], [dts_c])
                self.V(lambda e: e.tensor_tensor(out=S_(tA), in0=S_(csl), in1=S_(csv), op=ALU.subtract), [dts_c], [dts_c])
                self.A(lambda e: e.activation(out=S_(tA), in_=S_(tA), func=AF.Exp), [dts_c], [dts_c])
                self.V(lambda e: e.tensor_tensor(out=S_(Dec), in0=S_(tA), in1=S_(dtv), op=ALU.mult), [dts_c], [dts_c])
                for ui, u in enumerate(sg):
                    utok = u * UL
                    co = ui * NCH
                    nseg = UL // CSEG
                    for seg in range(nseg):
                        s0 = utok + seg * CSEG
                        cb0 = co + seg * NBS
                        lh = "real" if seg > 0 else ("flag" if (pair and ui == 1) else "zero")
                        rh = "real" if seg < nseg - 1 else ("flag" if (pair and ui == 0) else "zero")
                        for cl, ch in enumerate(chs):
                            k = self.rot("xin", 2)
                            a0 = s0 - (0 if lh == "zero" else 2)
                            a1 = s0 + CSEG + (0 if rh == "zero" else 2)
                            o0 = 0 if lh != "zero" else 2
                            self.dma("sp", xin[:, k, o0:o0 + (a1 - a0)], self.xbc[ch][:, a0:a1], reads=sgcells, writes=[xin_c[k]])
                            if lh == "zero":
                                self.G(lambda e: e.memset(xin[:, k, 0:2], 0.0), [], [xin_c[k]])
                            elif lh == "flag":
                                self.G(lambda e: e.tensor_scalar(out=xin[:, k, 0:2], in0=xin[:, k, 0:2], scalar1=self.flagc[:, 0:1], scalar2=None, op0=ALU.mult),
                                       [xin_c[k], self.flag_c], [xin_c[k]])
                            if rh == "zero":
                                self.G(lambda e: e.memset(xin[:, k, CSEG + 2:CSEG + 4], 0.0), [], [xin_c[k]])
                            elif rh == "flag":
                                self.G(lambda e: e.tensor_scalar(out=xin[:, k, CSEG + 2:CSEG + 4], in0=xin[:, k, CSEG + 2:CSEG + 4], scalar1=self.flagc[:, 0:1],
                                                                 scalar2=None, op0=ALU.mult), [xin_c[k], self.flag_c], [xin_c[k]])
                            if ch >= 16:
                                dstT, dst_cells = (BT, BT_c) if ch < 20 else (CT, CT_c)
                            else:
                                kk = self.rot("cvo", 2)
                            for b5 in range(CSEG // 512):
                                bk = self.next_bank()
                                for k5 in range(5):
                                    self.mm(bk.ap, diagW[:, cl, k5, :], xin[:, k, b5 * 512 + k5:b5 * 512 + k5 + 512], k5 == 0, k5 == 4,
                                            [dg_c, xin_c[k]], [bk.cell], inc=(k5 == 4))
                                if ch >= 16:
                                    c4 = cb0 + b5 * 4
                                    self.A(lambda e: e.activation(out=dstT[:, c4 * 128:c4 * 128 + 512], in_=bk.ap, func=AF.Silu, bias=convb[:, ch:ch + 1]),
                                           [bk.cell, sc], dst_cells[c4:c4 + 4])
                                else:
                                    self.A(lambda e: e.activation(out=cvo[:, kk, b5 * 512:(b5 + 1) * 512], in_=bk.ap, func=AF.Silu, bias=convb[:, ch:ch + 1]),
                                           [bk.cell, sc], [cvo_c[kk]])
                            if 16 <= ch < 20:
                                for b in range(NBS):
                                    self.P(lambda e, b=b: e.transpose(self.PT[:, b * 128:(b + 1) * 128], BT[:, (cb0 + b) * 128:(cb0 + b + 1) * 128], self.identB),
                                           [BT_c[cb0 + b], self.cB_c], PTh, inc=(b == NBS - 1))
                                self.V(lambda e: e.tensor_copy(out=Bg[:, cb0:cb0 + NBS, :], in_=self.PT[:, 0:NBS * 128].rearrange("p (b n) -> p b n", b=NBS)),
                                       PTh, Bg_c[cb0:cb0 + NBS])
                            elif ch < 16:
                                xi = ch - 4 * g
                                for b in range(NBS):
                                    self.P(lambda e, b=b: e.transpose(self.PT[:, b * 128:(b + 1) * 128], cvo[:, kk, b * 128:(b + 1) * 128], self.identB),
                                           [cvo_c[kk], self.cB_c], PTh, inc=(b == NBS - 1))
                                self.V(lambda e: e.tensor_copy(out=xg[:, cb0:cb0 + NBS, xi * 128:(xi + 1) * 128], in_=self.PT[:, 0:NBS * 128].rearrange("p (b n) -> p b n", b=NBS)),
                                       PTh, xg_c[cb0:cb0 + NBS])
                self.V(lambda e: e.memset(Hb[:], 0.0), [], [Hb_c])

                def pre_states(c):
                    k = self.rot("xdec", 2)
                    self.G(lambda e: e.tensor_tensor(out=v3(xdec[:, k, :]), in0=v3(xg[:, c, :]), in1=bc8(Dec[:, 1, c, :]), op=ALU.mult),
                           [xg_c[c], dts_c], [xdec_c[k]])
                    bk = self.next_bank()
                    self.mm(bk.ap, Bg[:, c, :], xdec[:, k, :], True, True, [Bg_c[c], xdec_c[k]], [bk.cell])
                    return bk

                def pre_rec(c, bk):
                    if pair and c == NCH - 1:
                        self.V(lambda e: e.tensor_scalar(out=Hb[:], in0=Hb[:], scalar1=self.flagc[:, 0:1], scalar2=None, op0=ALU.mult), [Hb_c, self.flag_c], [Hb_c])
                    self.A(lambda e: e.activation(out=hpb[:, c, :], in_=Hb[:], func=AF.Copy), [Hb_c], [hpb_c[c]])
                    j = self.rot("tH", 2)
                    self.V(lambda e: e.tensor_tensor(out=v3(tH[:, j, :]), in0=v3(Hb[:]), in1=bc8(CDt[:, 1, c, :]), op=ALU.mult), [Hb_c, dts_c], [tH_c[j]])
                    self.V(lambda e: e.tensor_tensor(out=Hb[:], in0=bk.ap, in1=tH[:, j, :], op=ALU.add), [bk.cell, tH_c[j]], [Hb_c])
                prev = None
                for c in range(NC - 1, -1, -1):
                    bk = pre_states(c)
                    if prev is not None:
                        pre_rec(*prev)
                    prev = (c, bk)
                pre_rec(*prev)
                self.V(lambda e: e.memset(Hf[:], 0.0), [], [Hf_c])
                self.V(lambda e: e.memset(Hf16[:], 0.0), [], [Hf16_c])
                stt = {}

                def Sa(c):
                    tc = slice(c * 128, (c + 1) * 128)
                    d_ = stt[c] = {}
                    bcb = self.next_bank()
                    self.mm(bcb.ap[:, 0:128], BT[:, tc], CT[:, tc], True, True, [BT_c[c], CT_c[c]], [bcb.cell])
                    kc_ = d_["cb"] = self.rot("cbT", 3)
                    self.A(lambda e: e.activation(out=cbT[:, kc_, :], in_=bcb.ap[:, 0:128], func=AF.Copy), [bcb.cell], [cbT_c[kc_]])
                    kx = d_["xdt"] = self.rot("xdt", 3)
                    self.V(lambda e: e.tensor_tensor(out=v3(xdt[:, kx, 0, :]), in0=v3(xg[:, c, :]), in1=bc8(dtv[:, 0, c, :]), op=ALU.mult),
                           [xg_c[c], dts_c], [xdt_c[kx]])
                    self.G(lambda e: e.tensor_tensor(out=v3(xdt[:, kx, 1, :]), in0=v3(xg[:, c, :]), in1=bc8(dtv[:, 1, c, :]), op=ALU.mult),
                           [xg_c[c], dts_c], [xdt_c[kx]])
                    kd = d_["xdec"] = self.rot("xdecm", 3)
                    self.G(lambda e: e.tensor_tensor(out=v3(xdecm[:, kd, :]), in0=v3(xg[:, c, :]), in1=bc8(Dec[:, 0, c, :]), op=ALU.mult), [xg_c[c], dts_c], [xdecm_c[kd]])
                    k3 = d_["t3"] = self.rot("t3", 5)
                    self.G(lambda e: e.tensor_tensor(out=v3(t3[:, k3, :]), in0=v3(xg[:, c, :]), in1=bc8(D_bc[:, 8 * g:8 * g + 8]), op=ALU.mult), [xg_c[c], sc], [t3_c[k3]])
                    d_["ex"] = []
                    for bi in range(4):
                        d = bi // 2
                        h0 = (bi % 2) * 4
                        tri = self.trifB if d == 0 else self.tribB
                        bs = self.next_bank()
                        self.mm(bs.ap, self.identB, mask4[:, d, :], True, False, [self.cB_c, sc], [bs.cell], inc=False)
                        self.mm(bs.ap, tri, nda16[:, d, c, h0:h0 + 4].unsqueeze(2).broadcast_to([128, 4, 128]), False, False, [self.cB_c, dts_c], [bs.cell], inc=False)
                        for hl in range(4):
                            self.mm(bs.ap[:, hl * 128:(hl + 1) * 128], da16[:, d, c, h0 + hl:h0 + hl + 1].broadcast_to([128, 128]), tri, False, hl == 3,
                                    [self.cB_c, dts_c], [bs.cell], inc=(hl == 3))
                        ke = self.rot("ex", 8)
                        self.A(lambda e: e.activation(out=ex[:, ke, :], in_=bs.ap, func=AF.Exp), [bs.cell], [ex_c[ke]])
                        d_["ex"].append(ke)

                def Sb(c):
                    d_ = stt[c]
                    d_["wt"] = []
                    for bi in range(4):
                        ke = d_["ex"][bi]
                        kw_ = self.rot("WT", 8)
                        self.V(lambda e: e.tensor_tensor(out=WT[:, kw_, :].rearrange("p (h i) -> p h i", h=4), in0=ex[:, ke, :].rearrange("p (h i) -> p h i", h=4),
                                                         in1=cbT[:, d_["cb"], :].unsqueeze(1).broadcast_to([128, 4, 128]), op=ALU.mult), [ex_c[ke], cbT_c[d_["cb"]]], [WT_c[kw_]])
                        d_["wt"].append(kw_)

                def Sc(c):
                    d_ = stt[c]
                    tc = slice(c * 128, (c + 1) * 128)
                    kx, kd, wts = d_["xdt"], d_["xdec"], d_["wt"]
                    if pair and c == NCH:
                        self.V(lambda e: e.tensor_scalar(out=Hf[:], in0=Hf[:], scalar1=self.flagc[:, 0:1], scalar2=None, op0=ALU.mult), [Hf_c, self.flag_c], [Hf_c])
                        self.A(lambda e: e.activation(out=Hf16[:], in_=Hf[:], func=AF.Copy), [Hf_c], [Hf16_c])
                    bof = self.next_bank()
                    self.mm(bof.ap, CT[:, tc], Hf16[:], True, True, [CT_c[c], Hf16_c], [bof.cell])
                    bst = self.next_bank()
                    self.mm(bst.ap, Bg[:, c, :], xdecm[:, kd, :], True, True, [Bg_c[c], xdecm_c[kd]], [bst.cell])
                    j = self.rot("tH", 2)
                    self.V(lambda e: e.tensor_tensor(out=v3(tH[:, j, :]), in0=v3(Hf[:]), in1=bc8(CDt[:, 0, c, :]), op=ALU.mult), [Hf_c, dts_c], [tH_c[j]])
                    self.V(lambda e: e.tensor_tensor(out=Hf[:], in0=bst.ap, in1=tH[:, j, :], op=ALU.add), [bst.cell, tH_c[j]], [Hf_c])
                    self.A(lambda e: e.activation(out=Hf16[:], in_=Hf[:], func=AF.Copy), [Hf_c], [Hf16_c])
                    bob = self.next_bank()
                    self.mm(bob.ap, CT[:, tc], hpb[:, c, :], True, True, [CT_c[c], hpb_c[c]], [bob.cell])
                    by = self.next_bank()
                    for h in range(8):
                        for d in range(2):
                            kw_ = wts[d * 2 + h // 4]
                            hl = h % 4
                            self.mm(by.ap[:, h * 64:(h + 1) * 64], WT[:, kw_, hl * 128:(hl + 1) * 128], xdt[:, kx, d, h * 64:(h + 1) * 64], d == 0, d == 1,
                                    [WT_c[kw_], xdt_c[kx]], [by.cell], inc=(h == 7 and d == 1))
                    kt = d_["t1"] = self.rot("t1", 2)
                    self.V(lambda e: e.tensor_tensor(out=v3(t1[:, kt, :]), in0=v3(bof.ap), in1=bc8(Ef[:, 0, c, :]), op=ALU.mult), [bof.cell, dts_c], [t1_c[kt]])
                    self.V(lambda e: e.tensor_tensor(out=v3(t2[:, kt, :]), in0=v3(bob.ap), in1=bc8(Ef[:, 1, c, :]), op=ALU.mult), [bob.cell, dts_c], [t2_c[kt]])
                    ky = d_["ysb"] = self.rot("ysb", 3)
                    self.A(lambda e: e.activation(out=ysb[:, ky, :], in_=by.ap, func=AF.Copy), [by.cell], [ysb_c[ky]])
                    kz = d_["zs"] = self.rot("zs", 3)
                    r0 = tok0 + c * 128
                    self.dma("sp", zs[:, kz, :], self.zsc[r0:r0 + 128, g * 512:(g + 1) * 512], reads=[self.ssm_c[r0 // TTn]], writes=[zs_c[kz]])

                def Sd(c):
                    d_ = stt[c]
                    kt, k3 = d_["t1"], d_["t3"]
                    self.G(lambda e: e.tensor_tensor(out=t1[:, kt, :], in0=t1[:, kt, :], in1=t2[:, kt, :], op=ALU.add), [t1_c[kt], t2_c[kt]], [t1_c[kt]])
                    self.G(lambda e: e.tensor_tensor(out=t3[:, k3, :], in0=t3[:, k3, :], in1=t1[:, kt, :], op=ALU.add), [t1_c[kt], t3_c[k3]], [t3_c[k3]])

                def Se(c):
                    d_ = stt[c]
                    ky, k3, kz = d_["ysb"], d_["t3"], d_["zs"]
                    self.V(lambda e: e.tensor_tensor(out=ysb[:, ky, :], in0=ysb[:, ky, :], in1=t3[:, k3, :], op=ALU.add), [ysb_c[ky], t3_c[k3]], [ysb_c[ky]])
                    kq = d_["yz"] = self.rot("yz", 2)
                    self.V(lambda e: e.tensor_tensor(out=yz[:, kq, :], in0=ysb[:, ky, :], in1=zs[:, kz, :], op=ALU.mult), [ysb_c[ky], zs_c[kz]], [yz_c[kq]])
                    kss = d_["ssq"] = self.rot("ssq", 3)
                    self.A(lambda e: e.activation(out=ysb[:, ky, :], in_=yz[:, kq, :], func=AF.Square, accum_out=ssq[:, kss, 0:1]), [yz_c[kq]], [ysb_c[ky], ssq_c[kss]])

                def Sf(c):
                    d_ = stt[c]
                    kq, kss = d_["yz"], d_["ssq"]
                    self.A(lambda e: e.activation(out=ssq[:, kss, 1:2], in_=ssq[:, kss, 0:1], func=AF.Ln, scale=1.0 / 512, bias=self.epsc[:]), [ssq_c[kss], self.eps_c], [ssq_c[kss]])
                    self.A(lambda e: e.activation(out=ssq[:, kss, 1:2], in_=ssq[:, kss, 1:2], func=AF.Exp, scale=-0.5), [ssq_c[kss]], [ssq_c[kss]])
                    kn = d_["yn"] = self.rot("yn", 2)
                    self.V(lambda e: e.scalar_tensor_tensor(out=yn[:, kn, :], in0=yz[:, kq, :], scalar=ssq[:, kss, 1:2], in1=ng_bc[:],
                                                            op0=ALU.mult, op1=ALU.mult), [yz_c[kq], ssq_c[kss], ng_c], [yn_c[kn]])

                def Sg(c):
                    d_ = stt.pop(c)
                    kn = d_["yn"]
                    for k4 in range(4):
                        self.P(lambda e, k4=k4: e.transpose(self.PT[:, k4 * 128:(k4 + 1) * 128], yn[:, kn, k4 * 128:(k4 + 1) * 128], self.identB),
                               [yn_c[kn], self.cB_c], [_ptc], inc=(k4 == 3))
                    ks = self.rot("yst", 2)
                    self.A(lambda e: e.activation(out=yst[:, ks, :], in_=self.PT[:, 0:512], func=AF.Copy), [_ptc], [yst_c[ks]])
                    r0 = tok0 + c * 128
                    u = r0 // UL
                    self.dma("pool", self.yT[4 * g:4 * g + 4].rearrange("k p n -> p k n")[:, :, r0:r0 + 128], yst[:, ks, :].rearrange("p (k n) -> p k n", k=4),
                             reads=[yst_c[ks]], accum=[self.yT_c[u]])

                order = [(Sc, 2), (Sa, 0), (Sb, 1), (Sd, 3), (Se, 4), (Sf, 5), (Sg, 6)]
                for i in range(NC + 6):
                    for fn, lag in order:
                        c = i - lag
                        if 0 <= c < NC:
                            fn(c)


class MK4(MK3):
    def stage_attn(self, es):
        UL = self.UL
        Sm = 2 * UL if self.NU >= 2 else UL
        NBm = Sm // 128
        sb = lambda n, s, d: self.sb(es, n, s, d)
        w = self.w_in
        sc = Cell()
        relb = sb("relb", [128, NBUCK * 8], F32)
        self.dma("sp", relb[:], w["rel_bias"].rearrange("b h -> (b h)").partition_broadcast(128), writes=[sc])
        cLx = sb("cLx", [128, 8], F32)
        cRx = sb("cRx", [128, 8], F32)
        self.V(lambda e: e.tensor_scalar(out=cLx[:], in0=relb[:, 15 * 8:16 * 8], scalar1=self.maskc[:, 0:1], scalar2=None, op0=ALU.add), [sc, self.flag_c], [sc])
        self.V(lambda e: e.tensor_scalar(out=cRx[:], in0=relb[:, 31 * 8:32 * 8], scalar1=self.maskc[:, 0:1], scalar2=None, op0=ALU.add), [sc, self.flag_c], [sc])
        zeroc = sb("zeroc", [128, 1], F32)
        self.V(lambda e: e.memset(zeroc[:], 0.0), [], [sc])
        lamb = sb("lamb", [128, 4, 64], F32)
        self.dma("sp", lamb[:], w["attn_lambda"][0].rearrange("a d -> (a d)").partition_broadcast(128), writes=[sc])
        lt = sb("lt", [128, 2, 64], F32)
        ls = sb("ls", [128, 4], F32)
        self.V(lambda e: e.tensor_tensor(out=lt[:, 0, :], in0=lamb[:, 0, :], in1=lamb[:, 1, :], op=ALU.mult), [sc], [sc])
        self.V(lambda e: e.tensor_tensor(out=lt[:, 1, :], in0=lamb[:, 2, :], in1=lamb[:, 3, :], op=ALU.mult), [sc], [sc])
        self.V(lambda e: e.reduce_sum(out=ls[:, 0:2], in_=lt[:], axis=mybir.AxisListType.X), [sc], [sc])
        self.A(lambda e: e.activation(out=ls[:, 0:2], in_=ls[:, 0:2], func=AF.Exp), [sc], [sc])
        self.V(lambda e: e.tensor_tensor(out=ls[:, 2:3], in0=ls[:, 1:2], in1=ls[:, 0:1], op=ALU.subtract), [sc], [sc])
        self.V(lambda e: e.tensor_scalar(out=ls[:, 3:4], in0=ls[:, 2:3], scalar1=-LAMBDA_INIT, scalar2=None, op0=ALU.add), [sc], [sc])
        neglam = ls[:, 3:4]
        gsub = sb("gsub", [128, 128], F32)
        self.dma("sp", gsub[:], w["attn_subln"][0].partition_broadcast(128), writes=[sc])
        self.V(lambda e: e.tensor_scalar(out=gsub[:], in0=gsub[:], scalar1=1.0 - LAMBDA_INIT, scalar2=None, op0=ALU.mult), [sc], [sc])
        tab = sb("tab", [NBUCK, 8], F32)
        ohs = sb("ohs", [NBUCK, FV], F32)
        fsb = sb("fsb", [8, FV], F32)
        self.dma("sp", tab[:], w["rel_bias"][:, :], writes=[sc])
        self.dma("sp", ohs[:], self.oh_in[:, :], writes=[sc])
        for i0 in range(0, FV, 512):
            n = min(512, FV - i0)
            bk = self.next_bank()
            self.mm(bk.ap[0:8, 0:n], tab[:], ohs[:, i0:i0 + n], True, True, [sc], [bk.cell])
            self.V(lambda e: e.tensor_copy(out=fsb[:, i0:i0 + n], in_=bk.ap[0:8, 0:n]), [bk.cell], [sc])
        self.dma("pool", self.fvec[:, :], fsb[:], reads=[sc], writes=[self.fvec_c])
        QT = sb("QT", [128, 2, Sm], BF16); QT_c = _grid(2)
        KT = sb("KT", [128, 2, Sm], BF16); KT_c = _grid(2)
        Vh = sb("Vh", [128, 2, NBm, 129], BF16); Vh_c = _grid(2)
        self.V(lambda e: e.memset(Vh[:, :, :, 128:129], 1.0), [], Vh_c)
        hk = sb("hk", [128, 2, 9 * 128], F32); hk_c = _grid(2)
        TBr = sb("TBr", [128, 2, 9 * 128], F32); TBr_c = _grid(2)
        et = sb("et", [128, 3, 2, 512], BF16); et_c = _grid(3, 2)
        accs = sb("accs", [128, 2, 8, 129], F32); accs_c = _grid(2)
        rr = sb("rr", [128, 2, 8], F32); rr_c = _grid(2)
        nl = sb("nl", [128, 2, 4], F32); nl_c = _grid(2)
        o0 = sb("o0", [128, 2, 512], F32); o0_c = _grid(2)
        o1 = sb("o1", [128, 2, 512], F32); o1_c = _grid(2)
        sqt = sb("sqt", [128, 512], F32); sqt_c = Cell()
        ss = sb("ss", [128, 2, 8], F32); ss_c = _grid(2)
        on16 = sb("on16", [128, 2, 512], BF16); on_c = _grid(2)
        ost = sb("ost", [128, 2, Sm], BF16); ost_c = _grid(2)
        _ptc = Cell()
        PTh = [_ptc, _ptc]
        acc_banks = [self.banks[4], self.banks[5], self.bankD]
        lg_pairs = [self.pairs[0], self.pairs[1]]

        def acc_ap(a):
            b, sl = divmod(a, 3)
            return acc_banks[b].ap[:, sl * 129:(sl + 1) * 129], acc_banks[b].cell

        sgs = self.seq_groups()
        for h in range(8):
            kh = self.rot("hk", 2)
            hap = bass.AP(tensor=self.fvec.tensor, offset=h * FV + (FV // 2 - 4 * 128 - 127), ap=[[1, 128], [1, 9 * 128]])
            self.dma("sp", hk[:, kh, :], hap, reads=[self.fvec_c], writes=[hk_c[kh]])
            for b0 in range(0, 9, 4):
                nb = min(4, 9 - b0)
                bk = self.next_bank()
                for i in range(b0, b0 + nb):
                    dl = 4 - i
                    self.mm(bk.ap[:, (i - b0) * 128:(i - b0 + 1) * 128], hk[:, kh, (dl + 4) * 128:(dl + 5) * 128], self.antiF, True, True,
                            [hk_c[kh], self.cF_c], [bk.cell], inc=(i == b0 + nb - 1))
                self.V(lambda e: e.tensor_scalar(out=TBr[:, kh, b0 * 128:(b0 + nb) * 128], in0=bk.ap[:, 0:nb * 128], scalar1=8.0, scalar2=None, op0=ALU.mult), [bk.cell], [TBr_c[kh]])
            for sg in sgs:
                S = len(sg) * UL
                NB = S // 128
                tok0 = sg[0] * UL
                kq = self.rot("qkv", 2)
                tcells = [self.qkv_c[(tok0 // self.TT) + i] for i in range(S // self.TT)]
                self.dma("sp", QT[:, kq, 0:S], self.qT[h][:, tok0:tok0 + S], reads=tcells, writes=[QT_c[kq]])
                self.dma("sp", KT[:, kq, 0:S], self.kT[h][:, tok0:tok0 + S], reads=tcells, writes=[KT_c[kq]])
                self.dma("sp", Vh[:, kq, 0:NB, 0:128], self.vtm[tok0:tok0 + S, h * 128:(h + 1) * 128].rearrange("(b p) d -> p b d", p=128),
                         reads=tcells, writes=[Vh_c[kq]])
                ko = self.rot("ost", 2)
                def front(qc, kb):
                    uq = (qc * 512) // UL
                    uk = (kb * 128) // UL
                    dl = kb - 4 * qc
                    near = -1 <= dl <= 4
                    pr = lg_pairs[self.rot("lgp", 2)]
                    pcs = [pr.c0, pr.c1]
                    if near:
                        bcol = self.maskc[:, 0:1] if uk != uq else zeroc[:, 0:1]
                    elif dl < -1:
                        bcol = cLx[:, h:h + 1] if uk != uq else relb[:, 15 * 8 + h:15 * 8 + h + 1]
                    else:
                        bcol = cRx[:, h:h + 1] if uk != uq else relb[:, 31 * 8 + h:31 * 8 + h + 1]
                    ke = self.rot("et", 3)
                    for m in range(2):
                        ms = slice(m * 512, (m + 1) * 512)
                        self.mm(pr.ap[:, ms], KT[64 * m:64 * m + 64, kq, kb * 128:(kb + 1) * 128],
                                QT[64 * m:64 * m + 64, kq, qc * 512:(qc + 1) * 512], True, True, [KT_c[kq], QT_c[kq]], [pcs[m]])
                    for m in range(2):
                        ms = slice(m * 512, (m + 1) * 512)
                        if near:
                            bview = TBr[:, kh, (4 - dl) * 128:(8 - dl) * 128]
                            self.V(lambda e: e.tensor_tensor(out=pr.ap[:, ms], in0=pr.ap[:, ms], in1=bview, op=ALU.add), [pcs[m], TBr_c[kh]], [pcs[m]])
                        self.A(lambda e: e.activation(out=et[:, ke, m, :], in_=pr.ap[:, ms], func=AF.Exp, scale=SCALE, bias=bcol),
                               [pcs[m], sc, self.flag_c], [et_c[ke][m]])
                    return ke

                def back(qc, kb, ke):
                    for m in range(2):
                        for qb in range(4):
                            ap_, cell_ = acc_ap(m * 4 + qb)
                            self.mm(ap_, et[:, ke, m, qb * 128:(qb + 1) * 128], Vh[:, kq, kb, :], kb == 0 and (m * 4 + qb) % 3 == 0, kb == NB - 1,
                                    [et_c[ke][m], Vh_c[kq]], [cell_], inc=(qb == 3), skip=True)
                    if kb == NB - 1:
                        finalize(qc)

                def finalize(qc):
                    ka = self.rot("accs", 2)
                    for b in range(3):
                        ns = 3 if b < 2 else 2
                        self.A(lambda e, b=b, ns=ns: e.activation(out=accs[:, ka, 3 * b:3 * b + ns, :].rearrange("p a d -> p (a d)"), in_=acc_banks[b].ap[:, 0:ns * 129], func=AF.Copy),
                               [acc_banks[b].cell], [accs_c[ka]])
                    self.V(lambda e: e.reciprocal(out=rr[:, ka, :], in_=accs[:, ka, :, 128]), [accs_c[ka]], [rr_c[ka]])
                    self.V(lambda e: e.tensor_scalar(out=nl[:, ka, :], in0=rr[:, ka, 4:8], scalar1=neglam, scalar2=None, op0=ALU.mult), [rr_c[ka], sc], [nl_c[ka]])
                    v4 = lambda ap: ap.rearrange("p (a d) -> p a d", a=4)
                    self.V(lambda e: e.tensor_tensor(out=v4(o0[:, ka, :]), in0=accs[:, ka, 0:4, 0:128], in1=rr[:, ka, 0:4].unsqueeze(2).broadcast_to([128, 4, 128]), op=ALU.mult),
                           [accs_c[ka], rr_c[ka]], [o0_c[ka]])
                    self.V(lambda e: e.tensor_tensor(out=v4(o1[:, ka, :]), in0=accs[:, ka, 4:8, 0:128], in1=nl[:, ka, :].unsqueeze(2).broadcast_to([128, 4, 128]), op=ALU.mult),
                           [accs_c[ka], nl_c[ka]], [o1_c[ka]])
                    self.G(lambda e: e.tensor_tensor(out=o0[:, ka, :], in0=o0[:, ka, :], in1=o1[:, ka, :], op=ALU.add), [o0_c[ka], o1_c[ka]], [o0_c[ka]])
                    self.G(lambda e: e.tensor_tensor(out=sqt[:], in0=o0[:, ka, :], in1=o0[:, ka, :], op=ALU.mult), [o0_c[ka]], [sqt_c])
                    self.V(lambda e: e.reduce_sum(out=ss[:, ka, 0:4], in_=v4(sqt[:]), axis=mybir.AxisListType.X), [sqt_c], [ss_c[ka]])
                    self.A(lambda e: e.activation(out=ss[:, ka, 4:8], in_=ss[:, ka, 0:4], func=AF.Ln, scale=1.0 / 128, bias=self.epsc[:]), [ss_c[ka], self.eps_c], [ss_c[ka]])
                    self.A(lambda e: e.activation(out=ss[:, ka, 4:8], in_=ss[:, ka, 4:8], func=AF.Exp, scale=-0.5), [ss_c[ka]], [ss_c[ka]])
                    self.V(lambda e: e.tensor_tensor(out=v4(o1[:, ka, :]), in0=v4(o0[:, ka, :]), in1=ss[:, ka, 4:8].unsqueeze(2).broadcast_to([128, 4, 128]), op=ALU.mult),
                           [o0_c[ka], ss_c[ka]], [o1_c[ka]])
                    self.V(lambda e: e.tensor_tensor(out=v4(on16[:, ka, :]), in0=v4(o1[:, ka, :]), in1=gsub[:].unsqueeze(1).broadcast_to([128, 4, 128]), op=ALU.mult),
                           [o1_c[ka], sc], [on_c[ka]])
                    ph = self.rot("PTh", 2)
                    for qb in range(4):
                        self.P(lambda e, qb=qb: e.transpose(self.PT[:, ph * 512 + qb * 128:ph * 512 + (qb + 1) * 128], on16[:, ka, qb * 128:(qb + 1) * 128], self.identB),
                               [on_c[ka], self.cB_c], [PTh[ph]], inc=(qb == 3))
                    self.A(lambda e: e.activation(out=ost[:, ko, qc * 512:(qc + 1) * 512], in_=self.PT[:, ph * 512:(ph + 1) * 512], func=AF.Copy), [PTh[ph]], [ost_c[ko]])

                its = [(qc, kb) for qc in range(S // 512) for kb in range(NB)]
                pend = None
                for it in its:
                    ke = front(*it)
                    if pend is not None:
                        back(*pend)
                    pend = (it[0], it[1], ke)
                back(*pend)
                self.dma("pool", self.oT[h][:, tok0:tok0 + S], ost[:, ko, 0:S], reads=[ost_c[ko]], accum=[self.oT_c[u] for u in sg])


WEIGHT_KEYS = ["norm_pre", "norm_post", "ffn_w_gate", "ffn_w_up", "ffn_w_down", "ssm_w_in", "ssm_conv_w",
               "ssm_conv_b", "ssm_dt_bias", "ssm_a_log", "ssm_d", "ssm_norm", "ssm_w_out", "attn_w_qkv",
               "attn_lambda", "attn_subln", "attn_w_out", "rel_bias"]


def build(NU=5, UL=2048, TT=1024, stages="AMBNC", debug=()):
    mk = MKF(NU, UL, TT)
    mk.debug = set(debug)
    mk.declare()
    with mk.es:
        mk.setup_engines()
        mk.setup_psum()
        mk.setup_consts()
        mk.cast_weights()
        for st in stages:
            with ExitStack() as es:
                if st in "ABC":
                    mk.tl_alloc(es)
                    {"A": mk.stage_A, "B": mk.stage_B, "C": mk.stage_C}[st](es)
                elif st == "M":
                    mk.stage_ssd(es)
                elif st == "N":
                    mk.stage_attn(es)
                mk.barrier()
        mk.barrier()
    return mk


_CACHE = {}


def kernel(**inputs):
    x_prompt = np.ascontiguousarray(inputs["x_prompt"], dtype=np.float32)
    x_sample = np.ascontiguousarray(inputs["x_sample"], dtype=np.float32)
    NB, S, _ = x_prompt.shape
    SB, SS, _ = x_sample.shape
    assert (NB, S, SB, SS) == (4, 4096, 32, 2048)
    if "mk" not in _CACHE:
        _CACHE["mk"] = build()
    mk = _CACHE["mk"]
    consts, oh = make_consts()
    in_maps = []
    plan = []
    for c in range(8):
        if c < 4:
            samp = [3 * c, 3 * c + 1, 3 * c + 2]
            xs = np.concatenate([x_prompt[c]] + [x_sample[i] for i in samp], axis=0)
            flag = 1.0
        else:
            samp = [12 + 5 * (c - 4) + i for i in range(5)]
            xs = np.concatenate([x_sample[i] for i in samp], axis=0)
            flag = 0.0
        plan.append(samp)
        m = {"x": np.ascontiguousarray(xs), "flag": np.full((1, 1), flag, np.float32), "consts": consts, "bucket_oh": oh}
        for k in WEIGHT_KEYS:
            m[k] = np.ascontiguousarray(inputs[k], dtype=np.float32)
        in_maps.append(m)
    res = run_bass_kernel_spmd(mk.nc, in_maps, core_ids=list(range(8)))
    y_prompt = np.empty_like(x_prompt)
    y_sample = np.empty_like(x_sample)
    for c in range(8):
        y = np.asarray(res.results[c]["y"], dtype=np.float32)
        off = 0
        if c < 4:
            y_prompt[c] = y[0:4096]
            off = 4096
        for i in plan[c]:
            y_sample[i] = y[off:off + 2048]
            off += 2048
    return (y_prompt, y_sample)

MKF = MK4
```

```python
import math
from contextlib import ExitStack
import numpy as np
import concourse.bass as bass
import concourse.mybir as mybir
from concourse.bass_utils import run_bass_kernel_spmd

F32 = mybir.dt.float32
BF16 = mybir.dt.bfloat16
AF = mybir.ActivationFunctionType
ALU = mybir.AluOpType

D = 1024
DC = 8
DFF = 2816
FC = 22
DIN = 2048
NHS = 32
HD = 64
NG = 4
NST = 128
CONVD = 3072
SSM_IN = 5184
EPS = 1e-6
NBUCK = 32
LAMBDA_INIT = 0.8 - 0.6 * math.exp(-0.3 * 1)
SCALE = 64 ** -0.5
NEGBIG = -30000.0
FV = 1280


def _bucket(rel):
    half = NBUCK // 2
    max_exact = half // 2
    ret = np.where(rel > 0, half, 0)
    n = np.abs(rel)
    nf = np.maximum(n, 1).astype(np.float32)
    large = max_exact + (np.log(nf / np.float32(max_exact)) / np.float32(math.log(128 / max_exact)) * np.float32(half - max_exact)).astype(np.int32)
    large = np.minimum(large, half - 1)
    return ret + np.where(n < max_exact, n, large)


def make_consts():
    c = {}
    i = np.arange(128)
    c["ident"] = np.eye(128, dtype=np.float32)
    c["antiid"] = np.eye(128, dtype=np.float32)[::-1].copy()
    c["trif"] = (i[:, None] <= i[None, :]).astype(np.float32)
    c["trib"] = (i[:, None] >= i[None, :]).astype(np.float32)
    c["maskf"] = np.where(i[None, :] >= i[:, None], 0.0, NEGBIG).astype(np.float32)
    c["maskb"] = np.where(i[None, :] <= i[:, None], 0.0, NEGBIG).astype(np.float32)
    rel = np.arange(-FV // 2, FV // 2)
    b = _bucket(rel)
    oh = np.zeros((NBUCK, FV), np.float32)
    oh[b, np.arange(FV)] = 1.0
    c["bucket_oh"] = oh
    return np.concatenate([c["ident"], c["antiid"], c["trif"], c["trib"], c["maskf"], c["maskb"]], axis=1), oh


class Cell:
    __slots__ = ("w", "r", "aw")

    def __init__(self):
        self.w = None
        self.r = {}
        self.aw = {}


class Sem:
    __slots__ = ("h", "id")

    def __init__(self, h, i):
        self.h = h
        self.id = i


class Eng:
    def __init__(self, name, eng, sem, is_pe=False):
        self.name = name
        self.eng = eng
        self.sem = sem
        self.count = 0
        self.seen = {}
        self.is_pe = is_pe
        self.pend_r = []
        self.pend_w = []


class Slot:
    def __init__(self, sem):
        self.sem = sem
        self.val = 0


def cells_of(x):
    if isinstance(x, Cell):
        return [x]
    out = []
    for y in x:
        out.extend(cells_of(y))
    return out


class MK:
    def __init__(self, NU=5, UL=2048, TT=1024, debug_stage=None):
        self.NU, self.UL, self.TT = NU, UL, TT
        self.SW = 512
        assert TT % 512 == 0 and UL % TT == 0
        self.TS = TT // 512
        self.NTOK = NU * UL
        self.NT = self.NTOK // TT
        self.NCH = UL // 128
        self.debug_stage = debug_stage
        self.nc = bass.Bass("TRN2", target_bir_lowering=False)
        self.es = ExitStack()
        self.n_inst = 0
        self._uid = 0

    def uid(self, p):
        self._uid += 1
        return f"{p}{self._uid}"

    def sb(self, es, name, shape, dt):
        return es.enter_context(self.nc.sbuf_tensor(self.uid(name), list(shape), dt))

    def dram(self, name, shape, dt, kind="Internal"):
        if name in getattr(self, "debug", ()):
            kind = "ExternalOutput"
        return self.nc.dram_tensor(name, list(shape), dt, kind=kind).ap()

    def setup_engines(self):
        nc = self.nc
        self.sems = []

        def mksem(name):
            h = self.es.enter_context(nc.semaphore(name))
            s = Sem(h, len(self.sems))
            self.sems.append(s)
            return s
        self.E = {
            "pe": Eng("pe", nc.tensor, mksem("s_pe"), is_pe=True),
            "act": Eng("act", nc.scalar, mksem("s_act")),
            "dve": Eng("dve", nc.vector, mksem("s_dve")),
            "pool": Eng("pool", nc.gpsimd, mksem("s_pool")),
            "sp": Eng("sp", nc.sync, mksem("s_sp")),
        }
        NS = 16
        self.slots = {q: [Slot(mksem(f"d_{q}{i}")) for i in range(NS)] for q in ("sp", "pool")}
        self.slot_i = {"sp": 0, "pool": 0}

    def _waits(self, e, reads, writes, is_dma=False, accum=()):
        need = {}

        def add(tok, raw):
            s, v = tok
            if (not is_dma) and s is e.sem:
                if e.is_pe:
                    return
            if need.get(s.id, (None, 0))[1] < v:
                need[s.id] = (s, v)
        for c in reads:
            if c.w is not None:
                add(c.w, True)
            for sid, tok in c.aw.items():
                add(tok, True)
        for c in writes:
            if c.w is not None:
                add(c.w, False)
            for sid, tok in c.aw.items():
                add(tok, False)
            for sid, tok in c.r.items():
                add(tok, False)
        for c in accum:
            if c.w is not None:
                add(c.w, False)
            for sid, tok in c.r.items():
                add(tok, False)
        for sid, (s, v) in need.items():
            if e.seen.get(sid, 0) < v:
                e.eng.wait_ge(s.h, v)
                e.seen[sid] = v
                self.n_inst += 1

    def emit(self, en, make, reads=(), writes=(), inc=True):
        e = self.E[en]
        reads = cells_of(reads)
        writes = cells_of(writes)
        self._waits(e, reads, writes)
        ins = make(e.eng)
        self.n_inst += 1
        if not inc:
            e.pend_r.extend(reads)
            e.pend_w.extend(writes)
            return ins
        e.count += 1
        ins.then_inc(e.sem.h, 1)
        tok = (e.sem, e.count)
        for c in e.pend_r + reads:
            c.r[e.sem.id] = tok
        for c in e.pend_w + writes:
            c.w = tok
            c.r = {}
            c.aw = {}
        e.pend_r = []
        e.pend_w = []
        return ins

    def dma(self, q, out, in_, reads=(), writes=(), accum=(), **kw):
        e = self.E[q]
        reads = cells_of(reads)
        writes = cells_of(writes)
        accum = cells_of(accum)
        i = self.slot_i[q]
        self.slot_i[q] = i + 1
        sl = self.slots[q][i % len(self.slots[q])]
        if sl.val > 0 and e.seen.get(sl.sem.id, 0) < sl.val:
            e.eng.wait_ge(sl.sem.h, sl.val)
            e.seen[sl.sem.id] = sl.val
        self._waits(e, reads, writes, is_dma=True, accum=accum)
        ins = e.eng.dma_start(out=out, in_=in_, **kw)
        self.n_inst += 1
        sl.val += 16
        ins.then_inc(sl.sem.h, 16)
        tok = (sl.sem, sl.val)
        for c in reads:
            c.r[sl.sem.id] = tok
        for c in writes:
            c.w = tok
            c.r = {}
            c.aw = {}
        for c in accum:
            c.aw[sl.sem.id] = tok
        return ins

    def barrier(self):
        toks = []
        for e in self.E.values():
            assert not e.pend_r and not e.pend_w
            if e.count > 0:
                toks.append((e.sem, e.count))
        for q in self.slots:
            for sl in self.slots[q]:
                if sl.val > 0:
                    toks.append((sl.sem, sl.val))
        for e in self.E.values():
            for s, v in toks:
                if s is e.sem:
                    continue
                if e.seen.get(s.id, 0) < v:
                    e.eng.wait_ge(s.h, v)
                    e.seen[s.id] = v
                    self.n_inst += 1

    def V(self, make, r=(), w=(), inc=True):
        return self.emit("dve", make, r, w, inc)

    def A(self, make, r=(), w=(), inc=True):
        return self.emit("act", make, r, w, inc)

    def G(self, make, r=(), w=(), inc=True):
        return self.emit("pool", make, r, w, inc)

    def P(self, make, r=(), w=(), inc=True):
        return self.emit("pe", make, r, w, inc)

    def mm(self, out, lhsT, rhs, start, stop, r=(), w=(), inc=True, skip=False):
        if skip:
            return self.P(lambda e: e.matmul(out, lhsT=lhsT, rhs=rhs, start=start, stop=stop, skip_group_check=True), r, w, inc)
        return self.P(lambda e: e.matmul(out, lhsT=lhsT, rhs=rhs, start=start, stop=stop), r, w, inc)

    def declare(self):
        nc = self.nc
        NTOK = self.NTOK
        ext = lambda n, s: nc.dram_tensor(n, list(s), F32, kind="ExternalInput").ap()
        self.x_in = ext("x", [NTOK, D])
        self.flag_in = ext("flag", [1, 1])
        self.consts_in = ext("consts", [128, 6 * 128])
        self.oh_in = ext("bucket_oh", [NBUCK, FV])
        self.w_in = {
            "norm_pre": ext("norm_pre", [2, 3, D]), "norm_post": ext("norm_post", [2, 3, D]),
            "ffn_w_gate": ext("ffn_w_gate", [2, 2, D, DFF]), "ffn_w_up": ext("ffn_w_up", [2, 2, D, DFF]),
            "ffn_w_down": ext("ffn_w_down", [2, 2, DFF, D]),
            "ssm_w_in": ext("ssm_w_in", [1, D, SSM_IN]), "ssm_conv_w": ext("ssm_conv_w", [1, 5, CONVD]),
            "ssm_conv_b": ext("ssm_conv_b", [1, CONVD]), "ssm_dt_bias": ext("ssm_dt_bias", [1, 2, NHS]),
            "ssm_a_log": ext("ssm_a_log", [1, 2, NHS]), "ssm_d": ext("ssm_d", [1, NHS]),
            "ssm_norm": ext("ssm_norm", [1, DIN]), "ssm_w_out": ext("ssm_w_out", [1, DIN, D]),
            "attn_w_qkv": ext("attn_w_qkv", [1, D, 3072]), "attn_lambda": ext("attn_lambda", [1, 4, 64]),
            "attn_subln": ext("attn_subln", [1, 128]), "attn_w_out": ext("attn_w_out", [1, D, D]),
            "rel_bias": ext("rel_bias", [NBUCK, 8]),
        }
        self.y_out = nc.dram_tensor("y", [NTOK, D], F32, kind="ExternalOutput").ap()
        self.wb = {}
        self.wb_cell = {}
        for li in range(2):
            for fi in range(2):
                for nm, shp in (("gate", [D, DFF]), ("up", [D, DFF]), ("down", [DFF, D])):
                    k = f"{nm}{li}{fi}"
                    self.wb[k] = self.dram("wb_" + k, shp, BF16)
                    self.wb_cell[k] = Cell()
        for k, shp in (("ssm_in", [D, SSM_IN]), ("ssm_out", [DIN, D]), ("qkv", [D, 3072]), ("attn_out", [D, D])):
            self.wb[k] = self.dram("wb_" + k, shp, BF16)
            self.wb_cell[k] = Cell()
        NT, TT = self.NT, self.TT
        self.xres = self.dram("xres", [NT, 128, DC * TT], F32)
        self.xres_c = [Cell() for _ in range(NT)]
        self.xbc = self.dram("xbc", [24, 128, NTOK], BF16)
        self.zsc = self.dram("zsc", [NTOK, DIN], BF16)
        self.dtr = self.dram("dtr", [NTOK, 64], F32)
        self.ssm_c = [Cell() for _ in range(NT)]
        self.yT = self.dram("yT", [16, 128, NTOK], BF16)
        self.yT_c = [Cell() for _ in range(self.NU)]
        self.qT = self.dram("qT", [8, 128, NTOK], BF16)
        self.kT = self.dram("kT", [8, 128, NTOK], BF16)
        self.vtm = self.dram("vtm", [NTOK, D], BF16)
        self.qkv_c = [Cell() for _ in range(NT)]
        self.oT = self.dram("oT", [8, 128, NTOK], BF16)
        self.oT_c = [Cell() for _ in range(self.NU)]
        self.fvec = self.dram("fvec", [8, FV], F32)
        self.fvec_c = Cell()
        self.dbg_out = None

    def setup_consts(self):
        es = self.es
        nc = self.nc
        sb = lambda n, s, d: self.sb(es, n, s, d)
        self.cF = sb("cF", [128, 6 * 128], F32)
        self.cF_c = Cell()
        self.cB = sb("cB", [128, 6 * 128], BF16)
        self.cB_c = Cell()
        self.onesB = sb("onesB", [128, 128], BF16)
        self.onesB_c = Cell()
        self.dma("sp", self.cF[:], self.consts_in[:, :], writes=[self.cF_c])
        self.V(lambda e: e.tensor_copy(out=self.cB[:], in_=self.cF[:]), [self.cF_c], [self.cB_c])
        self.V(lambda e: e.memset(self.onesB[:], 1.0), [], [self.onesB_c])
        self.identF = self.cF[:, 0:128]
        self.antiF = self.cF[:, 128:256]
        self.identB = self.cB[:, 0:128]
        self.trifB = self.cB[:, 256:384]
        self.tribB = self.cB[:, 384:512]
        self.maskfB = self.cB[:, 512:640]
        self.maskbB = self.cB[:, 640:768]
        self.gpre = sb("gpre", [128, 6, DC], F32)
        self.gpost = sb("gpost", [128, 6, DC], F32)
        self.gposth = sb("gposth", [128, 6, DC], F32)
        self.g_c = Cell()
        self.dma("sp", self.gpre[:], self.w_in["norm_pre"].rearrange("l j (c p) -> p (l j) c", p=128),
                 writes=[self.g_c], allow_slow_non_contiguous=True)
        self.dma("sp", self.gpost[:], self.w_in["norm_post"].rearrange("l j (c p) -> p (l j) c", p=128),
                 writes=[self.g_c], allow_slow_non_contiguous=True)
        self.V(lambda e: e.tensor_scalar(out=self.gposth[:], in0=self.gpost[:], scalar1=0.5, scalar2=None, op0=ALU.mult),
               [self.g_c], [self.g_c])
        self.flagc = sb("flagc", [128, 1], F32)
        self.maskc = sb("maskc", [128, 1], F32)
        self.flag_c = Cell()
        self.dma("sp", self.flagc[:], self.flag_in.partition_broadcast(128), writes=[self.flag_c])
        self.V(lambda e: e.tensor_scalar(out=self.maskc[:], in0=self.flagc[:], scalar1=-NEGBIG, scalar2=NEGBIG,
                                         op0=ALU.mult, op1=ALU.add), [self.flag_c], [self.flag_c])
        self.epsc = sb("epsc", [128, 1], F32)
        self.eps_c = Cell()
        self.V(lambda e: e.memset(self.epsc[:], EPS), [], [self.eps_c])

    def cast_weights(self):
        def cast(dst, src, cell, rows, cols, nsplit):
            rs = rows // nsplit
            for i in range(nsplit):
                self.dma("pool", dst[i * rs:(i + 1) * rs, :], src[i * rs:(i + 1) * rs, :], accum=[cell])
        order = []
        for li in range(2):
            for fi in range(2):
                order.append((li, fi))
        w = self.w_in
        def ffn(li, fi):
            cast(self.wb[f"gate{li}{fi}"], w["ffn_w_gate"][li, fi], self.wb_cell[f"gate{li}{fi}"], D, DFF, 4)
            cast(self.wb[f"up{li}{fi}"], w["ffn_w_up"][li, fi], self.wb_cell[f"up{li}{fi}"], D, DFF, 4)
            cast(self.wb[f"down{li}{fi}"], w["ffn_w_down"][li, fi], self.wb_cell[f"down{li}{fi}"], DFF, D, 4)
        ffn(0, 0)
        cast(self.wb["ssm_in"], w["ssm_w_in"][0], self.wb_cell["ssm_in"], D, SSM_IN, 8)
        cast(self.wb["ssm_out"], w["ssm_w_out"][0], self.wb_cell["ssm_out"], DIN, D, 2)
        ffn(0, 1)
        ffn(1, 0)
        cast(self.wb["qkv"], w["attn_w_qkv"][0], self.wb_cell["qkv"], D, 3072, 4)
        cast(self.wb["attn_out"], w["attn_w_out"][0], self.wb_cell["attn_out"], D, D, 1)
        ffn(1, 1)


class Bank:
    def __init__(self, ap, cell):
        self.ap = ap
        self.cell = cell


class Pair:
    def __init__(self, ap, c0, c1):
        self.ap = ap
        self.c0 = c0
        self.c1 = c1


def _grid(*dims):
    if len(dims) == 1:
        return [Cell() for _ in range(dims[0])]
    return [_grid(*dims[1:]) for _ in range(dims[0])]


class MK2(MK):
    def setup_psum(self):
        nc = self.nc
        es = self.es
        self.PA = es.enter_context(nc.psum_tensor("PA", [128, 1024], F32))
        self.PB = es.enter_context(nc.psum_tensor("PB", [128, 1024], F32))
        self.PC = es.enter_context(nc.psum_tensor("PC", [128, 1024], F32))
        self.PD0 = es.enter_context(nc.psum_tensor("PD0", [128, 512], F32))
        self.PT = es.enter_context(nc.psum_tensor("PT", [128, 1024], BF16))
        self.pairs = []
        self.banks = []
        for t in (self.PA, self.PB, self.PC):
            c0, c1 = Cell(), Cell()
            self.pairs.append(Pair(t, c0, c1))
            self.banks.append(Bank(t[:, 0:512], c0))
            self.banks.append(Bank(t[:, 512:1024], c1))
        self.bankD = Bank(self.PD0[:, :], Cell())
        self.banks.append(self.bankD)
        self.PT_c = Cell()
        self.rotc = {}

    def rot(self, name, n):
        i = self.rotc.get(name, 0)
        self.rotc[name] = i + 1
        return i % n

    def next_bank(self):
        return self.banks[self.rot("bank", len(self.banks))]

    def next_pair(self):
        p = self.pairs[self.rot("pair", 3)]
        return p

    def tl_alloc(self, es):
        TT, TS = self.TT, self.TS
        sb = lambda n, s, d: self.sb(es, n, s, d)
        self.xT = sb("xT", [128, DC, TT], F32)
        self.xT_c = _grid(DC, TS)
        self.uT = sb("uT", [128, DC, TT], BF16)
        self.uT_c = _grid(DC, TS)
        self.hT = sb("hT", [128, FC, TT], BF16)
        self.hT_c = _grid(FC, TS)
        self.hout = sb("hout", [128, DC, TT], F32)
        self.hout_c = _grid(DC, TS)
        self.sq = sb("sq", [128, DC, 512], BF16)
        self.sq_c = _grid(DC)
        self.NW = 3
        self.wpool = [sb(f"wp{i}", [128, 5632], BF16) for i in range(self.NW)]
        self.wpool_c = _grid(self.NW)
        self.sg = sb("sg", [128, 3, 512], F32)
        self.sg_c = _grid(3)
        self.lnv = sb("lnv", [128, 512], F32)
        self.lnv_c = Cell()
        self.rstd = sb("rstd", [128, 2, 512], F32)
        self.rstd_c = _grid(2)
        self.tmpn = sb("tmpn", [128, 3, 512], F32)
        self.tmpn_c = _grid(3)
        self.xstage = sb("xstage", [128, 2, D], F32)
        self.xstage_c = _grid(2)
        self.zst = sb("zst", [128, 3, 512], BF16)
        self.zst_c = _grid(3)
        self.dst = sb("dst", [128, 2, 64], F32)
        self.dst_c = _grid(2)

    def tsl(self, ts):
        return slice(ts * 512, (ts + 1) * 512)

    def wload(self, W, wcell, KC, col0, pw):
        i = self.rot("w", self.NW)
        buf = self.wpool[i]
        view = buf[:, 0:KC * pw].rearrange("p (k n) -> p k n", k=KC)
        src = W.rearrange("(k p) n -> p k n", p=128)[:, :, col0:col0 + pw]
        self.dma("sp", view, src, reads=[wcell], writes=[self.wpool_c[i]])
        return view, self.wpool_c[i]

    def rms_stats(self, srcs, nfeat):
        C = len(srcs)
        for c, (ap, cl) in enumerate(srcs):
            self.A(lambda e, ap=ap, c=c: e.activation(out=self.sq[:, c, :], in_=ap, func=AF.Square), cl, [self.sq_c[c]])
        bank = self.next_bank()
        for c in range(C):
            self.mm(bank.ap, self.onesB[:], self.sq[:, c, :], c == 0, c == C - 1,
                    [self.onesB_c, self.sq_c[c]], [bank.cell], inc=(c == C - 1))
        j = self.rot("rs", 2)
        self.A(lambda e: e.activation(out=self.lnv[:], in_=bank.ap, func=AF.Ln, scale=1.0 / nfeat, bias=self.epsc[:]),
               [bank.cell, self.eps_c], [self.lnv_c])
        self.A(lambda e: e.activation(out=self.rstd[:, j, :], in_=self.lnv[:], func=AF.Exp, scale=-0.5),
               [self.lnv_c], [self.rstd_c[j]])
        return self.rstd[:, j, :], self.rstd_c[j]

    def prenorm(self, n):
        for ts in range(self.TS):
            sl = self.tsl(ts)
            srcs = [(self.xT[:, c, sl], [self.xT_c[c][ts]]) for c in range(DC)]
            rs, rc = self.rms_stats(srcs, D)
            for c in range(DC):
                self.V(lambda e, c=c: e.scalar_tensor_tensor(out=self.uT[:, c, sl], in0=self.xT[:, c, sl],
                                                             scalar=self.gpre[:, n, c:c + 1], in1=rs,
                                                             op0=ALU.mult, op1=ALU.mult),
                       [self.xT_c[c][ts], self.g_c, rc], [self.uT_c[c][ts]])

    def post_res(self, n, half):
        g = self.gposth if half else self.gpost
        for ts in range(self.TS):
            sl = self.tsl(ts)
            srcs = [(self.hout[:, c, sl], [self.hout_c[c][ts]]) for c in range(DC)]
            rs, rc = self.rms_stats(srcs, D)
            for c in range(DC):
                j = self.rot("tmpn", 3)
                self.V(lambda e, c=c, j=j: e.scalar_tensor_tensor(out=self.tmpn[:, j, :], in0=self.hout[:, c, sl],
                                                                  scalar=g[:, n, c:c + 1], in1=rs,
                                                                  op0=ALU.mult, op1=ALU.mult),
                       [self.hout_c[c][ts], self.g_c, rc], [self.tmpn_c[j]])
                (self.G if c % 2 == 0 else self.V)(lambda e, c=c, j=j: e.tensor_tensor(out=self.xT[:, c, sl], in0=self.xT[:, c, sl],
                                                                                  in1=self.tmpn[:, j, :], op=ALU.add),
                                                   [self.xT_c[c][ts], self.tmpn_c[j]], [self.xT_c[c][ts]])

    def linear_fm(self, src, src_c, KC, W, wcell, col0, ncols, consumer):
        PW = 512 if KC <= 8 else 256
        for pc0 in range(0, ncols, PW):
            pw = min(PW, ncols - pc0)
            wbuf, wc = self.wload(W, wcell, KC, col0 + pc0, pw)
            for ts in range(self.TS):
                for ml in range(pw // 128):
                    m = pc0 // 128 + ml
                    bank = self.next_bank()
                    for kc in range(KC):
                        self.mm(bank.ap, wbuf[:, kc, ml * 128:(ml + 1) * 128], src[:, kc, self.tsl(ts)],
                                kc == 0, kc == KC - 1, [wc, src_c[kc][ts]], [bank.cell], inc=(kc == KC - 1))
                    consumer(m, ts, bank)

    def linear_tm(self, src, src_c, KC, W, wcell, col0, ncols, consumer):
        for pc0 in range(0, ncols, 512):
            pw = min(512, ncols - pc0)
            wbuf, wc = self.wload(W, wcell, KC, col0 + pc0, pw)
            for tb in range(self.TT // 128):
                ts = (tb * 128) // 512
                bank = self.next_bank()
                for kc in range(KC):
                    self.mm(bank.ap[:, 0:pw], src[:, kc, tb * 128:(tb + 1) * 128], wbuf[:, kc, :],
                            kc == 0, kc == KC - 1, [wc, src_c[kc][ts]], [bank.cell], inc=(kc == KC - 1))
                consumer(tb, pc0, pw, bank)

    def ffn(self, li, fi):
        TS = self.TS
        n = li * 3 + (0 if fi == 0 else 2)
        self.prenorm(n)
        Wg, Wu, Wd = self.wb[f"gate{li}{fi}"], self.wb[f"up{li}{fi}"], self.wb[f"down{li}{fi}"]
        cg, cu, cd = self.wb_cell[f"gate{li}{fi}"], self.wb_cell[f"up{li}{fi}"], self.wb_cell[f"down{li}{fi}"]
        for p0 in range(0, FC, 4):
            nf = min(4, FC - p0)
            pw = nf * 128
            gbuf, gc = self.wload(Wg, cg, DC, p0 * 128, pw)
            ubuf, uc = self.wload(Wu, cu, DC, p0 * 128, pw)
            for ts in range(TS):
                sl = self.tsl(ts)
                for fl in range(nf):
                    f = p0 + fl
                    pr = self.next_pair()
                    for kc in range(DC):
                        self.mm(pr.ap[:, 0:512], gbuf[:, kc, fl * 128:(fl + 1) * 128], self.uT[:, kc, sl],
                                kc == 0, kc == DC - 1, [gc, self.uT_c[kc][ts]], [pr.c0], inc=(kc == DC - 1))
                    for kc in range(DC):
                        self.mm(pr.ap[:, 512:1024], ubuf[:, kc, fl * 128:(fl + 1) * 128], self.uT[:, kc, sl],
                                kc == 0, kc == DC - 1, [uc, self.uT_c[kc][ts]], [pr.c1], inc=(kc == DC - 1))
                    j = self.rot("sg", 3)
                    self.A(lambda e, j=j: e.activation(out=self.sg[:, j, :], in_=pr.ap[:, 0:512], func=AF.Silu),
                           [pr.c0], [self.sg_c[j]])
                    self.V(lambda e, j=j, f=f: e.tensor_tensor(out=self.hT[:, f, sl], in0=pr.ap[:, 512:1024],
                                                               in1=self.sg[:, j, :], op=ALU.mult),
                           [pr.c1, self.sg_c[j]], [self.hT_c[f][ts]])

        def cons(m, ts, bank):
            self.V(lambda e: e.tensor_copy(out=self.hout[:, m, self.tsl(ts)], in_=bank.ap),
                   [bank.cell], [self.hout_c[m][ts]])
        self.linear_fm(self.hT, self.hT_c, FC, Wd, cd, 0, D, cons)
        self.post_res(n, half=True)

    def load_x_tile(self, t):
        TT = self.TT
        for tb in range(TT // 128):
            ts = (tb * 128) // 512
            k = self.rot("xs", 2)
            r0 = t * TT + tb * 128
            self.dma("sp", self.xstage[:, k, :], self.x_in[r0:r0 + 128, :], writes=[self.xstage_c[k]])
            for hb in range(2):
                bank = self.next_bank()
                for cl in range(4):
                    c = hb * 4 + cl
                    self.P(lambda e, c=c, cl=cl: e.transpose(bank.ap[:, cl * 128:(cl + 1) * 128],
                                                              self.xstage[:, k, c * 128:(c + 1) * 128], self.identF),
                           [self.xstage_c[k], self.cF_c], [bank.cell], inc=(cl == 3))
                outv = self.xT[:, hb * 4:(hb + 1) * 4, tb * 128:(tb + 1) * 128]
                inv = bank.ap.rearrange("p (c n) -> p c n", c=4)
                wc = [self.xT_c[hb * 4 + cl][ts] for cl in range(4)]
                if hb == 0:
                    self.A(lambda e: e.activation(out=outv, in_=inv, func=AF.Copy), [bank.cell], wc)
                else:
                    self.V(lambda e: e.tensor_copy(out=outv, in_=inv), [bank.cell], wc)

    def store_y_tile(self, t):
        TT = self.TT
        for tb in range(TT // 128):
            ts = (tb * 128) // 512
            k = self.rot("xs", 2)
            for hb in range(2):
                bank = self.next_bank()
                for cl in range(4):
                    c = hb * 4 + cl
                    self.P(lambda e, c=c, cl=cl: e.transpose(bank.ap[:, cl * 128:(cl + 1) * 128],
                                                              self.xT[:, c, tb * 128:(tb + 1) * 128], self.identF),
                           [self.xT_c[c][ts], self.cF_c], [bank.cell], inc=(cl == 3))
                if hb == 0:
                    self.A(lambda e: e.activation(out=self.xstage[:, k, 0:512], in_=bank.ap, func=AF.Copy),
                           [bank.cell], [self.xstage_c[k]])
                else:
                    self.V(lambda e: e.tensor_copy(out=self.xstage[:, k, 512:1024], in_=bank.ap),
                           [bank.cell], [self.xstage_c[k]])
            r0 = t * TT + tb * 128
            self.dma("pool", self.y_out[r0:r0 + 128, :], self.xstage[:, k, :], reads=[self.xstage_c[k]])

    def all_xT_cells(self):
        return cells_of(self.xT_c)

    def store_xres(self, t):
        self.dma("pool", self.xres[t].rearrange("p (c n) -> p c n", c=DC), self.xT[:],
                 reads=self.all_xT_cells(), writes=[self.xres_c[t]])

    def load_xres(self, t):
        self.dma("sp", self.xT[:], self.xres[t].rearrange("p (c n) -> p c n", c=DC),
                 reads=[self.xres_c[t]], writes=self.all_xT_cells())

    def stage_A(self, es):
        TT = self.TT
        for t in range(self.NT):
            tok0 = t * TT
            self.load_x_tile(t)
            self.ffn(0, 0)
            self.prenorm(1)
            W, wc = self.wb["ssm_in"], self.wb_cell["ssm_in"]

            def cons_xbc(m, ts, bank):
                j = self.rot("zst", 3)
                self.V(lambda e: e.tensor_copy(out=self.zst[:, j, :], in_=bank.ap), [bank.cell], [self.zst_c[j]])
                a = tok0 + ts * 512
                self.dma("pool", self.xbc[m][:, a:a + 512], self.zst[:, j, :], reads=[self.zst_c[j]], accum=[self.ssm_c[t]])
            self.linear_fm(self.uT, self.uT_c, DC, W, wc, DIN, CONVD, cons_xbc)

            def cons_z(tb, pc0, pw, bank):
                j = self.rot("zst", 3)
                self.A(lambda e: e.activation(out=self.zst[:, j, 0:pw], in_=bank.ap[:, 0:pw], func=AF.Silu), [bank.cell], [self.zst_c[j]])
                r0 = tok0 + tb * 128
                self.dma("pool", self.zsc[r0:r0 + 128, pc0:pc0 + pw], self.zst[:, j, 0:pw], reads=[self.zst_c[j]], accum=[self.ssm_c[t]])
            self.linear_tm(self.uT, self.uT_c, DC, W, wc, 0, DIN, cons_z)

            def cons_dt(tb, pc0, pw, bank):
                j = self.rot("dst", 2)
                self.V(lambda e: e.tensor_copy(out=self.dst[:, j, :], in_=bank.ap[:, 0:64]), [bank.cell], [self.dst_c[j]])
                r0 = tok0 + tb * 128
                self.dma("pool", self.dtr[r0:r0 + 128, :], self.dst[:, j, :], reads=[self.dst_c[j]], accum=[self.ssm_c[t]])
            self.linear_tm(self.uT, self.uT_c, DC, W, wc, DIN + CONVD, 64, cons_dt)
            self.store_xres(t)

    def mixer_out(self, t, src_dram, src_cell, KC, wkey, n, pre=None):
        TT = self.TT
        tok0 = t * TT
        u = tok0 // self.UL
        view = self.hT[:, 0:KC, :]
        self.dma("sp", view, src_dram.rearrange("c p n -> p c n")[:, :, tok0:tok0 + TT],
                 reads=[src_cell[u]], writes=[self.hT_c[c] for c in range(KC)])
        if pre is not None:
            pre()

        def cons(m, ts, bank):
            self.V(lambda e: e.tensor_copy(out=self.hout[:, m, self.tsl(ts)], in_=bank.ap),
                   [bank.cell], [self.hout_c[m][ts]])
        self.linear_fm(self.hT, self.hT_c, KC, self.wb[wkey], self.wb_cell[wkey], 0, D, cons)
        self.post_res(n, half=False)

    def stage_B(self, es):
        TT = self.TT
        for t in range(self.NT):
            tok0 = t * TT
            self.mixer_out(t, self.yT, self.yT_c, 16, "ssm_out", 1, pre=lambda t=t: self.load_xres(t))
            self.ffn(0, 1)
            self.ffn(1, 0)
            self.prenorm(4)
            W, wc = self.wb["qkv"], self.wb_cell["qkv"]

            def cons_qk(m, ts, bank):
                j = self.rot("zst", 3)
                self.A(lambda e: e.activation(out=self.zst[:, j, :], in_=bank.ap, func=AF.Copy), [bank.cell], [self.zst_c[j]])
                a = tok0 + ts * 512
                dstT = self.qT[m] if m < 8 else self.kT[m - 8]
                self.dma("pool", dstT[:, a:a + 512], self.zst[:, j, :], reads=[self.zst_c[j]], accum=[self.qkv_c[t]])
            self.linear_fm(self.uT, self.uT_c, DC, W, wc, 0, 2048, cons_qk)

            def cons_v(tb, pc0, pw, bank):
                j = self.rot("zst", 3)
                self.V(lambda e: e.tensor_copy(out=self.zst[:, j, 0:pw], in_=bank.ap[:, 0:pw]), [bank.cell], [self.zst_c[j]])
                r0 = tok0 + tb * 128
                self.dma("pool", self.vtm[r0:r0 + 128, pc0:pc0 + pw], self.zst[:, j, 0:pw], reads=[self.zst_c[j]], accum=[self.qkv_c[t]])
            self.linear_tm(self.uT, self.uT_c, DC, W, wc, 2048, 1024, cons_v)
            self.store_xres(t)

    def stage_C(self, es):
        for t in range(self.NT):
            self.mixer_out(t, self.oT, self.oT_c, 8, "attn_out", 4, pre=lambda t=t: self.load_xres(t))
            self.ffn(1, 1)
            self.store_y_tile(t)


class MK3(MK2):
    def seq_groups(self):
        sgs = [[0, 1]] if self.NU >= 2 else [[0]]
        sgs += [[u] for u in range(2, self.NU)]
        return sgs

    def stage_ssd(self, es):
        UL, NCH = self.UL, self.NCH
        NCm = 2 * NCH
        CSEG = min(1024, UL)
        NBS = CSEG // 128
        sb = lambda n, s, d: self.sb(es, n, s, d)
        w = self.w_in
        a_bc = sb("a_bc", [128, 64], F32)
        dtb_bc = sb("dtb_bc", [128, 64], F32)
        D_bc = sb("D_bc", [128, 32], F32)
        convw = sb("convw", [128, 5, 24], F32)
        convb = sb("convb", [128, 24], F32)
        ng_bc = sb("ng_bc", [128, 512], F32)
        ng_c = Cell()
        mask4 = sb("mask4", [128, 2, 512], BF16)
        diagW = sb("diagW", [128, 6, 5, 128], BF16)
        sc = Cell()
        dg_c = Cell()
        self.dma("sp", a_bc[:], w["ssm_a_log"][0].rearrange("d h -> (d h)").partition_broadcast(128), writes=[sc])
        self.dma("sp", dtb_bc[:], w["ssm_dt_bias"][0].rearrange("d h -> (d h)").partition_broadcast(128), writes=[sc])
        self.dma("sp", D_bc[:], w["ssm_d"][0].partition_broadcast(128), writes=[sc])
        for k5 in range(5):
            self.dma("sp", convw[:, k5, :], w["ssm_conv_w"][0][k5].rearrange("(c p) -> p c", p=128), writes=[sc], allow_slow_non_contiguous=True)
        self.dma("sp", convb[:], w["ssm_conv_b"][0].rearrange("(c p) -> p c", p=128), writes=[sc], allow_slow_non_contiguous=True)
        self.A(lambda e: e.activation(out=a_bc[:], in_=a_bc[:], func=AF.Exp), [sc], [sc])
        self.V(lambda e: e.tensor_scalar(out=a_bc[:], in0=a_bc[:], scalar1=-1.0, scalar2=None, op0=ALU.mult), [sc], [sc])
        for d, mk_ in enumerate((self.maskfB, self.maskbB)):
            self.V(lambda e, d=d, mk_=mk_: e.tensor_copy(out=mask4[:, d, :].rearrange("p (h i) -> p h i", h=4),
                                                       in_=mk_.unsqueeze(1).broadcast_to([128, 4, 128])), [self.cB_c, sc], [sc])
        xg = sb("xg", [128, NCm, 512], BF16); xg_c = _grid(NCm)
        Bg = sb("Bg", [128, NCm, 128], BF16); Bg_c = _grid(NCm)
        BT = sb("BT", [128, NCm * 128], BF16); BT_c = _grid(NCm)
        CT = sb("CT", [128, NCm * 128], BF16); CT_c = _grid(NCm)
        hpb = sb("hpb", [128, NCm, 512], BF16); hpb_c = _grid(NCm)
        mkdt = lambda n, d_: sb(n, [128, 2, NCm, 8], d_)
        dtraw = mkdt("dtraw", F32); tA = mkdt("tA", F32); tB = dtraw
        dtv = mkdt("dtv", F32); dav = mkdt("dav", F32); csv = mkdt("csv", F32); csl = mkdt("csl", F32)
        Ef = mkdt("Ef", F32); Dec = mkdt("Dec", F32); CDt = mkdt("CDt", F32)
        da16 = mkdt("da16", BF16); nda16 = mkdt("nda16", BF16)
        dts_c = Cell()
        xin = sb("xin", [128, 2, CSEG + 4], BF16); xin_c = _grid(2)
        cvo = sb("cvo", [128, 2, CSEG], BF16); cvo_c = _grid(2)
        cbT = sb("cbT", [128, 3, 128], BF16); cbT_c = _grid(3)
        ex = sb("ex", [128, 8, 512], BF16); ex_c = _grid(8)
        WT = sb("WT", [128, 8, 512], BF16); WT_c = _grid(8)
        xdt = sb("xdt", [128, 3, 2, 512], BF16); xdt_c = _grid(3)
        xdec = sb("xdec", [128, 2, 512], BF16); xdec_c = _grid(2)
        xdecm = sb("xdecm", [128, 3, 512], BF16); xdecm_c = _grid(3)
        Hf = sb("Hf", [128, 512], F32); Hf_c = Cell()
        Hb = sb("Hb", [128, 512], F32); Hb_c = Cell()
        Hf16 = sb("Hf16", [128, 512], BF16); Hf16_c = Cell()
        tH = sb("tH", [128, 2, 512], F32); tH_c = _grid(2)
        t1 = sb("t1", [128, 2, 512], F32); t1_c = _grid(2)
        t2 = sb("t2", [128, 2, 512], F32); t2_c = _grid(2)
        t3 = sb("t3", [128, 5, 512], F32); t3_c = _grid(5)
        ysb = sb("ysb", [128, 3, 512], F32); ysb_c = _grid(3)
        zs = sb("zs", [128, 3, 512], BF16); zs_c = _grid(3)
        yz = sb("yz", [128, 2, 512], F32); yz_c = _grid(2)
        ssq = sb("ssq", [128, 3, 2], F32); ssq_c = _grid(3)
        yn = sb("yn", [128, 2, 512], BF16); yn_c = _grid(2)
        yst = sb("yst", [128, 2, 512], BF16); yst_c = _grid(2)
        _ptc = Cell()
        PTh = [_ptc, _ptc]

        def bc8(ap8):
            return ap8.unsqueeze(2).broadcast_to([128, 8, 64])

        def v3(ap512):
            return ap512.rearrange("p (h d) -> p h d", h=8)

        TTn = self.TT
        for g in range(NG):
            chs = [4 * g, 4 * g + 1, 4 * g + 2, 4 * g + 3, 16 + g, 20 + g]
            self.dma("sp", ng_bc[:], w["ssm_norm"][0][g * 512:(g + 1) * 512].partition_broadcast(128), writes=[ng_c])
            for cl, ch in enumerate(chs):
                for k5 in range(5):
                    self.G(lambda e, cl=cl, ch=ch, k5=k5: e.tensor_scalar(out=diagW[:, cl, k5, :], in0=self.identF, scalar1=convw[:, k5, ch:ch + 1],
                                                                          scalar2=None, op0=ALU.mult), [self.cF_c, sc], [dg_c])
            for sg in self.seq_groups():
                NC = len(sg) * NCH
                tok0 = sg[0] * UL
                pair = len(sg) == 2
                sgcells = [self.ssm_c[(tok0 // TTn) + i] for i in range(NC * 128 // TTn)]
                for d in range(2):
                    src = self.dtr[tok0:tok0 + NC * 128, d * 32 + 8 * g:d * 32 + 8 * g + 8].rearrange("(c p) h -> p c h", p=128)
                    self.dma("sp", dtraw[:, d, 0:NC, :], src, reads=sgcells, writes=[dts_c])
                S_ = lambda t_: t_[:, :, 0:NC, :]
                bsl = dtb_bc[:].rearrange("p (d h) -> p d h", d=2)[:, :, 8 * g:8 * g + 8].unsqueeze(2).broadcast_to([128, 2, NC, 8])
                asl = a_bc[:].rearrange("p (d h) -> p d h", d=2)[:, :, 8 * g:8 * g + 8].unsqueeze(2).broadcast_to([128, 2, NC, 8])
                self.V(lambda e: e.tensor_tensor(out=S_(tA), in0=S_(dtraw), in1=bsl, op=ALU.add), [dts_c, sc], [dts_c])
                self.A(lambda e: e.activation(out=S_(tB), in_=S_(tA), func=AF.Abs), [dts_c], [dts_c])
                self.A(lambda e: e.activation(out=S_(tB), in_=S_(tB), func=AF.Exp, scale=-1.0), [dts_c], [dts_c])
                self.A(lambda e: e.activation(out=S_(tB), in_=S_(tB), func=AF.Ln, bias=1.0), [dts_c], [dts_c])
                self.V(lambda e: e.scalar_tensor_tensor(out=S_(dtv), in0=S_(tA), scalar=0.0, in1=S_(tB), op0=ALU.max, op1=ALU.add), [dts_c], [dts_c])
                self.V(lambda e: e.tensor_tensor(out=S_(dav), in0=S_(dtv), in1=asl, op=ALU.mult), [dts_c, sc], [dts_c])
                self.V(lambda e: e.tensor_copy(out=S_(da16), in_=S_(dav)), [dts_c], [dts_c])
                self.V(lambda e: e.tensor_scalar(out=S_(nda16), in0=S_(dav), scalar1=-1.0, scalar2=None, op0=ALU.mult), [dts_c], [dts_c])
                bk1 = self.next_bank()
                bk2 = self.next_bank()
                for c in range(NC):
                    for d in range(2):
                        tri = self.trifB if d == 0 else self.tribB
                        o = (d * NC + c) * 8
                        self.mm(bk1.ap[:, o:o + 8], tri, da16[:, d, c, :], True, True, [self.cB_c, dts_c], [bk1.cell], inc=False)
                        self.mm(bk2.ap[:, o:o + 8], self.onesB[:], da16[:, d, c, :], True, True, [self.onesB_c, dts_c], [bk1.cell, bk2.cell], inc=(c == NC - 1 and d == 1))
                vb = lambda bk: bk.ap[:, 0:2 * NC * 8].rearrange("p (d c h) -> p d c h", d=2, c=NC)
                self.V(lambda e: e.tensor_copy(out=S_(csv), in_=vb(bk1)), [bk1.cell], [dts_c])
                self.V(lambda e: e.tensor_copy(out=S_(csl), in_=vb(bk2)), [bk2.cell], [dts_c])
                self.A(lambda e: e.activation(out=S_(Ef), in_=S_(csv), func=AF.Exp), [dts_c], [dts_c])
                self.A(lambda e: e.activation(out=S_(CDt), in_=S_(csl), func=AF.Exp), [dts_c], [dts_c])
                self.V(lambda e: e.tensor_tensor(out=S_(tA), in0=S_(csl), in1=S_(csv), op=ALU.subtract), [dts_c], [dts_c])
                self.A(lambda e: e.activation(out=S_(tA), in_=S_(tA), func=AF.Exp), [dts_c], [dts_c])
                self.V(lambda e: e.tensor_tensor(out=S_(Dec), in0=S_(tA), in1=S_(dtv), op=ALU.mult), [dts_c], [dts_c])
                for ui, u in enumerate(sg):
                    utok = u * UL
                    co = ui * NCH
                    nseg = UL // CSEG
                    for seg in range(nseg):
                        s0 = utok + seg * CSEG
                        cb0 = co + seg * NBS
                        lh = "real" if seg > 0 else ("flag" if (pair and ui == 1) else "zero")
                        rh = "real" if seg < nseg - 1 else ("flag" if (pair and ui == 0) else "zero")
                        for cl, ch in enumerate(chs):
                            k = self.rot("xin", 2)
                            a0 = s0 - (0 if lh == "zero" else 2)
                            a1 = s0 + CSEG + (0 if rh == "zero" else 2)
                            o0 = 0 if lh != "zero" else 2
                            self.dma("sp", xin[:, k, o0:o0 + (a1 - a0)], self.xbc[ch][:, a0:a1], reads=sgcells, writes=[xin_c[k]])
                            if lh == "zero":
                                self.G(lambda e: e.memset(xin[:, k, 0:2], 0.0), [], [xin_c[k]])
                            elif lh == "flag":
                                self.G(lambda e: e.tensor_scalar(out=xin[:, k, 0:2], in0=xin[:, k, 0:2], scalar1=self.flagc[:, 0:1], scalar2=None, op0=ALU.mult),
                                       [xin_c[k], self.flag_c], [xin_c[k]])
                            if rh == "zero":
                                self.G(lambda e: e.memset(xin[:, k, CSEG + 2:CSEG + 4], 0.0), [], [xin_c[k]])
                            elif rh == "flag":
                                self.G(lambda e: e.tensor_scalar(out=xin[:, k, CSEG + 2:CSEG + 4], in0=xin[:, k, CSEG + 2:CSEG + 4], scalar1=self.flagc[:, 0:1],
                                                                 scalar2=None, op0=ALU.mult), [xin_c[k], self.flag_c], [xin_c[k]])
                            if ch >= 16:
                                dstT, dst_cells = (BT, BT_c) if ch < 20 else (CT, CT_c)
                            else:
                                kk = self.rot("cvo", 2)
                            for b5 in range(CSEG // 512):
                                bk = self.next_bank()
                                for k5 in range(5):
                                    self.mm(bk.ap, diagW[:, cl, k5, :], xin[:, k, b5 * 512 + k5:b5 * 512 + k5 + 512], k5 == 0, k5 == 4,
                                            [dg_c, xin_c[k]], [bk.cell], inc=(k5 == 4))
                                if ch >= 16:
                                    c4 = cb0 + b5 * 4
                                    self.A(lambda e: e.activation(out=dstT[:, c4 * 128:c4 * 128 + 512], in_=bk.ap, func=AF.Silu, bias=convb[:, ch:ch + 1]),
                                           [bk.cell, sc], dst_cells[c4:c4 + 4])
                                else:
                                    self.A(lambda e: e.activation(out=cvo[:, kk, b5 * 512:(b5 + 1) * 512], in_=bk.ap, func=AF.Silu, bias=convb[:, ch:ch + 1]),
                                           [bk.cell, sc], [cvo_c[kk]])
                            if 16 <= ch < 20:
                                for b in range(NBS):
                                    self.P(lambda e, b=b: e.transpose(self.PT[:, b * 128:(b + 1) * 128], BT[:, (cb0 + b) * 128:(cb0 + b + 1) * 128], self.identB),
                                           [BT_c[cb0 + b], self.cB_c], PTh, inc=(b == NBS - 1))
                                self.V(lambda e: e.tensor_copy(out=Bg[:, cb0:cb0 + NBS, :], in_=self.PT[:, 0:NBS * 128].rearrange("p (b n) -> p b n", b=NBS)),
                                       PTh, Bg_c[cb0:cb0 + NBS])
                            elif ch < 16:
                                xi = ch - 4 * g
                                for b in range(NBS):
                                    self.P(lambda e, b=b: e.transpose(self.PT[:, b * 128:(b + 1) * 128], cvo[:, kk, b * 128:(b + 1) * 128], self.identB),
                                           [cvo_c[kk], self.cB_c], PTh, inc=(b == NBS - 1))
                                self.V(lambda e: e.tensor_copy(out=xg[:, cb0:cb0 + NBS, xi * 128:(xi + 1) * 128], in_=self.PT[:, 0:NBS * 128].rearrange("p (b n) -> p b n", b=NBS)),
                                       PTh, xg_c[cb0:cb0 + NBS])
                self.V(lambda e: e.memset(Hb[:], 0.0), [], [Hb_c])

                def pre_states(c):
                    k = self.rot("xdec", 2)
                    self.G(lambda e: e.tensor_tensor(out=v3(xdec[:, k, :]), in0=v3(xg[:, c, :]), in1=bc8(Dec[:, 1, c, :]), op=ALU.mult),
                           [xg_c[c], dts_c], [xdec_c[k]])
                    bk = self.next_bank()
                    self.mm(bk.ap, Bg[:, c, :], xdec[:, k, :], True, True, [Bg_c[c], xdec_c[k]], [bk.cell])
                    return bk

                def pre_rec(c, bk):
                    if pair and c == NCH - 1:
                        self.V(lambda e: e.tensor_scalar(out=Hb[:], in0=Hb[:], scalar1=self.flagc[:, 0:1], scalar2=None, op0=ALU.mult), [Hb_c, self.flag_c], [Hb_c])
                    self.A(lambda e: e.activation(out=hpb[:, c, :], in_=Hb[:], func=AF.Copy), [Hb_c], [hpb_c[c]])
                    j = self.rot("tH", 2)
                    self.V(lambda e: e.tensor_tensor(out=v3(tH[:, j, :]), in0=v3(Hb[:]), in1=bc8(CDt[:, 1, c, :]), op=ALU.mult), [Hb_c, dts_c], [tH_c[j]])
                    self.V(lambda e: e.tensor_tensor(out=Hb[:], in0=bk.ap, in1=tH[:, j, :], op=ALU.add), [bk.cell, tH_c[j]], [Hb_c])
                prev = None
                for c in range(NC - 1, -1, -1):
                    bk = pre_states(c)
                    if prev is not None:
                        pre_rec(*prev)
                    prev = (c, bk)
                pre_rec(*prev)
                self.V(lambda e: e.memset(Hf[:], 0.0), [], [Hf_c])
                self.V(lambda e: e.memset(Hf16[:], 0.0), [], [Hf16_c])
                stt = {}

                def Sa(c):
                    tc = slice(c * 128, (c + 1) * 128)
                    d_ = stt[c] = {}
                    bcb = self.next_bank()
                    self.mm(bcb.ap[:, 0:128], BT[:, tc], CT[:, tc], True, True, [BT_c[c], CT_c[c]], [bcb.cell])
                    kc_ = d_["cb"] = self.rot("cbT", 3)
                    self.A(lambda e: e.activation(out=cbT[:, kc_, :], in_=bcb.ap[:, 0:128], func=AF.Copy), [bcb.cell], [cbT_c[kc_]])
                    kx = d_["xdt"] = self.rot("xdt", 3)
                    self.V(lambda e: e.tensor_tensor(out=v3(xdt[:, kx, 0, :]), in0=v3(xg[:, c, :]), in1=bc8(dtv[:, 0, c, :]), op=ALU.mult),
                           [xg_c[c], dts_c], [xdt_c[kx]])
                    self.G(lambda e: e.tensor_tensor(out=v3(xdt[:, kx, 1, :]), in0=v3(xg[:, c, :]), in1=bc8(dtv[:, 1, c, :]), op=ALU.mult),
                           [xg_c[c], dts_c], [xdt_c[kx]])
                    kd = d_["xdec"] = self.rot("xdecm", 3)
                    self.G(lambda e: e.tensor_tensor(out=v3(xdecm[:, kd, :]), in0=v3(xg[:, c, :]), in1=bc8(Dec[:, 0, c, :]), op=ALU.mult), [xg_c[c], dts_c], [xdecm_c[kd]])
                    k3 = d_["t3"] = self.rot("t3", 5)
                    self.G(lambda e: e.tensor_tensor(out=v3(t3[:, k3, :]), in0=v3(xg[:, c, :]), in1=bc8(D_bc[:, 8 * g:8 * g + 8]), op=ALU.mult), [xg_c[c], sc], [t3_c[k3]])
                    d_["ex"] = []
                    for bi in range(4):
                        d = bi // 2
                        h0 = (bi % 2) * 4
                        tri = self.trifB if d == 0 else self.tribB
                        bs = self.next_bank()
                        self.mm(bs.ap, self.identB, mask4[:, d, :], True, False, [self.cB_c, sc], [bs.cell], inc=False)
                        self.mm(bs.ap, tri, nda16[:, d, c, h0:h0 + 4].unsqueeze(2).broadcast_to([128, 4, 128]), False, False, [self.cB_c, dts_c], [bs.cell], inc=False)
                        for hl in range(4):
                            self.mm(bs.ap[:, hl * 128:(hl + 1) * 128], da16[:, d, c, h0 + hl:h0 + hl + 1].broadcast_to([128, 128]), tri, False, hl == 3,
                                    [self.cB_c, dts_c], [bs.cell], inc=(hl == 3))
                        ke = self.rot("ex", 8)
                        self.A(lambda e: e.activation(out=ex[:, ke, :], in_=bs.ap, func=AF.Exp), [bs.cell], [ex_c[ke]])
                        d_["ex"].append(ke)

                def Sb(c):
                    d_ = stt[c]
                    d_["wt"] = []
                    for bi in range(4):
                        ke = d_["ex"][bi]
                        kw_ = self.rot("WT", 8)
                        self.V(lambda e: e.tensor_tensor(out=WT[:, kw_, :].rearrange("p (h i) -> p h i", h=4), in0=ex[:, ke, :].rearrange("p (h i) -> p h i", h=4),
                                                         in1=cbT[:, d_["cb"], :].unsqueeze(1).broadcast_to([128, 4, 128]), op=ALU.mult), [ex_c[ke], cbT_c[d_["cb"]]], [WT_c[kw_]])
                        d_["wt"].append(kw_)

                def Sc(c):
                    d_ = stt[c]
                    tc = slice(c * 128, (c + 1) * 128)
                    kx, kd, wts = d_["xdt"], d_["xdec"], d_["wt"]
                    if pair and c == NCH:
                        self.V(lambda e: e.tensor_scalar(out=Hf[:], in0=Hf[:], scalar1=self.flagc[:, 0:1], scalar2=None, op0=ALU.mult), [Hf_c, self.flag_c], [Hf_c])
                        self.A(lambda e: e.activation(out=Hf16[:], in_=Hf[:], func=AF.Copy), [Hf_c], [Hf16_c])
                    bof = self.next_bank()
                    self.mm(bof.ap, CT[:, tc], Hf16[:], True, True, [CT_c[c], Hf16_c], [bof.cell])
                    bst = self.next_bank()
                    self.mm(bst.ap, Bg[:, c, :], xdecm[:, kd, :], True, True, [Bg_c[c], xdecm_c[kd]], [bst.cell])
                    j = self.rot("tH", 2)
                    self.V(lambda e: e.tensor_tensor(out=v3(tH[:, j, :]), in0=v3(Hf[:]), in1=bc8(CDt[:, 0, c, :]), op=ALU.mult), [Hf_c, dts_c], [tH_c[j]])
                    self.V(lambda e: e.tensor_tensor(out=Hf[:], in0=bst.ap, in1=tH[:, j, :], op=ALU.add), [bst.cell, tH_c[j]], [Hf_c])
                    self.A(lambda e: e.activation(out=Hf16[:], in_=Hf[:], func=AF.Copy), [Hf_c], [Hf16_c])
                    bob = self.next_bank()
                    self.mm(bob.ap, CT[:, tc], hpb[:, c, :], True, True, [CT_c[c], hpb_c[c]], [bob.cell])
                    by = self.next_bank()
                    for h in range(8):
                        for d in range(2):
                            kw_ = wts[d * 2 + h // 4]
                            hl = h % 4
                            self.mm(by.ap[:, h * 64:(h + 1) * 64], WT[:, kw_, hl * 128:(hl + 1) * 128], xdt[:, kx, d, h * 64:(h + 1) * 64], d == 0, d == 1,
                                    [WT_c[kw_], xdt_c[kx]], [by.cell], inc=(h == 7 and d == 1))
                    kt = d_["t1"] = self.rot("t1", 2)
                    self.V(lambda e: e.tensor_tensor(out=v3(t1[:, kt, :]), in0=v3(bof.ap), in1=bc8(Ef[:, 0, c, :]), op=ALU.mult), [bof.cell, dts_c], [t1_c[kt]])
                    self.V(lambda e: e.tensor_tensor(out=v3(t2[:, kt, :]), in0=v3(bob.ap), in1=bc8(Ef[:, 1, c, :]), op=ALU.mult), [bob.cell, dts_c], [t2_c[kt]])
                    ky = d_["ysb"] = self.rot("ysb", 3)
                    self.A(lambda e: e.activation(out=ysb[:, ky, :], in_=by.ap, func=AF.Copy), [by.cell], [ysb_c[ky]])
                    kz = d_["zs"] = self.rot("zs", 3)
                    r0 = tok0 + c * 128
                    self.dma("sp", zs[:, kz, :], self.zsc[r0:r0 + 128, g * 512:(g + 1) * 512], reads=[self.ssm_c[r0 // TTn]], writes=[zs_c[kz]])

                def Sd(c):
                    d_ = stt[c]
                    kt, k3 = d_["t1"], d_["t3"]
                    self.G(lambda e: e.tensor_tensor(out=t1[:, kt, :], in0=t1[:, kt, :], in1=t2[:, kt, :], op=ALU.add), [t1_c[kt], t2_c[kt]], [t1_c[kt]])
                    self.G(lambda e: e.tensor_tensor(out=t3[:, k3, :], in0=t3[:, k3, :], in1=t1[:, kt, :], op=ALU.add), [t1_c[kt], t3_c[k3]], [t3_c[k3]])

                def Se(c):
                    d_ = stt[c]
                    ky, k3, kz = d_["ysb"], d_["t3"], d_["zs"]
                    self.V(lambda e: e.tensor_tensor(out=ysb[:, ky, :], in0=ysb[:, ky, :], in1=t3[:, k3, :], op=ALU.add), [ysb_c[ky], t3_c[k3]], [ysb_c[ky]])
                    kq = d_["yz"] = self.rot("yz", 2)
                    self.V(lambda e: e.tensor_tensor(out=yz[:, kq, :], in0=ysb[:, ky, :], in1=zs[:, kz, :], op=ALU.mult), [ysb_c[ky], zs_c[kz]], [yz_c[kq]])
                    kss = d_["ssq"] = self.rot("ssq", 3)
                    self.A(lambda e: e.activation(out=ysb[:, ky, :], in_=yz[:, kq, :], func=AF.Square, accum_out=ssq[:, kss, 0:1]), [yz_c[kq]], [ysb_c[ky], ssq_c[kss]])

                def Sf(c):
                    d_ = stt[c]
                    kq, kss = d_["yz"], d_["ssq"]
                    self.A(lambda e: e.activation(out=ssq[:, kss, 1:2], in_=ssq[:, kss, 0:1], func=AF.Ln, scale=1.0 / 512, bias=self.epsc[:]), [ssq_c[kss], self.eps_c], [ssq_c[kss]])
                    self.A(lambda e: e.activation(out=ssq[:, kss, 1:2], in_=ssq[:, kss, 1:2], func=AF.Exp, scale=-0.5), [ssq_c[kss]], [ssq_c[kss]])
                    kn = d_["yn"] = self.rot("yn", 2)
                    self.V(lambda e: e.scalar_tensor_tensor(out=yn[:, kn, :], in0=yz[:, kq, :], scalar=ssq[:, kss, 1:2], in1=ng_bc[:],
                                                            op0=ALU.mult, op1=ALU.mult), [yz_c[kq], ssq_c[kss], ng_c], [yn_c[kn]])

                def Sg(c):
                    d_ = stt.pop(c)
                    kn = d_["yn"]
                    for k4 in range(4):
                        self.P(lambda e, k4=k4: e.transpose(self.PT[:, k4 * 128:(k4 + 1) * 128], yn[:, kn, k4 * 128:(k4 + 1) * 128], self.identB),
                               [yn_c[kn], self.cB_c], [_ptc], inc=(k4 == 3))
                    ks = self.rot("yst", 2)
                    self.A(lambda e: e.activation(out=yst[:, ks, :], in_=self.PT[:, 0:512], func=AF.Copy), [_ptc], [yst_c[ks]])
                    r0 = tok0 + c * 128
                    u = r0 // UL
                    self.dma("pool", self.yT[4 * g:4 * g + 4].rearrange("k p n -> p k n")[:, :, r0:r0 + 128], yst[:, ks, :].rearrange("p (k n) -> p k n", k=4),
                             reads=[yst_c[ks]], accum=[self.yT_c[u]])

                order = [(Sc, 2), (Sa, 0), (Sb, 1), (Sd, 3), (Se, 4), (Sf, 5), (Sg, 6)]
                for i in range(NC + 6):
                    for fn, lag in order:
                        c = i - lag
                        if 0 <= c < NC:
                            fn(c)


class MK4(MK3):
    def stage_attn(self, es):
        UL = self.UL
        Sm = 2 * UL if self.NU >= 2 else UL
        NBm = Sm // 128
        sb = lambda n, s, d: self.sb(es, n, s, d)
        w = self.w_in
        sc = Cell()
        relb = sb("relb", [128, NBUCK * 8], F32)
        self.dma("sp", relb[:], w["rel_bias"].rearrange("b h -> (b h)").partition_broadcast(128), writes=[sc])
        cLx = sb("cLx", [128, 8], F32)
        cRx = sb("cRx", [128, 8], F32)
        self.V(lambda e: e.tensor_scalar(out=cLx[:], in0=relb[:, 15 * 8:16 * 8], scalar1=self.maskc[:, 0:1], scalar2=None, op0=ALU.add), [sc, self.flag_c], [sc])
        self.V(lambda e: e.tensor_scalar(out=cRx[:], in0=relb[:, 31 * 8:32 * 8], scalar1=self.maskc[:, 0:1], scalar2=None, op0=ALU.add), [sc, self.flag_c], [sc])
        zeroc = sb("zeroc", [128, 1], F32)
        self.V(lambda e: e.memset(zeroc[:], 0.0), [], [sc])
        lamb = sb("lamb", [128, 4, 64], F32)
        self.dma("sp", lamb[:], w["attn_lambda"][0].rearrange("a d -> (a d)").partition_broadcast(128), writes=[sc])
        lt = sb("lt", [128, 2, 64], F32)
        ls = sb("ls", [128, 4], F32)
        self.V(lambda e: e.tensor_tensor(out=lt[:, 0, :], in0=lamb[:, 0, :], in1=lamb[:, 1, :], op=ALU.mult), [sc], [sc])
        self.V(lambda e: e.tensor_tensor(out=lt[:, 1, :], in0=lamb[:, 2, :], in1=lamb[:, 3, :], op=ALU.mult), [sc], [sc])
        self.V(lambda e: e.reduce_sum(out=ls[:, 0:2], in_=lt[:], axis=mybir.AxisListType.X), [sc], [sc])
        self.A(lambda e: e.activation(out=ls[:, 0:2], in_=ls[:, 0:2], func=AF.Exp), [sc], [sc])
        self.V(lambda e: e.tensor_tensor(out=ls[:, 2:3], in0=ls[:, 1:2], in1=ls[:, 0:1], op=ALU.subtract), [sc], [sc])
        self.V(lambda e: e.tensor_scalar(out=ls[:, 3:4], in0=ls[:, 2:3], scalar1=-LAMBDA_INIT, scalar2=None, op0=ALU.add), [sc], [sc])
        neglam = ls[:, 3:4]
        gsub = sb("gsub", [128, 128], F32)
        self.dma("sp", gsub[:], w["attn_subln"][0].partition_broadcast(128), writes=[sc])
        self.V(lambda e: e.tensor_scalar(out=gsub[:], in0=gsub[:], scalar1=1.0 - LAMBDA_INIT, scalar2=None, op0=ALU.mult), [sc], [sc])
        tab = sb("tab", [NBUCK, 8], F32)
        ohs = sb("ohs", [NBUCK, FV], F32)
        fsb = sb("fsb", [8, FV], F32)
        self.dma("sp", tab[:], w["rel_bias"][:, :], writes=[sc])
        self.dma("sp", ohs[:], self.oh_in[:, :], writes=[sc])
        for i0 in range(0, FV, 512):
            n = min(512, FV - i0)
            bk = self.next_bank()
            self.mm(bk.ap[0:8, 0:n], tab[:], ohs[:, i0:i0 + n], True, True, [sc], [bk.cell])
            self.V(lambda e: e.tensor_copy(out=fsb[:, i0:i0 + n], in_=bk.ap[0:8, 0:n]), [bk.cell], [sc])
        self.dma("pool", self.fvec[:, :], fsb[:], reads=[sc], writes=[self.fvec_c])
        QT = sb("QT", [128, 2, Sm], BF16); QT_c = _grid(2)
        KT = sb("KT", [128, 2, Sm], BF16); KT_c = _grid(2)
        Vh = sb("Vh", [128, 2, NBm, 129], BF16); Vh_c = _grid(2)
        self.V(lambda e: e.memset(Vh[:, :, :, 128:129], 1.0), [], Vh_c)
        hk = sb("hk", [128, 2, 9 * 128], F32); hk_c = _grid(2)
        TBr = sb("TBr", [128, 2, 9 * 128], F32); TBr_c = _grid(2)
        et = sb("et", [128, 3, 2, 512], BF16); et_c = _grid(3, 2)
        accs = sb("accs", [128, 2, 8, 129], F32); accs_c = _grid(2)
        rr = sb("rr", [128, 2, 8], F32); rr_c = _grid(2)
        nl = sb("nl", [128, 2, 4], F32); nl_c = _grid(2)
        o0 = sb("o0", [128, 2, 512], F32); o0_c = _grid(2)
        o1 = sb("o1", [128, 2, 512], F32); o1_c = _grid(2)
        sqt = sb("sqt", [128, 512], F32); sqt_c = Cell()
        ss = sb("ss", [128, 2, 8], F32); ss_c = _grid(2)
        on16 = sb("on16", [128, 2, 512], BF16); on_c = _grid(2)
        ost = sb("ost", [128, 2, Sm], BF16); ost_c = _grid(2)
        _ptc = Cell()
        PTh = [_ptc, _ptc]
        acc_banks = [self.banks[4], self.banks[5], self.bankD]
        lg_pairs = [self.pairs[0], self.pairs[1]]

        def acc_ap(a):
            b, sl = divmod(a, 3)
            return acc_banks[b].ap[:, sl * 129:(sl + 1) * 129], acc_banks[b].cell

        sgs = self.seq_groups()
        for h in range(8):
            kh = self.rot("hk", 2)
            hap = bass.AP(tensor=self.fvec.tensor, offset=h * FV + (FV // 2 - 4 * 128 - 127), ap=[[1, 128], [1, 9 * 128]])
            self.dma("sp", hk[:, kh, :], hap, reads=[self.fvec_c], writes=[hk_c[kh]])
            for b0 in range(0, 9, 4):
                nb = min(4, 9 - b0)
                bk = self.next_bank()
                for i in range(b0, b0 + nb):
                    dl = 4 - i
                    self.mm(bk.ap[:, (i - b0) * 128:(i - b0 + 1) * 128], hk[:, kh, (dl + 4) * 128:(dl + 5) * 128], self.antiF, True, True,
                            [hk_c[kh], self.cF_c], [bk.cell], inc=(i == b0 + nb - 1))
                self.V(lambda e: e.tensor_scalar(out=TBr[:, kh, b0 * 128:(b0 + nb) * 128], in0=bk.ap[:, 0:nb * 128], scalar1=8.0, scalar2=None, op0=ALU.mult), [bk.cell], [TBr_c[kh]])
            for sg in sgs:
                S = len(sg) * UL
                NB = S // 128
                tok0 = sg[0] * UL
                kq = self.rot("qkv", 2)
                tcells = [self.qkv_c[(tok0 // self.TT) + i] for i in range(S // self.TT)]
                self.dma("sp", QT[:, kq, 0:S], self.qT[h][:, tok0:tok0 + S], reads=tcells, writes=[QT_c[kq]])
                self.dma("sp", KT[:, kq, 0:S], self.kT[h][:, tok0:tok0 + S], reads=tcells, writes=[KT_c[kq]])
                self.dma("sp", Vh[:, kq, 0:NB, 0:128], self.vtm[tok0:tok0 + S, h * 128:(h + 1) * 128].rearrange("(b p) d -> p b d", p=128),
                         reads=tcells, writes=[Vh_c[kq]])
                ko = self.rot("ost", 2)
                def front(qc, kb):
                    uq = (qc * 512) // UL
                    uk = (kb * 128) // UL
                    dl = kb - 4 * qc
                    near = -1 <= dl <= 4
                    pr = lg_pairs[self.rot("lgp", 2)]
                    pcs = [pr.c0, pr.c1]
                    if near:
                        bcol = self.maskc[:, 0:1] if uk != uq else zeroc[:, 0:1]
                    elif dl < -1:
                        bcol = cLx[:, h:h + 1] if uk != uq else relb[:, 15 * 8 + h:15 * 8 + h + 1]
                    else:
                        bcol = cRx[:, h:h + 1] if uk != uq else relb[:, 31 * 8 + h:31 * 8 + h + 1]
                    ke = self.rot("et", 3)
                    for m in range(2):
                        ms = slice(m * 512, (m + 1) * 512)
                        self.mm(pr.ap[:, ms], KT[64 * m:64 * m + 64, kq, kb * 128:(kb + 1) * 128],
                                QT[64 * m:64 * m + 64, kq, qc * 512:(qc + 1) * 512], True, True, [KT_c[kq], QT_c[kq]], [pcs[m]])
                    for m in range(2):
                        ms = slice(m * 512, (m + 1) * 512)
                        if near:
                            bview = TBr[:, kh, (4 - dl) * 128:(8 - dl) * 128]
                            self.V(lambda e: e.tensor_tensor(out=pr.ap[:, ms], in0=pr.ap[:, ms], in1=bview, op=ALU.add), [pcs[m], TBr_c[kh]], [pcs[m]])
                        self.A(lambda e: e.activation(out=et[:, ke, m, :], in_=pr.ap[:, ms], func=AF.Exp, scale=SCALE, bias=bcol),
                               [pcs[m], sc, self.flag_c], [et_c[ke][m]])
                    return ke

                def back(qc, kb, ke):
                    for m in range(2):
                        for qb in range(4):
                            ap_, cell_ = acc_ap(m * 4 + qb)
                            self.mm(ap_, et[:, ke, m, qb * 128:(qb + 1) * 128], Vh[:, kq, kb, :], kb == 0 and (m * 4 + qb) % 3 == 0, kb == NB - 1,
                                    [et_c[ke][m], Vh_c[kq]], [cell_], inc=(qb == 3), skip=True)
                    if kb == NB - 1:
                        finalize(qc)

                def finalize(qc):
                    ka = self.rot("accs", 2)
                    for b in range(3):
                        ns = 3 if b < 2 else 2
                        self.A(lambda e, b=b, ns=ns: e.activation(out=accs[:, ka, 3 * b:3 * b + ns, :].rearrange("p a d -> p (a d)"), in_=acc_banks[b].ap[:, 0:ns * 129], func=AF.Copy),
                               [acc_banks[b].cell], [accs_c[ka]])
                    self.V(lambda e: e.reciprocal(out=rr[:, ka, :], in_=accs[:, ka, :, 128]), [accs_c[ka]], [rr_c[ka]])
                    self.V(lambda e: e.tensor_scalar(out=nl[:, ka, :], in0=rr[:, ka, 4:8], scalar1=neglam, scalar2=None, op0=ALU.mult), [rr_c[ka], sc], [nl_c[ka]])
                    v4 = lambda ap: ap.rearrange("p (a d) -> p a d", a=4)
                    self.V(lambda e: e.tensor_tensor(out=v4(o0[:, ka, :]), in0=accs[:, ka, 0:4, 0:128], in1=rr[:, ka, 0:4].unsqueeze(2).broadcast_to([128, 4, 128]), op=ALU.mult),
                           [accs_c[ka], rr_c[ka]], [o0_c[ka]])
                    self.V(lambda e: e.tensor_tensor(out=v4(o1[:, ka, :]), in0=accs[:, ka, 4:8, 0:128], in1=nl[:, ka, :].unsqueeze(2).broadcast_to([128, 4, 128]), op=ALU.mult),
                           [accs_c[ka], nl_c[ka]], [o1_c[ka]])
                    self.G(lambda e: e.tensor_tensor(out=o0[:, ka, :], in0=o0[:, ka, :], in1=o1[:, ka, :], op=ALU.add), [o0_c[ka], o1_c[ka]], [o0_c[ka]])
                    self.G(lambda e: e.tensor_tensor(out=sqt[:], in0=o0[:, ka, :], in1=o0[:, ka, :], op=ALU.mult), [o0_c[ka]], [sqt_c])
                    self.V(lambda e: e.reduce_sum(out=ss[:, ka, 0:4], in_=v4(sqt[:]), axis=mybir.AxisListType.X), [sqt_c], [ss_c[ka]])
                    self.A(lambda e: e.activation(out=ss[:, ka, 4:8], in_=ss[:, ka, 0:4], func=AF.Ln, scale=1.0 / 128, bias=self.epsc[:]), [ss_c[ka], self.eps_c], [ss_c[ka]])
                    self.A(lambda e: e.activation(out=ss[:, ka, 4:8], in_=ss[:, ka, 4:8], func=AF.Exp, scale=-0.5), [ss_c[ka]], [ss_c[ka]])
                    self.V(lambda e: e.tensor_tensor(out=v4(o1[:, ka, :]), in0=v4(o0[:, ka, :]), in1=ss[:, ka, 4:8].unsqueeze(2).broadcast_to([128, 4, 128]), op=ALU.mult),
                           [o0_c[ka], ss_c[ka]], [o1_c[ka]])
                    self.V(lambda e: e.tensor_tensor(out=v4(on16[:, ka, :]), in0=v4(o1[:, ka, :]), in1=gsub[:].unsqueeze(1).broadcast_to([128, 4, 128]), op=ALU.mult),
                           [o1_c[ka], sc], [on_c[ka]])
                    ph = self.rot("PTh", 2)
                    for qb in range(4):
                        self.P(lambda e, qb=qb: e.transpose(self.PT[:, ph * 512 + qb * 128:ph * 512 + (qb + 1) * 128], on16[:, ka, qb * 128:(qb + 1) * 128], self.identB),
                               [on_c[ka], self.cB_c], [PTh[ph]], inc=(qb == 3))
                    self.A(lambda e: e.activation(out=ost[:, ko, qc * 512:(qc + 1) * 512], in_=self.PT[:, ph * 512:(ph + 1) * 512], func=AF.Copy), [PTh[ph]], [ost_c[ko]])

                its = [(qc, kb) for qc in range(S // 512) for kb in range(NB)]
                pend = None
                for it in its:
                    ke = front(*it)
                    if pend is not None:
                        back(*pend)
                    pend = (it[0], it[1], ke)
                back(*pend)
                self.dma("pool", self.oT[h][:, tok0:tok0 + S], ost[:, ko, 0:S], reads=[ost_c[ko]], accum=[self.oT_c[u] for u in sg])


WEIGHT_KEYS = ["norm_pre", "norm_post", "ffn_w_gate", "ffn_w_up", "ffn_w_down", "ssm_w_in", "ssm_conv_w",
               "ssm_conv_b", "ssm_dt_bias", "ssm_a_log", "ssm_d", "ssm_norm", "ssm_w_out", "attn_w_qkv",
               "attn_lambda", "attn_subln", "attn_w_out", "rel_bias"]


def build(NU=5, UL=2048, TT=1024, stages="AMBNC", debug=()):
    mk = MKF(NU, UL, TT)
    mk.debug = set(debug)
    mk.declare()
    with mk.es:
        mk.setup_engines()
        mk.setup_psum()
        mk.setup_consts()
        mk.cast_weights()
        for st in stages:
            with ExitStack() as es:
                if st in "ABC":
                    mk.tl_alloc(es)
                    {"A": mk.stage_A, "B": mk.stage_B, "C": mk.stage_C}[st](es)
                elif st == "M":
                    mk.stage_ssd(es)
                elif st == "N":
                    mk.stage_attn(es)
                mk.barrier()
        mk.barrier()
    return mk


_CACHE = {}


def kernel(**inputs):
    x_prompt = np.ascontiguousarray(inputs["x_prompt"], dtype=np.float32)
    x_sample = np.ascontiguousarray(inputs["x_sample"], dtype=np.float32)
    NB, S, _ = x_prompt.shape
    SB, SS, _ = x_sample.shape
    assert (NB, S, SB, SS) == (4, 4096, 32, 2048)
    if "mk" not in _CACHE:
        _CACHE["mk"] = build()
    mk = _CACHE["mk"]
    consts, oh = make_consts()
    in_maps = []
    plan = []
    for c in range(8):
        if c < 4:
            samp = [3 * c, 3 * c + 1, 3 * c + 2]
            xs = np.concatenate([x_prompt[c]] + [x_sample[i] for i in samp], axis=0)
            flag = 1.0
        else:
            samp = [12 + 5 * (c - 4) + i for i in range(5)]
            xs = np.concatenate([x_sample[i] for i in samp], axis=0)
            flag = 0.0
        plan.append(samp)
        m = {"x": np.ascontiguousarray(xs), "flag": np.full((1, 1), flag, np.float32), "consts": consts, "bucket_oh": oh}
        for k in WEIGHT_KEYS:
            m[k] = np.ascontiguousarray(inputs[k], dtype=np.float32)
        in_maps.append(m)
    res = run_bass_kernel_spmd(mk.nc, in_maps, core_ids=list(range(8)))
    y_prompt = np.empty_like(x_prompt)
    y_sample = np.empty_like(x_sample)
    for c in range(8):
        y = np.asarray(res.results[c]["y"], dtype=np.float32)
        off = 0
        if c < 4:
            y_prompt[c] = y[0:4096]
            off = 4096
        for i in plan[c]:
            y_sample[i] = y[off:off + 2048]
            off += 2048
    return (y_prompt, y_sample)

MKF = MK4
```
